# Optimizing a Trainium2 kernel written in Bass

```python
import jax, jax.numpy as jnp
from jax import lax
import numpy as np

D_MODEL = 1024
BATCH = 1
SEQ = 16384
DEPTH = 1

N_META = 16
CHUNK = 64
EPS = 1e-6
M_HEADS = 4
M_DV = D_MODEL // M_HEADS
M_DQK = M_DV // 2
M_QK = M_HEADS * M_DQK
M_V = M_HEADS * M_DV
CONV_W = 4
F_BIAS = 3.0
G_HEADS = 4
G_DV = D_MODEL // G_HEADS
G_DK = G_DV // 2
G_QK = G_HEADS * G_DK
G_V = G_HEADS * G_DV
G_RANK = 16
G_TAU = 16.0
D_FF = ((8 * D_MODEL + 3 * 256 - 1) // (3 * 256)) * 256
PROJ_WIDTHS = (M_QK, M_QK, M_V, M_HEADS, M_HEADS, M_V, G_QK, G_QK, G_V, G_RANK, G_V, D_MODEL, D_MODEL)
N_PROJ = sum(PROJ_WIDTHS)

kernel_name = 'hybrid_mlstm_gla_block'


def rmsnorm(x, g):
    xf = x.astype(jnp.float32)
    y = xf * lax.rsqrt(jnp.mean(xf * xf, axis=-1, keepdims=True) + EPS)
    return (y * g.astype(jnp.float32)).astype(x.dtype)


def head_rmsnorm(h, g):
    y = h * lax.rsqrt(jnp.mean(h * h, axis=-1, keepdims=True) + EPS)
    return y * g.astype(jnp.float32)


def split_cols(p):
    idx = np.cumsum(np.array(PROJ_WIDTHS))[:-1]
    return jnp.split(p, idx, axis=-1)


def to_chunks(t, n_heads):
    b, tp = t.shape[:2]
    t = t.reshape(b, tp // CHUNK, CHUNK, n_heads, -1)
    return jnp.transpose(t, (0, 3, 1, 2, 4)).astype(jnp.float32)


def from_chunks(t):
    b, h, nc, l, d = t.shape
    return jnp.transpose(t, (0, 2, 3, 1, 4)).reshape(b, nc * l, h, d)


def causal_depthwise_conv(x, w, bias):
    k = w.shape[0]
    y = lax.conv_general_dilated(x, w[:, None, :].astype(x.dtype), window_strides=(1,),
                                 padding=[(k - 1, 0)], dimension_numbers=('NWC', 'WIO', 'NWC'),
                                 feature_group_count=x.shape[-1])
    return y + bias.astype(x.dtype)


def mlstm_chunkwise(q, k, v, logi, logf):
    L = q.shape[3]
    b = jnp.cumsum(logf, axis=-1)
    g = b[..., -1]
    causal = jnp.tril(jnp.ones((L, L), dtype=bool))
    dmat = jnp.where(causal, b[..., :, None] - b[..., None, :] + logi[..., None, :], -jnp.inf)
    wlog = g[..., None] - b + logi

    def step(carry, inp):
        C, n, m = carry
        kc, vc, wc, gc = inp
        m_new = jnp.maximum(gc + m, jnp.max(wc, axis=-1))
        a = jnp.exp(gc + m - m_new)
        w = jnp.exp(wc - m_new[..., None])
        C_new = a[..., None, None] * C + jnp.einsum('bhl,bhlv,bhlk->bhvk', w, vc, kc)
        n_new = a[..., None] * n + jnp.einsum('bhl,bhlk->bhk', w, kc)
        return (C_new, n_new, m_new), (C, n, m)

    bsz, nh, _, _, dqk = q.shape
    dv = v.shape[-1]
    init = (jnp.zeros((bsz, nh, dv, dqk), jnp.float32), jnp.zeros((bsz, nh, dqk), jnp.float32),
            jnp.zeros((bsz, nh), jnp.float32))
    xs = (jnp.moveaxis(k, 2, 0), jnp.moveaxis(v, 2, 0), jnp.moveaxis(wlog, 2, 0), jnp.moveaxis(g, 2, 0))
    _, (Cs, ns, ms) = lax.scan(step, init, xs)
    Cs = jnp.moveaxis(Cs, 0, 2)
    ns = jnp.moveaxis(ns, 0, 2)
    ms = jnp.moveaxis(ms, 0, 2)

    inter_log = b + ms[..., None]
    m_row = jnp.maximum(inter_log, jnp.max(dmat, axis=-1))
    sim = jnp.einsum('bhcjd,bhcsd->bhcjs', q, k)
    wts = jnp.exp(dmat - m_row[..., None]) * sim
    a_inter = jnp.exp(inter_log - m_row)
    num = (a_inter[..., None] * jnp.einsum('bhcvd,bhcjd->bhcjv', Cs, q)
           + jnp.einsum('bhcjs,bhcsv->bhcjv', wts, v))
    den = a_inter * jnp.einsum('bhcd,bhcjd->bhcj', ns, q) + jnp.sum(wts, axis=-1)
    return num / jnp.maximum(jnp.abs(den), jnp.exp(-m_row))[..., None]


def gla_chunked(q, k, v, loga):
    L = q.shape[3]
    bc = jnp.cumsum(loga, axis=3)
    btot = bc[..., -1, :]
    q_dec = q * jnp.exp(bc)
    k_inv = k * jnp.exp(-bc)
    k_end = k * jnp.exp(btot[..., None, :] - bc)
    causal = jnp.tril(jnp.ones((L, L), dtype=bool))
    att = jnp.where(causal, jnp.einsum('bhcjd,bhcsd->bhcjs', q_dec, k_inv), 0.0)
    intra = jnp.einsum('bhcjs,bhcsv->bhcjv', att, v)

    def step(S, inp):
        ke, vc, bt = inp
        S_new = jnp.exp(bt)[..., None] * S + jnp.einsum('bhlk,bhlv->bhkv', ke, vc)
        return S_new, S

    bsz, nh, _, _, dk = q.shape
    dv = v.shape[-1]
    S0 = jnp.zeros((bsz, nh, dk, dv), jnp.float32)
    _, S_prev = lax.scan(step, S0, (jnp.moveaxis(k_end, 2, 0), jnp.moveaxis(v, 2, 0), jnp.moveaxis(btot, 2, 0)))
    S_prev = jnp.moveaxis(S_prev, 0, 2)
    return intra + jnp.einsum('bhcjk,bhckv->bhcjv', q_dec, S_prev)


def setup_inputs(seed: int = 0) -> dict:
    key = jax.random.key(seed)
    ks = jax.random.split(key, 24)
    f32 = jnp.float32
    nrm = lambda k, shape, s: jax.random.normal(k, shape, f32) * s
    m_gate_b = jnp.stack([nrm(ks[6], (DEPTH, M_HEADS), 0.01),
                          F_BIAS + nrm(ks[7], (DEPTH, M_HEADS), 0.1)], axis=1)
    return {
        'x': nrm(ks[0], (BATCH, SEQ, D_MODEL), 1.0),
        'meta_tokens': nrm(ks[1], (N_META, D_MODEL), 1.0),
        'norm1_g': 1.0 + nrm(ks[2], (DEPTH, D_MODEL), 0.02),
        'w_in': nrm(ks[3], (DEPTH, D_MODEL, N_PROJ), D_MODEL ** -0.5),
        'conv_w': nrm(ks[4], (DEPTH, CONV_W, 2 * M_QK), CONV_W ** -0.5),
        'conv_b': nrm(ks[5], (DEPTH, 2 * M_QK), 0.01),
        'm_gate_b': m_gate_b,
        'g_a2': nrm(ks[8], (DEPTH, G_RANK, G_QK), G_RANK ** -0.5),
        'g_a2_b': nrm(ks[9], (DEPTH, G_QK), 0.01),
        'm_head_g': 1.0 + nrm(ks[10], (DEPTH, M_HEADS, M_DV), 0.02),
        'g_head_g': 1.0 + nrm(ks[11], (DEPTH, G_HEADS, G_DV), 0.02),
        'w_branch_m': nrm(ks[12], (DEPTH, M_V, D_MODEL), M_V ** -0.5),
        'w_branch_g': nrm(ks[13], (DEPTH, G_V, D_MODEL), G_V ** -0.5),
        'w_out': nrm(ks[14], (DEPTH, D_MODEL, D_MODEL), D_MODEL ** -0.5),
        'norm2_g': 1.0 + nrm(ks[15], (DEPTH, D_MODEL), 0.02),
        'w_ff_gate': nrm(ks[16], (DEPTH, D_MODEL, D_FF), D_MODEL ** -0.5),
        'w_ff_up': nrm(ks[17], (DEPTH, D_MODEL, D_FF), D_MODEL ** -0.5),
        'w_ff_down': nrm(ks[18], (DEPTH, D_FF, D_MODEL), D_FF ** -0.5),
        'final_g': 1.0 + nrm(ks[19], (D_MODEL,), 0.02),
    }


def reference(x, meta_tokens, norm1_g, w_in, conv_w, conv_b, m_gate_b, g_a2, g_a2_b, m_head_g, g_head_g,
              w_branch_m, w_branch_g, w_out, norm2_g, w_ff_gate, w_ff_up, w_ff_down, final_g):
    f32 = jnp.float32
    bsz, _, d = x.shape
    dt = x.dtype
    n_pad = CHUNK - N_META
    meta = jnp.broadcast_to(meta_tokens.astype(dt)[None], (bsz, N_META, d))
    h = jnp.concatenate([jnp.zeros((bsz, n_pad, d), dt), meta, x], axis=1)
    tp = h.shape[1]
    valid = (jnp.arange(tp) >= n_pad)[None, :, None]

    for l in range(DEPTH):
        xn = rmsnorm(h, norm1_g[l])
        proj = jnp.where(valid, xn @ w_in[l].astype(dt), 0.0).astype(dt)
        (mq, mk, mv, mi, mf, mo, gq, gk, gv, ga, gr, gate_m, gate_g) = split_cols(proj)

        mqk = jax.nn.silu(causal_depthwise_conv(jnp.concatenate([mq, mk], axis=-1), conv_w[l], conv_b[l]))
        mq, mk = jnp.split(mqk, 2, axis=-1)
        logi = jnp.where(valid, mi.astype(f32) + m_gate_b[l, 0], -jnp.inf)
        logf = jnp.where(valid, jax.nn.log_sigmoid(mf.astype(f32) + m_gate_b[l, 1]), 0.0)
        hm = mlstm_chunkwise(to_chunks(mq, M_HEADS) * (M_DQK ** -0.5), to_chunks(mk, M_HEADS),
                             to_chunks(mv, M_HEADS), to_chunks(logi, M_HEADS)[..., 0],
                             to_chunks(logf, M_HEADS)[..., 0])
        hm = head_rmsnorm(from_chunks(hm), m_head_g[l]) * jax.nn.sigmoid(mo.astype(f32)).reshape(bsz, tp, M_HEADS, M_DV)
        y_m = hm.reshape(bsz, tp, M_V).astype(dt)

        za = ga @ g_a2[l].astype(dt) + g_a2_b[l].astype(dt)
        loga = jnp.where(valid, jax.nn.log_sigmoid(za.astype(f32)) / G_TAU, 0.0)
        hg = gla_chunked(to_chunks(gq, G_HEADS) * (G_DK ** -0.5), to_chunks(gk, G_HEADS),
                         to_chunks(gv, G_HEADS), to_chunks(loga, G_HEADS))
        hg = head_rmsnorm(from_chunks(hg), g_head_g[l]) * jax.nn.silu(gr.astype(f32)).reshape(bsz, tp, G_HEADS, G_DV)
        y_g = hg.reshape(bsz, tp, G_V).astype(dt)

        merged = (jax.nn.sigmoid(gate_m) * (y_m @ w_branch_m[l].astype(dt))
                  + jax.nn.sigmoid(gate_g) * (y_g @ w_branch_g[l].astype(dt)))
        h = h + merged @ w_out[l].astype(dt)

        hn = rmsnorm(h, norm2_g[l])
        ff = jax.nn.silu(hn @ w_ff_gate[l].astype(dt)) * (hn @ w_ff_up[l].astype(dt))
        h = h + ff @ w_ff_down[l].astype(dt)

    out = rmsnorm(h, final_g)
    return out[:, CHUNK:, :]
```

```python
import math
from contextlib import ExitStack

import numpy as np
import concourse.bass as bass
import concourse.mybir as mybir
from concourse.bass_utils import run_bass_kernel_spmd

F32 = mybir.dt.float32
BF16 = mybir.dt.bfloat16
AF = mybir.ActivationFunctionType
ALU = mybir.AluOpType
AX = mybir.AxisListType

NCORES = 8
D = 1024
KT = 8
TL = 2048
NT = 17
TT = NT * 128
NCH = 2 * NT
DFF = 2816
NPROJ = 8216
EPS = 1e-6
BLOCKS = [(0, 4), (4, 4), (8, 4), (12, 4), (16, 1)]
LN_C = -0.5 * math.log(128.0)

DEBUG = {}
MODE = "split"


class Tok:
    __slots__ = ("sem", "val", "eng", "key")

    def __init__(self, sem, val, eng, key):
        self.sem, self.val, self.eng, self.key = sem, val, eng, key


class Buf:
    __slots__ = ("name", "w", "r", "excl")

    def __init__(self, name="", excl=False):
        self.name = name
        self.w = None
        self.r = {}
        self.excl = excl


class Eng:
    def __init__(self, e, name, sems, is_pe=False):
        self.e = e
        self.name = name
        self.sems = sems
        self.ep = 0
        self.cnt = 0
        self.is_pe = is_pe
        self.waited = {}
        self.last = None

    def wait_tok(self, tok):
        if tok is None:
            return
        if tok.eng is self and self.is_pe:
            return
        if self.waited.get(tok.key, 0) >= tok.val:
            return
        self.e.wait_ge(tok.sem, tok.val)
        self.waited[tok.key] = tok.val

    def bump(self, ins):
        sem = self.sems[self.ep]
        ins.then_inc(sem, 1)
        self.cnt += 1
        t = Tok(sem, self.cnt, self, (self.name, self.ep))
        self.last = t
        return t

    def new_epoch(self):
        self.ep += 1
        self.cnt = 0


class DmaQ:
    def __init__(self, eng, sems, name):
        self.eng = eng
        self.sems = sems
        self.cnt = [0] * len(sems)
        self.last = [None] * len(sems)
        self.i = 0
        self.name = name

    def issue(self, out, in_, **kw):
        i = self.i
        self.i = (self.i + 1) % len(self.sems)
        self.eng.wait_tok(self.last[i])
        ins = self.eng.e.dma_start(out=out, in_=in_, **kw)
        ins.then_inc(self.sems[i], 16)
        self.cnt[i] += 16
        t = Tok(self.sems[i], self.cnt[i], None, (self.name, i))
        self.last[i] = t
        return t


class K:
    def op(self, eng, fn, reads=(), writes=()):
        ex = [b for b in reads if b.excl]
        if ex:
            reads = [b for b in reads if not b.excl]
            writes = list(writes) + [b for b in ex if b not in writes]
        for b in reads:
            eng.wait_tok(b.w)
        for b in writes:
            eng.wait_tok(b.w)
            for t in b.r.values():
                eng.wait_tok(t)
        tok = eng.bump(fn())
        for b in reads:
            b.r[tok.key] = tok
        for b in writes:
            b.w = tok
            b.r = {}
        return tok

    def dma(self, q, out, in_, reads=(), writes=(), **kw):
        for b in reads:
            q.eng.wait_tok(b.w)
        for b in writes:
            q.eng.wait_tok(b.w)
            for t in b.r.values():
                q.eng.wait_tok(t)
        tok = q.issue(out, in_, **kw)
        for b in reads:
            b.r[tok.key] = tok
        for b in writes:
            b.w = tok
            b.r = {}
        return tok

    def barrier(self):
        toks = []
        for e in (self.pe, self.act, self.dve, self.pool):
            if e.last is not None:
                toks.append(e.last)
        for q in (self.qs, self.qg):
            for t in q.last:
                if t is not None:
                    toks.append(t)
        for e in (self.pe, self.act, self.dve, self.pool, self.sp):
            for t in toks:
                if t.eng is e:
                    continue
                e.wait_tok(t)


def build_program(mode="fused"):
    nc = bass.Bass("TRN2", target_bir_lowering=False)
    k = K()
    k.nc = nc
    es = ExitStack()
    k.es = es

    A_DROP = ("fg", "wbm", "wbg", "wout", "wfg", "wfu", "wfd", "g2", "sel", "onem")

    def din(name, shape, dt=F32):
        if mode == "A" and name in A_DROP:
            return None
        return nc.dram_tensor(name, list(shape), dt, kind="ExternalInput").ap()

    xin = din("xin", [TT, D])
    maskrow_d = din("maskrow", [128, 128])
    negrow_d = din("negrow", [128, 128])
    sel_d = din("sel", [128, 8])
    onem_d = din("onem", [128, 8])
    ident_d = din("ident", [128, 128])
    cmask_d = din("cmask", [128, 128])
    rst_d = din("rst", [128, 512])
    selh_d = din("selh", [4, 512])
    g1_d = din("g1", [128, 8])
    g2_d = din("g2", [128, 8])
    cw_d = din("cw", [128, 32])
    cb_d = din("cb", [128, 8])
    mgb_d = din("mgb", [4, 2])
    a2b_d = din("a2b", [128, 4])
    hg_d = din("hg", [128, 2048])
    fg_d = din("fg", [128, 1024])
    w_in = din("w_in", [D, NPROJ])
    g_a2 = din("g_a2", [16, 512])
    wbm = din("wbm", [D, D])
    wbg = din("wbg", [D, D])
    wout = din("wout", [D, D])
    wfg = din("wfg", [D, DFF])
    wfu = din("wfu", [D, DFF])
    wfd = din("wfd", [DFF, D])
    if mode != "A":
        out_d = nc.dram_tensor("out", [TL, D], F32, kind="ExternalOutput").ap()
    dbg_d = {}
    for name, shape in DEBUG.items():
        dbg_d[name] = nc.dram_tensor("dbg_" + name, list(shape), F32, kind="ExternalOutput").ap()

    if mode == "fused":
        pub_t = nc.dram_tensor("pub", [128, 8 * 258], F32)
        gath_t = nc.dram_tensor("gath", [8 * 128, 8 * 258], F32)
        pub = pub_t.ap()
        gath = gath_t.ap()
    elif mode == "A":
        pub = nc.dram_tensor("pub", [128, 8 * 258], F32, kind="ExternalOutput").ap()
        gath = None
    else:
        pub = None
        gath = nc.dram_tensor("gath", [8 * 128, 8 * 258], F32, kind="ExternalInput").ap()

    def sem(name):
        return es.enter_context(nc.semaphore(name))

    NEP = 3
    k.pe = Eng(nc.tensor, "pe", [sem(f"pe{i}") for i in range(NEP)], is_pe=True)
    k.act = Eng(nc.scalar, "act", [sem(f"act{i}") for i in range(NEP)])
    k.dve = Eng(nc.vector, "dve", [sem(f"dve{i}") for i in range(NEP)])
    k.pool = Eng(nc.gpsimd, "pool", [sem(f"pool{i}") for i in range(NEP)])
    k.sp = Eng(nc.sync, "sp", [sem("sp0")])
    k.qs = DmaQ(k.sp, [sem(f"qs{i}") for i in range(8)], "qs")
    k.qg = DmaQ(k.pool, [sem(f"qg{i}") for i in range(8)], "qg")
    ccsem = sem("cc")
    pe, act, dve, pool, qs, qg = k.pe, k.act, k.dve, k.pool, k.qs, k.qg
    op, dma = k.op, k.dma

    def sb(name, shape, dt=F32):
        return es.enter_context(nc.sbuf_tensor(name, list(shape), dt))

    cst = sb("cst", [128, 8])
    maskrow = sb("maskrow_s", [128, 128])
    negrow = sb("negrow_s", [128, 128])
    sel = sb("sel_s", [128, 8])
    onem = sb("onem_s", [128, 8])
    identb = sb("identb", [128, 128], BF16)
    identf = sb("identf", [128, 128])
    cmask = sb("cmask_s", [128, 128])
    rst = sb("rst_s", [128, 512])
    selh = sb("selh_s", [4, 512])
    g1 = sb("g1_s", [128, 8])
    g2 = sb("g2_s", [128, 8])
    cw = sb("cw_s", [128, 32])
    cb = sb("cb_s", [128, 8])
    mgb = sb("mgb_s", [4, 2])
    nbf = sb("nbf_s", [4, 1])
    a2b = sb("a2b_s", [128, 4])
    na2b = sb("na2b_s", [128, 4])
    hg = sb("hg_s", [128, 2048])
    fg = sb("fg_s", [128, 1024])
    etok = sb("etok", [128, NT, 8])
    abc = sb("abc", [128, 4, 36])
    ssA = sb("ssA", [128, 64])
    rsA = sb("rsA", [128, 64])
    tmpA = sb("tmpA", [128, 64])
    junk = sb("junk", [128, 1024], BF16)
    wa2 = sb("wa2", [16, 512], BF16)
    wif = sb("wif", [128, 8, 8], BF16)

    R1 = sb("R1", [128, KT * TT], BF16)
    R2 = sb("R2", [128, 32768], BF16)
    R3 = sb("R3", [128, 20480])

    xnT = R1[:, :].rearrange("p (k t) -> p k t", k=KT)
    hnT = R1[:, 0:KT * TL].rearrange("p (k t) -> p k t", k=KT)
    yT = R2[:, :].rearrange("p (k t) -> p k t", k=16)
    h2 = R2[:, :].bitcast(F32).rearrange("p (t d) -> p t d", t=16)

    class Arena:
        def __init__(self):
            self.off = 0

        def f32(self, n):
            a = R3[:, self.off:self.off + n]
            self.off += n
            assert self.off <= 20480, self.off
            return a

        def bf16(self, n):
            n2 = (n + 1) // 2
            a = R3[:, self.off:self.off + n2].bitcast(BF16)
            self.off += n2
            assert self.off <= 20480, self.off
            return a[:, 0:n]

    ps = [es.enter_context(nc.psum_tensor(f"ps{i}", [128, 512], F32)) for i in range(8)]
    psb = [Buf(f"ps{i}", excl=True) for i in range(8)]

    B_c = Buf("consts")
    for dst, src in ((maskrow, maskrow_d), (negrow, negrow_d), (sel, sel_d), (onem, onem_d),
                     (identf, ident_d), (cmask, cmask_d), (rst, rst_d), (selh, selh_d),
                     (g1, g1_d), (g2, g2_d), (cw, cw_d), (cb, cb_d), (mgb, mgb_d),
                     (a2b, a2b_d), (hg, hg_d), (fg, fg_d)):
        if src is not None:
            dma(qs, dst[:], src[:, :], writes=[B_c])
    dma(qg, identb[:], ident_d[:, :], writes=[B_c])
    dma(qg, wa2[:], g_a2[:, :], writes=[B_c])
    dma(qg, wif[:], w_in[:, 2048:2056].rearrange("(k p) c -> p k c", p=128), writes=[B_c])
    B_cst = Buf("cst")
    for col, val in enumerate((EPS, 1.0, -LN_C, LN_C, 0.0, -0.5)):
        op(pool, lambda col=col, val=val: nc.gpsimd.memset(cst[:, col:col + 1], float(val)), writes=[B_cst])
    c_eps, c_one, c_lnsq, c_lnc, c_zero = (cst[:, i:i + 1] for i in range(5))
    k.barrier()
    B_small = Buf("small")
    op(dve, lambda: nc.vector.tensor_scalar(nbf[:], mgb[:, 1:2], -1.0, None, ALU.mult), reads=[B_c], writes=[B_small])
    op(dve, lambda: nc.vector.tensor_scalar(na2b[:], a2b[:], -1.0, None, ALU.mult), reads=[B_c], writes=[B_small])

    Bss = Buf("ss")
    rs_idx = [0]

    def rstd(ss_ap, n, npart=128, width=1):
        i = rs_idx[0] % (64 // width)
        rs_idx[0] += 1
        t1 = tmpA[0:npart, i * width:(i + 1) * width]
        o = rsA[0:npart, i * width:(i + 1) * width]
        op(act, lambda: nc.scalar.activation(out=t1, in_=ss_ap, func=AF.Ln, bias=c_eps[0:npart], scale=1.0 / n),
           reads=[Bss, B_cst], writes=[Bss])
        op(act, lambda: nc.scalar.activation(out=o, in_=t1, func=AF.Exp, scale=-0.5), reads=[Bss], writes=[Bss])
        return o

    def sigmoid_into(out_ap, in_ap, ta, tb, rbufs, wbufs, neg_bias=None):
        if neg_bias is None:
            op(act, lambda: nc.scalar.activation(out=ta, in_=in_ap, func=AF.Exp, scale=-1.0), reads=rbufs, writes=wbufs)
        else:
            op(act, lambda: nc.scalar.activation(out=ta, in_=in_ap, func=AF.Exp, scale=-1.0, bias=neg_bias),
               reads=rbufs, writes=wbufs)
        op(act, lambda: nc.scalar.activation(out=tb, in_=ta, func=AF.Ln, bias=c_one[0:ta.shape[0]], scale=1.0),
           reads=list(wbufs) + [B_cst], writes=wbufs)
        op(act, lambda: nc.scalar.activation(out=out_ap, in_=tb, func=AF.Exp, scale=-1.0), reads=wbufs, writes=wbufs)

    def dbg(name, src_ap, rbufs):
        if name in dbg_d:
            dma(qs, dbg_d[name], src_ap, reads=rbufs)

    ar = Arena()
    xt = [ar.f32(1024) for _ in range(2)]
    xnb = [ar.bf16(1024) for _ in range(2)]
    Bxt = [Buf("xt0"), Buf("xt1")]
    Bxnb = [Buf("xnb0"), Buf("xnb1")]
    BxnT = [Buf(f"xnT{t}") for t in range(NT)]
    psT = ps[7][:, :].bitcast(BF16)
    BpsT = psb[7]
    for t in range(NT):
        i = t % 2
        dma(qs, xt[i], xin[t * 128:(t + 1) * 128, :], writes=[Bxt[i]])
        op(act, lambda: nc.scalar.activation(out=junk[:], in_=xt[i], func=AF.Square, accum_out=ssA[:, t:t + 1]),
           reads=[Bxt[i]], writes=[Bss])
        r = rstd(ssA[:, t:t + 1], D)
        op(act, lambda: nc.scalar.activation(out=xnb[i], in_=xt[i], func=AF.Copy, scale=r),
           reads=[Bxt[i], Bss], writes=[Bxnb[i]])
        for kk in range(KT):
            op(pe, lambda kk=kk: nc.tensor.transpose(psT[:, kk * 128:(kk + 1) * 128], xnb[i][:, kk * 128:(kk + 1) * 128], identb[:]),
               reads=[Bxnb[i], B_c], writes=[BpsT])
        op(dve, lambda: nc.vector.tensor_tensor(
            out=xnT[:, :, t * 128:(t + 1) * 128], in0=psT.rearrange("p (k t) -> p k t", k=KT),
            in1=g1[:, :].unsqueeze(2).to_broadcast([128, KT, 128]), op=ALU.mult),
           reads=[BpsT, B_c], writes=[BxnT[t]])
    if "xnT" in dbg_d:
        xf = ar.f32(TT)
        Bxf = Buf()
        op(dve, lambda: nc.vector.tensor_copy(out=xf, in_=xnT[:, 0, :]), reads=BxnT, writes=[Bxf])
        dbg("xnT", xf, [Bxf])
    k.barrier()

    ar = Arena()
    li = ar.f32(TT)[0:4]
    t1g = ar.f32(TT)[0:4]
    t2g = ar.f32(TT)[0:4]
    nbg = ar.f32(TT)[0:4]
    rg = ar.f32(TT)[0:4]
    eg = ar.f32(TT)[0:4]
    flg = ar.f32(TT)[0:4]
    smallg = ar.f32(256)[0:4]
    Rg = smallg[:, 0:34]
    gg_ = smallg[:, 34:68]
    marr = smallg[:, 68:103]
    Mg = smallg[:, 103:137]
    dg = smallg[:, 137:171]
    cat = smallg[:, 171:207]
    sumnb = smallg[:, 207:208]
    gaT = sb("gaT", [16, TT], BF16)
    Bg = Buf("gates")
    BgaT = Buf("gaT")
    for (t0, ntl) in BLOCKS:
        n = ntl * 128
        c0 = t0 * 128
        for j, (pst, pb) in enumerate(((ps[0], psb[0]), (ps[1], psb[1]))):
            for kk in range(KT):
                op(pe, lambda kk=kk, j=j, pst=pst: nc.tensor.matmul(
                    pst[0:4, 0:n], lhsT=wif[:, kk, j * 4:(j + 1) * 4], rhs=xnT[:, kk, c0:c0 + n],
                    start=(kk == 0), stop=(kk == KT - 1)), reads=[B_c] + BxnT[t0:t0 + ntl], writes=[pb])
        op(act, lambda: nc.scalar.activation(out=li[:, c0:c0 + n], in_=ps[0][0:4, 0:n], func=AF.Identity,
                                             bias=mgb[:, 0:1], scale=1.0), reads=[psb[0], B_c], writes=[Bg])
        op(act, lambda: nc.scalar.activation(out=t1g[:, c0:c0 + n], in_=ps[1][0:4, 0:n], func=AF.Exp,
                                             bias=nbf[:, 0:1], scale=-1.0), reads=[psb[1], B_small], writes=[Bg])
        op(act, lambda: nc.scalar.activation(out=t2g[:, c0:c0 + n], in_=t1g[:, c0:c0 + n], func=AF.Ln,
                                             bias=c_one[0:4], scale=1.0), reads=[Bg, B_cst], writes=[Bg])
    op(dve, lambda: nc.vector.tensor_tensor(out=t2g[:, 0:128], in0=t2g[:, 0:128], in1=maskrow[0:4, :], op=ALU.mult),
       reads=[Bg, B_c], writes=[Bg])
    for (t0, ntl) in BLOCKS:
        n = ntl * 128
        c0 = t0 * 128
        op(dve, lambda: nc.vector.tensor_tensor_scan(out=nbg[:, c0:c0 + n], data0=rst[0:4, 0:n], data1=t2g[:, c0:c0 + n],
                                                     initial=0.0, op0=ALU.mult, op1=ALU.add), reads=[Bg, B_c], writes=[Bg])
    op(dve, lambda: nc.vector.tensor_tensor(out=rg, in0=li, in1=nbg, op=ALU.add), reads=[Bg], writes=[Bg])
    op(dve, lambda: nc.vector.tensor_tensor(out=rg[:, 0:128], in0=rg[:, 0:128], in1=maskrow[0:4, :], op=ALU.mult),
       reads=[Bg, B_c], writes=[Bg])
    op(dve, lambda: nc.vector.tensor_tensor(out=rg[:, 0:128], in0=rg[:, 0:128], in1=negrow[0:4, :], op=ALU.add),
       reads=[Bg, B_c], writes=[Bg])
    rg3 = rg.rearrange("p (c l) -> p c l", l=64)
    nb3 = nbg.rearrange("p (c l) -> p c l", l=64)
    op(dve, lambda: nc.vector.tensor_reduce(out=Rg, in_=rg3, axis=AX.X, op=ALU.max), reads=[Bg], writes=[Bg])
    op(dve, lambda: nc.vector.tensor_scalar(gg_, nb3[:, :, 63], -1.0, None, ALU.mult), reads=[Bg], writes=[Bg])
    op(dve, lambda: nc.vector.memset(marr[:, 0:1], 0.0), writes=[Bg])
    op(dve, lambda: nc.vector.tensor_tensor_scan(out=marr[:, 1:35], data0=Rg, data1=gg_, initial=0.0,
                                                 op0=ALU.max, op1=ALU.add), reads=[Bg], writes=[Bg])
    op(dve, lambda: nc.vector.tensor_tensor(out=Mg, in0=marr[:, 0:34], in1=Rg, op=ALU.max), reads=[Bg], writes=[Bg])
    op(dve, lambda: nc.vector.tensor_tensor(out=dg, in0=marr[:, 0:34], in1=Mg, op=ALU.subtract), reads=[Bg], writes=[Bg])
    op(act, lambda: nc.scalar.activation(out=cat[:, 0:34], in_=dg, func=AF.Exp), reads=[Bg], writes=[Bg])
    op(act, lambda: nc.scalar.activation(out=cat[:, 34:35], in_=marr[:, 34:35], func=AF.Exp), reads=[Bg], writes=[Bg])
    op(dve, lambda: nc.vector.tensor_reduce(out=sumnb, in_=nb3[:, :, 63], axis=AX.X, op=ALU.add), reads=[Bg], writes=[Bg])
    op(act, lambda: nc.scalar.activation(out=cat[:, 35:36], in_=sumnb, func=AF.Exp, scale=-1.0), reads=[Bg], writes=[Bg])
    Mb = Mg.unsqueeze(2).to_broadcast([4, NCH, 64])
    op(dve, lambda: nc.vector.tensor_tensor(out=rg3, in0=rg3, in1=Mb, op=ALU.subtract), reads=[Bg], writes=[Bg])
    op(act, lambda: nc.scalar.activation(out=eg, in_=rg, func=AF.Exp), reads=[Bg], writes=[Bg])
    op(dve, lambda: nc.vector.tensor_tensor(out=nb3, in0=nb3, in1=Mb, op=ALU.subtract), reads=[Bg], writes=[Bg])
    op(act, lambda: nc.scalar.activation(out=flg, in_=nbg, func=AF.Exp, bias=c_lnsq[0:4], scale=1.0),
       reads=[Bg, B_cst], writes=[Bg])
    for h in range(4):
        op(pe, lambda h=h: nc.tensor.matmul(ps[0][:, h * 36:(h + 1) * 36], lhsT=selh[0:4, h * 128:(h + 1) * 128],
                                            rhs=cat, start=True, stop=True), reads=[Bg, B_c], writes=[psb[0]])
    Babc = Buf("abc")
    op(dve, lambda: nc.vector.tensor_copy(out=abc[:, :, :], in_=ps[0][:, 0:144].rearrange("p (h c) -> p h c", h=4)),
       reads=[psb[0]], writes=[Babc])
    for t in range(NT):
        op(pe, lambda t=t: nc.tensor.matmul(ps[1][:, t * 8:t * 8 + 4], lhsT=eg[:, t * 128:(t + 1) * 128],
                                            rhs=identf[0:4, 0:4], start=True, stop=True), reads=[Bg, B_c], writes=[psb[1]])
        op(pe, lambda t=t: nc.tensor.matmul(ps[1][:, t * 8 + 4:t * 8 + 8], lhsT=flg[:, t * 128:(t + 1) * 128],
                                            rhs=identf[0:4, 0:4], start=True, stop=True), reads=[Bg, B_c], writes=[psb[1]])
    Betok = Buf("etok")
    op(dve, lambda: nc.vector.tensor_copy(out=etok[:, :, :], in_=ps[1][:, 0:NT * 8].rearrange("p (t c) -> p t c", c=8)),
       reads=[psb[1]], writes=[Betok])
    w16 = ar.bf16(KT * 16).rearrange("p (k c) -> p k c", k=KT)
    Bw16 = Buf("w16")
    dma(qg, w16, w_in[:, 5128:5144].rearrange("(k p) c -> p k c", p=128), writes=[Bw16])
    for (t0, ntl) in BLOCKS:
        n = ntl * 128
        c0 = t0 * 128
        for kk in range(KT):
            op(pe, lambda kk=kk: nc.tensor.matmul(ps[2][0:16, 0:n], lhsT=w16[:, kk, :], rhs=xnT[:, kk, c0:c0 + n],
                                                  start=(kk == 0), stop=(kk == KT - 1)),
               reads=[Bw16] + BxnT[t0:t0 + ntl], writes=[psb[2]])
        op(act, lambda: nc.scalar.activation(out=gaT[:, c0:c0 + n], in_=ps[2][0:16, 0:n], func=AF.Copy),
           reads=[psb[2]], writes=[BgaT])
    dbg("etok", etok[:, :, :].rearrange("p t c -> p (t c)"), [Betok])
    dbg("abc", abc[:, :, :].rearrange("p h c -> p (h c)"), [Babc])
    k.barrier()

    ar = Arena()
    wbuf = [ar.bf16(KT * 768).rearrange("p (k c) -> p k c", k=KT) for _ in range(2)]
    Bw = [[Buf(f"w{s_}_{j}") for j in range(4)] for s_ in range(2)]
    kpre = ar.f32(516)
    qpre = ar.f32(516)
    kc = ar.f32(512)
    qc = ar.f32(512)
    ta = ar.f32(512)
    tb = ar.f32(512)
    sgt = ar.f32(512)
    nbt = ar.f32(512)
    ekt = ar.f32(512)
    kTb = [ar.bf16(512) for _ in range(2)]
    qTb = [ar.bf16(512) for _ in range(2)]
    ktok = ar.bf16(8 * 128).rearrange("p (t c) -> p t c", t=8)
    vtok = ar.bf16(8 * 258).rearrange("p (t c) -> p t c", t=8)
    Gt = ar.f32(8 * 256).rearrange("p (t c) -> p t c", t=8)
    ga_ = ar.f32(256)
    gb_ = ar.f32(256)
    gc_ = ar.f32(256)
    go_ = ar.f32(256)
    attb = ar.bf16(128)
    Sbf = [ar.bf16(258) for _ in range(4)]
    Xs = [ar.f32(258) for _ in range(2)]
    ybuf = [ar.bf16(256) for _ in range(2)]
    gsb = ar.f32(8 * 258).rearrange("p (r c) -> p r c", r=8)
    pubb = ar.f32(258)
    avec = ar.f32(40)
    nbend = ar.f32(40)
    decp = ar.f32(8)
    tiny = ar.f32(16)
    Bkpre, Bqpre, Bkc, Bqc = Buf(), Buf(), Buf(), Buf()
    Bta, Bnb, Bek = Buf(), Buf(), Buf()
    BkT = [Buf(), Buf()]
    BqT = [Buf(), Buf()]
    Bktok = [Buf() for _ in range(8)]
    Bvtok = [Buf() for _ in range(8)]
    BG = [Buf() for _ in range(8)]
    Bgt = Buf()
    Batt = Buf()
    BS = [Buf() for _ in range(4)]
    BX = [Buf(), Buf()]
    By = [Buf(), Buf()]
    Bgs, Bpub, Btiny = Buf(), Buf(), Buf()
    Bav = [Buf() for _ in range(len(BLOCKS) + 1)]
    ByT = [Buf(f"yT{t}") for t in range(16)]

    def load_unit_weights(u, phase, slot):
        m = u < 4
        h = u % 4
        if m:
            cq, ck, cv, co = h * 128, 512 + h * 128, 1024 + h * 256, 2056 + h * 256
        else:
            cq, ck, cv, co = 3080 + h * 128, 3592 + h * 128, 4104 + h * 256, 5144 + h * 256
        w = wbuf[slot]

        def ld(dst0, c, n, bi_):
            dma(qg, w[:, :, dst0:dst0 + n], w_in[:, c:c + n].rearrange("(k p) c -> p k c", p=128), writes=[Bw[slot][bi_]])
        ld(128, ck, 128, 1)
        ld(256, cv, 256, 2)
        if phase == 2:
            ld(0, cq, 128, 0)
            ld(512, co, 256, 3)

    psQ, psK, psV, psA, psO = ps[0], ps[1], ps[2], ps[3], ps[6]
    BpsQ, BpsK, BpsV, BpsA, BpsO = psb[0], psb[1], psb[2], psb[3], psb[6]
    psP = [ps[4], ps[5]]
    BpsP = [psb[4], psb[5]]
    TILES = [(bi, tt) for bi, (t0, ntl) in enumerate(BLOCKS) for tt in range(ntl)]

    def mixer_unit(u, phase, slot):
        m = u < 4
        h = u % 4
        NV = 257 if m else 256
        w = wbuf[slot]
        full = phase == 2
        st = {"xi": 0, "sidx": {}}
        op(dve, lambda: nc.vector.memset(Xs[0][:, 0:NV], 0.0), writes=[BX[0]])
        if full:
            dma(qs, gsb[:, :, :], gath[:, u * 258:(u + 1) * 258].rearrange("(r p) c -> p r c", p=128), writes=[Bgs])
            op(dve, lambda: nc.vector.tensor_tensor(out=decp[:, 0:8], in0=gsb[:, :, 257], in1=sel[:, :], op=ALU.mult),
               reads=[Bgs, B_c], writes=[Btiny])
            op(dve, lambda: nc.vector.tensor_tensor(out=decp[:, 0:8], in0=decp[:, 0:8], in1=onem[:, :], op=ALU.add),
               reads=[Btiny, B_c], writes=[Btiny])
            op(dve, lambda: nc.vector.tensor_tensor(out=gsb[:, 0:7, 0:NV], in0=gsb[:, 0:7, 0:NV],
                                                    in1=sel[:, 0:7].unsqueeze(2).to_broadcast([128, 7, NV]), op=ALU.mult),
               reads=[B_c, Bgs], writes=[Bgs])
            for j in range(7):
                op(dve, lambda j=j: nc.vector.scalar_tensor_tensor(out=Xs[0][:, 0:NV], in0=Xs[0][:, 0:NV],
                                                                   scalar=decp[:, j:j + 1], in1=gsb[:, j, 0:NV],
                                                                   op0=ALU.mult, op1=ALU.add),
                   reads=[Bgs, Btiny], writes=[BX[0]])
        if m:
            op(pool, lambda: nc.gpsimd.memset(kpre[:, 0:3], 0.0), writes=[Bkpre])
            if full:
                op(pool, lambda: nc.gpsimd.memset(qpre[:, 0:3], 0.0), writes=[Bqpre])
        else:
            op(pool, lambda: nc.gpsimd.memset(avec[:, 0:1], 1.0), writes=[Bav[0]])

        def S0_stages(bi):
            t0, ntl = BLOCKS[bi]
            par = bi % 2
            n = ntl * 128
            c0 = t0 * 128
            rx = BxnT[t0:t0 + ntl]
            kT_, qT_ = kTb[par], qTb[par]

            def proj_k():
                for kk in range(KT):
                    op(pe, lambda kk=kk: nc.tensor.matmul(psK[:, 0:n], lhsT=w[:, kk, 128:256], rhs=xnT[:, kk, c0:c0 + n],
                                                          start=(kk == 0), stop=(kk == KT - 1)), reads=[Bw[slot][1]] + rx, writes=[BpsK])

            def proj_q():
                for kk in range(KT):
                    op(pe, lambda kk=kk: nc.tensor.matmul(psQ[:, 0:n], lhsT=w[:, kk, 0:128], rhs=xnT[:, kk, c0:c0 + n],
                                                          start=(kk == 0), stop=(kk == KT - 1)), reads=[Bw[slot][0]] + rx, writes=[BpsQ])

            if m:
                chains = [(psK, BpsK, kpre, Bkpre, kc, Bkc, kT_, BkT[par], 4 + h)]
                if full:
                    chains.append((psQ, BpsQ, qpre, Bqpre, qc, Bqc, qT_, BqT[par], h))

                def f1():
                    proj_k()
                    if full:
                        proj_q()
                    for (psx, Bpsx, pre, Bpre, cx, Bcx, outb, Bout, ktile) in chains:
                        op(act, lambda psx=psx, pre=pre: nc.scalar.activation(out=pre[:, 3:3 + n], in_=psx[:, 0:n], func=AF.Copy),
                           reads=[Bpsx], writes=[Bpre])

                def f2(chain):
                    for (psx, Bpsx, pre, Bpre, cx, Bcx, outb, Bout, ktile) in [chain]:
                        wv = cw[:, ktile * 4:(ktile + 1) * 4]
                        op(dve, lambda cx=cx, pre=pre, wv=wv, ktile=ktile: nc.vector.tensor_scalar(
                            cx[:, 0:n], pre[:, 0:n], wv[:, 0:1], cb[:, ktile:ktile + 1], ALU.mult, ALU.add),
                           reads=[Bpre, B_c], writes=[Bcx])
                        for i in range(1, 4):
                            op(dve, lambda i=i, cx=cx, pre=pre, wv=wv: nc.vector.scalar_tensor_tensor(
                                out=cx[:, 0:n], in0=pre[:, i:i + n], scalar=wv[:, i:i + 1], in1=cx[:, 0:n], op0=ALU.mult, op1=ALU.add),
                               reads=[Bpre, B_c, Bcx], writes=[Bcx])
                        op(pool, lambda pre=pre: nc.gpsimd.tensor_copy(out=pre[:, 0:3], in_=pre[:, n:n + 3]), reads=[Bpre], writes=[Bpre])

                def f3(chain):
                    for (psx, Bpsx, pre, Bpre, cx, Bcx, outb, Bout, ktile) in [chain]:
                        sigmoid_into(sgt[:, 0:n], cx[:, 0:n], ta[:, 0:n], tb[:, 0:n], [Bcx], [Bta])
                        op(pool, lambda outb=outb, cx=cx: nc.gpsimd.tensor_tensor(out=outb[:, 0:n], in0=cx[:, 0:n], in1=sgt[:, 0:n], op=ALU.mult),
                           reads=[Bcx, Bta], writes=[Bout])
            else:
                nch = 2 * ntl
                nb3_ = nbt[:, 0:n].rearrange("p (c l) -> p c l", l=64)

                def f1():
                    op(pe, lambda: nc.tensor.matmul(psQ[:, 0:n], lhsT=wa2[:, h * 128:(h + 1) * 128], rhs=gaT[:, c0:c0 + n],
                                                    start=True, stop=True), reads=[B_c, BgaT], writes=[BpsQ])
                    op(act, lambda: nc.scalar.activation(out=ta[:, 0:n], in_=psQ[:, 0:n], func=AF.Exp, bias=na2b[:, h:h + 1], scale=-1.0),
                       reads=[BpsQ, B_small], writes=[Bta])
                    proj_k()
                    if full:
                        proj_q()
                    op(act, lambda: nc.scalar.activation(out=tb[:, 0:n], in_=ta[:, 0:n], func=AF.Ln, bias=c_one, scale=1.0),
                       reads=[Bta, B_cst], writes=[Bta])

                def f2():
                    if bi == 0:
                        op(dve, lambda: nc.vector.tensor_tensor(out=tb[:, 0:128], in0=tb[:, 0:128], in1=maskrow[:, :], op=ALU.mult),
                           reads=[Bta, B_c], writes=[Bta])
                    op(dve, lambda: nc.vector.tensor_tensor_scan(out=nbt[:, 0:n], data0=rst[:, 0:n], data1=tb[:, 0:n], initial=0.0,
                                                                 op0=ALU.mult, op1=ALU.add), reads=[Bta, B_c], writes=[Bnb])

                def f2b():
                    op(act, lambda: nc.scalar.activation(out=avec[:, 2 * t0 + 1:2 * t0 + 1 + nch], in_=nb3_[:, :, 63], func=AF.Exp,
                                                         scale=-1.0 / 16.0), reads=[Bnb], writes=[Bav[bi + 1]])
                    if not full:
                        op(dve, lambda: nc.vector.tensor_copy(out=nbend[:, 2 * t0:2 * t0 + nch], in_=nb3_[:, :, 63]), reads=[Bnb],
                           writes=[Bav[bi + 1]])
                    op(act, lambda: nc.scalar.activation(out=ekt[:, 0:n], in_=nbt[:, 0:n], func=AF.Exp, scale=1.0 / 16.0),
                       reads=[Bnb], writes=[Bek])

                def f3():
                    if bi == 0:
                        op(dve, lambda: nc.vector.tensor_tensor(out=ekt[:, 0:128], in0=ekt[:, 0:128], in1=maskrow[:, :], op=ALU.mult),
                           reads=[Bek, B_c], writes=[Bek])
                    op(dve, lambda: nc.vector.tensor_tensor(out=kT_[:, 0:n], in0=psK[:, 0:n], in1=ekt[:, 0:n], op=ALU.mult),
                       reads=[BpsK, Bek], writes=[BkT[par]])

                def f3b():
                    if full:
                        op(act, lambda: nc.scalar.activation(out=ekt[:, 0:n], in_=nbt[:, 0:n], func=AF.Exp, bias=c_lnc, scale=-1.0 / 16.0),
                           reads=[Bnb, B_cst], writes=[Bek])
                        op(dve, lambda: nc.vector.tensor_tensor(out=qT_[:, 0:n], in0=psQ[:, 0:n], in1=ekt[:, 0:n], op=ALU.mult),
                           reads=[BpsQ, Bek], writes=[BqT[par]])
            if m:
                stages = [f1] + [(lambda c=c: f2(c)) for c in chains] + [(lambda c=c: f3(c)) for c in chains]
            else:
                stages = [f1, f2, f2b, f3, f3b]
            return stages + [lambda: S0b(bi)]

        def S0b(bi):
            t0, ntl = BLOCKS[bi]
            par = bi % 2
            n = ntl * 128
            kT_ = kTb[par]
            for tt in range(ntl):
                op(pe, lambda tt=tt: nc.tensor.transpose(psT[:, tt * 128:(tt + 1) * 128], kT_[:, tt * 128:(tt + 1) * 128], identb[:]),
                   reads=[BkT[par], B_c], writes=[BpsT])
            op(act, lambda: nc.scalar.activation(out=ktok[:, par * 4:par * 4 + ntl, :],
                                                 in_=psT[:, 0:n].rearrange("p (t c) -> p t c", t=ntl),
                                                 func=AF.Copy), reads=[BpsT], writes=Bktok[par * 4:par * 4 + ntl])

        def S1(bi, tt):
            t0, ntl = BLOCKS[bi]
            t = t0 + tt
            sl = (bi % 2) * 4 + tt
            outp = full and t >= 1
            ncol = 512 if outp else 256
            for kk in range(KT):
                op(pe, lambda kk=kk: nc.tensor.matmul(psV[:, 0:ncol], lhsT=xnT[:, kk, t * 128:(t + 1) * 128],
                                                      rhs=w[:, kk, 256:256 + ncol], start=(kk == 0), stop=(kk == KT - 1)),
                   reads=[Bw[slot][2], Bw[slot][3], BxnT[t]], writes=[BpsV])
            if m:
                op(act, lambda: nc.scalar.activation(out=vtok[:, sl, 0:256], in_=psV[:, 0:256], func=AF.Copy,
                                                     scale=etok[:, t, h:h + 1]), reads=[BpsV, Betok], writes=[Bvtok[sl]])
                op(pool, lambda: nc.gpsimd.tensor_copy(out=vtok[:, sl, 256:257], in_=etok[:, t, h:h + 1]),
                   reads=[Betok, Bvtok[sl]], writes=[Bvtok[sl]])
            else:
                op(act, lambda: nc.scalar.activation(out=vtok[:, sl, 0:256], in_=psV[:, 0:256], func=AF.Copy),
                   reads=[BpsV], writes=[Bvtok[sl]])
            if outp:
                hgv = hg[:, u * 256:(u + 1) * 256]
                if m:
                    sigmoid_into(gc_, psV[:, 256:512], ga_, gb_, [BpsV], [Bgt])
                    op(pool, lambda: nc.gpsimd.tensor_tensor(out=Gt[:, sl, :], in0=gc_, in1=hgv, op=ALU.mult),
                       reads=[Bgt, B_c], writes=[BG[sl]])
                else:
                    sigmoid_into(gc_, psV[:, 256:512], ga_, gb_, [BpsV], [Bgt])
                    op(pool, lambda: nc.gpsimd.tensor_tensor(out=gc_, in0=gc_, in1=hgv, op=ALU.mult),
                       reads=[Bgt, B_c], writes=[Bgt])
                    op(dve, lambda: nc.vector.tensor_tensor(out=Gt[:, sl, :], in0=psV[:, 256:512], in1=gc_, op=ALU.mult),
                       reads=[BpsV, Bgt], writes=[BG[sl]])

        def S2a(bi, tt):
            t0, ntl = BLOCKS[bi]
            t = t0 + tt
            par = bi % 2
            sl = par * 4 + tt
            outp = full and t >= 1
            sidx = []
            st["sidx"][t] = sidx
            for half in range(2):
                c = 2 * t + half
                rows = slice(half * 64, (half + 1) * 64)
                a_c = abc[:, h, c:c + 1] if m else avec[:, c:c + 1]
                Ba = Babc if m else (Bav[0] if c == 0 else Bav[(c - 1) // 8 + 1])
                xi = st["xi"]
                xin_, xout_ = Xs[xi], Xs[1 - xi]
                if outp:
                    si = c % 4
                    sidx.append(si)
                    op(dve, lambda: nc.vector.tensor_scalar(Sbf[si][:, 0:NV], xin_[:, 0:NV], a_c, None, ALU.mult),
                       reads=[BX[xi], Ba], writes=[BS[si]])
                pp = c % 2
                op(pe, lambda: nc.tensor.matmul(psP[pp][:, 0:NV], lhsT=ktok[rows, sl, :], rhs=vtok[rows, sl, 0:NV],
                                                start=True, stop=True), reads=[Bktok[sl], Bvtok[sl]], writes=[BpsP[pp]])
                op(dve, lambda: nc.vector.scalar_tensor_tensor(out=xout_[:, 0:NV], in0=xin_[:, 0:NV], scalar=a_c,
                                                               in1=psP[pp][:, 0:NV], op0=ALU.mult, op1=ALU.add),
                   reads=[BX[xi], Ba, BpsP[pp]], writes=[BX[1 - xi]])
                st["xi"] = 1 - xi

        def S2b(bi, tt):
            t0, ntl = BLOCKS[bi]
            t = t0 + tt
            par = bi % 2
            sl = par * 4 + tt
            outp = full and t >= 1
            if not outp:
                return
            kT_, qT_ = kTb[par], qTb[par]
            sidx = st["sidx"][t]
            tq = slice(tt * 128, (tt + 1) * 128)
            op(pe, lambda: nc.tensor.matmul(psA[:, 0:128], lhsT=kT_[:, tq], rhs=qT_[:, tq], start=True, stop=True),
               reads=[BkT[par], BqT[par]], writes=[BpsA])
            op(dve, lambda: nc.vector.tensor_tensor(out=attb, in0=psA[:, 0:128], in1=cmask[:, :], op=ALU.mult),
               reads=[BpsA, B_c], writes=[Batt])
            op(pe, lambda: nc.tensor.matmul(psO[:, 0:NV], lhsT=attb, rhs=vtok[:, sl, 0:NV], start=True, stop=False),
               reads=[Batt, Bvtok[sl]], writes=[BpsO])
            op(pe, lambda: nc.tensor.matmul(psO[0:64, 0:NV], lhsT=qT_[:, tt * 128:tt * 128 + 64], rhs=Sbf[sidx[0]][:, 0:NV],
                                            start=False, stop=False), reads=[BqT[par], BS[sidx[0]]], writes=[BpsO])
            op(pe, lambda: nc.tensor.matmul(psO[64:128, 0:NV], lhsT=qT_[:, tt * 128 + 64:tt * 128 + 128], rhs=Sbf[sidx[1]][:, 0:NV],
                                            start=False, stop=True), reads=[BqT[par], BS[sidx[1]]], writes=[BpsO])

        def S2c(bi, tt):
            t0, ntl = BLOCKS[bi]
            t = t0 + tt
            sl = (bi % 2) * 4 + tt
            if not (full and t >= 1):
                return
            ci = (u * 16 + (t - 1)) % 16
            ssc = ssA[:, 32 + ci:33 + ci]
            if m:
                d1 = tiny[:, 0:1]
                rden = tiny[:, 1:2]
                op(dve, lambda: nc.vector.scalar_tensor_tensor(out=tiny[:, 3:4], in0=psO[:, 256:257], scalar=-1.0,
                                                               in1=etok[:, t, 4 + h:5 + h], op0=ALU.mult, op1=ALU.max),
                   reads=[BpsO, Betok], writes=[Btiny])
                op(dve, lambda: nc.vector.tensor_tensor(out=d1, in0=psO[:, 256:257], in1=tiny[:, 3:4], op=ALU.max),
                   reads=[BpsO, Btiny], writes=[Btiny])
                op(dve, lambda: nc.vector.reciprocal(out=rden, in_=d1), reads=[Btiny], writes=[Btiny])
                op(act, lambda: nc.scalar.activation(out=junk[:, 0:256], in_=psO[:, 0:256], func=AF.Square, scale=rden,
                                                     accum_out=ssc), reads=[BpsO, Btiny], writes=[Bss])
            else:
                op(act, lambda: nc.scalar.activation(out=junk[:, 0:256], in_=psO[:, 0:256], func=AF.Square, accum_out=ssc),
                   reads=[BpsO], writes=[Bss])
            r = rstd(ssc, 256)
            if m:
                sc = tiny[:, 2:3]
                op(dve, lambda: nc.vector.tensor_tensor(out=sc, in0=rden, in1=r, op=ALU.mult), reads=[Btiny, Bss], writes=[Btiny])
                rb = [Btiny]
            else:
                sc = r
                rb = [Bss]
            yi = t % 2
            op(dve, lambda: nc.vector.scalar_tensor_tensor(out=ybuf[yi], in0=psO[:, 0:256], scalar=sc, in1=Gt[:, sl, :],
                                                           op0=ALU.mult, op1=ALU.mult), reads=[BpsO, BG[sl]] + rb, writes=[By[yi]])

        def S3(bi, tt):
            t0, ntl = BLOCKS[bi]
            t = t0 + tt
            if not (full and t >= 1):
                return
            yi = t % 2
            for j in range(2):
                op(pe, lambda j=j: nc.tensor.transpose(psT[:, 512 + j * 128:512 + (j + 1) * 128], ybuf[yi][:, j * 128:(j + 1) * 128],
                                                       identb[:]), reads=[By[yi], B_c], writes=[BpsT])
            op(dve, lambda: nc.vector.tensor_copy(out=yT[:, 2 * u:2 * u + 2, (t - 1) * 128:t * 128],
                                                  in_=psT[:, 512:768].rearrange("p (j c) -> p j c", j=2)),
               reads=[BpsT], writes=[ByT[t - 1]])

        for f in S0_stages(0):
            f()
        S1(*TILES[0])
        sched = {}
        for i, (bi, tt) in enumerate(TILES):
            ntl_b = BLOCKS[bi][1]
            if tt == 0 and bi + 1 < len(BLOCKS):
                stages = S0_stages(bi + 1)
                L = len(stages) - 1
                nslot = 2 * ntl_b - 2
                for j, f in enumerate(stages[:-1]):
                    slot_ = (j * nslot) // L if nslot > 0 else 0
                    sched.setdefault((i + slot_ // 2, slot_ % 2), []).append(f)
                sched.setdefault((i + ntl_b - 1, 0), []).append(stages[-1])
            for f in sched.pop((i, 0), []):
                f()
            if full:
                if i == 0:
                    S2a(bi, tt)
                if i + 1 < len(TILES):
                    S1(*TILES[i + 1])
                S2b(bi, tt)
                if i + 1 < len(TILES):
                    S2a(*TILES[i + 1])
                S2c(bi, tt)
            else:
                if i + 1 < len(TILES):
                    S1(*TILES[i + 1])
                S2a(bi, tt)
            if i >= 1:
                S3(*TILES[i - 1])
            for f in sched.pop((i, 1), []):
                f()
        assert not sched, sched.keys()
        S3(*TILES[-1])
        xi = st["xi"]
        if not full:
            xf_ = Xs[xi]
            if m:
                fin = abc[:, h, 34:35]
                dec = abc[:, h, 35:36]
                Bf = [Babc]
            else:
                fin = avec[:, NCH:NCH + 1]
                op(dve, lambda: nc.vector.tensor_reduce(out=tiny[:, 4:5], in_=nbend[:, 0:NCH], axis=AX.X, op=ALU.add),
                   reads=Bav, writes=[Btiny])
                op(act, lambda: nc.scalar.activation(out=tiny[:, 5:6], in_=tiny[:, 4:5], func=AF.Exp, scale=-1.0 / 16.0),
                   reads=[Btiny], writes=[Btiny])
                dec = tiny[:, 5:6]
                Bf = Bav
            op(dve, lambda: nc.vector.memset(pubb[:, 0:258], 0.0), writes=[Bpub])
            op(dve, lambda: nc.vector.tensor_scalar(pubb[:, 0:NV], xf_[:, 0:NV], fin, None, ALU.mult),
               reads=[BX[xi]] + Bf, writes=[Bpub])
            op(dve, lambda: nc.vector.tensor_copy(out=pubb[:, 257:258], in_=dec), reads=Bf + [Btiny], writes=[Bpub])
            dma(qs, pub[:, u * 258:(u + 1) * 258], pubb[:, 0:258], reads=[Bpub])

    for phase in {"fused": (1, 2), "A": (1,), "B": (2,)}[mode]:
        load_unit_weights(0, phase, 0)
        for u in range(8):
            if u + 1 < 8:
                load_unit_weights(u + 1, phase, (u + 1) % 2)
            mixer_unit(u, phase, u % 2)
        if phase == 1 and mode == "A":
            for q in (qs, qg):
                for t_ in q.last:
                    k.sp.wait_tok(t_)
            k.barrier()
            es.close()
            return nc
        if phase == 1:
            for t_ in qs.last:
                pool.wait_tok(t_)
            import os
            if os.environ.get("NOCC"):
                nc.gpsimd.dma_start(out=gath[0:128, :], in_=pub[:, :]).then_inc(ccsem, 16)
                k.sp.e.wait_ge(ccsem, 16)
                nc.gpsimd.wait_ge(ccsem, 16)
            else:
                nc.gpsimd.collective_compute("AllGather", ALU.bypass, replica_groups=[list(range(NCORES))],
                                             ins=[pub_t.ap().opt()], outs=[gath_t.ap().opt()]).then_inc(ccsem)
                k.sp.e.wait_ge(ccsem, 1)
                nc.gpsimd.wait_ge(ccsem, 1)
            for e in (pe, act, dve, pool):
                e.new_epoch()
    if "yT" in dbg_d:
        yf = R3[:, 0:2048]
        Byf = Buf()
        k.barrier()
        op(dve, lambda: nc.vector.tensor_copy(out=yf, in_=yT[:, 0, :]), reads=ByT, writes=[Byf])
        dbg("yT", yf, [Byf])
    k.barrier()

    ar = Arena()
    mergedT = ar.bf16(KT * TL).rearrange("p (k t) -> p k t", k=KT)
    x1_mark = ar.off
    wx = [ar.bf16(KT * 512).rearrange("p (k j c) -> p k j c", k=KT, j=4) for _ in range(2)]
    Bwx = [[Buf() for _ in range(4)] for _ in range(2)]
    sa = ar.f32(512)
    sb_ = ar.f32(512)
    sgm = ar.f32(512)
    sgg = ar.f32(512)
    m1 = ar.f32(512)
    m2 = ar.f32(512)
    Bsa, Bsgm, Bsgg, Bm1, Bm2 = Buf(), Buf(), Buf(), Buf(), Buf()
    BmT = [Buf() for _ in range(4)]

    def load_wx(o, slot):
        cs = slice(o * 128, (o + 1) * 128)
        dma(qg, wx[slot][:, :, 0, :], wbm[:, cs].rearrange("(k p) c -> p k c", p=128), writes=[Bwx[slot][0]])
        dma(qg, wx[slot][:, :, 1, :], wbg[:, cs].rearrange("(k p) c -> p k c", p=128), writes=[Bwx[slot][1]])
        dma(qg, wx[slot][:, :, 2, :], w_in[:, 6168 + o * 128:6168 + (o + 1) * 128].rearrange("(k p) c -> p k c", p=128), writes=[Bwx[slot][2]])
        dma(qg, wx[slot][:, :, 3, :], w_in[:, 7192 + o * 128:7192 + (o + 1) * 128].rearrange("(k p) c -> p k c", p=128), writes=[Bwx[slot][3]])

    load_wx(0, 0)
    it = 0
    for o in range(8):
        if o + 1 < 8:
            load_wx(o + 1, (o + 1) % 2)
        s = o % 2
        for b in range(4):
            pb = (it % 2) * 4
            it += 1
            y0 = b * 512
            c0 = 128 + b * 512
            for j in range(4):
                for kk in range(KT):
                    if j == 0:
                        rhs = yT[:, kk, y0:y0 + 512]
                    elif j == 1:
                        rhs = yT[:, 8 + kk, y0:y0 + 512]
                    else:
                        rhs = xnT[:, kk, c0:c0 + 512]
                    op(pe, lambda kk=kk, j=j, rhs=rhs: nc.tensor.matmul(ps[pb + j][:, :], lhsT=wx[s][:, kk, j, :], rhs=rhs,
                                                                        start=(kk == 0), stop=(kk == KT - 1)),
                       reads=[Bwx[s][j]], writes=[psb[pb + j]])
            sigmoid_into(sgm, ps[pb + 2][:, :], sa, sb_, [psb[pb + 2]], [Bsgm])
            sigmoid_into(sgg, ps[pb + 3][:, :], sa, sb_, [psb[pb + 3]], [Bsgg])
            op(dve, lambda: nc.vector.tensor_tensor(out=m1, in0=ps[pb][:, :], in1=sgm, op=ALU.mult), reads=[psb[pb], Bsgm], writes=[Bm1])
            op(dve, lambda: nc.vector.tensor_tensor(out=m2, in0=ps[pb + 1][:, :], in1=sgg, op=ALU.mult), reads=[psb[pb + 1], Bsgg], writes=[Bm2])
            op(pool, lambda: nc.gpsimd.tensor_tensor(out=mergedT[:, o, y0:y0 + 512], in0=m1, in1=m2, op=ALU.add),
               reads=[Bm1, Bm2], writes=[BmT[b]])
    k.barrier()

    ar = Arena()
    ar.off = x1_mark
    wo = ar.bf16(KT * 1024).rearrange("p (k c) -> p k c", k=KT)
    Bwo = [Buf(), Buf()]
    dma(qg, wo[:, :, 0:512], wout[:, 0:512].rearrange("(k p) c -> p k c", p=128), writes=[Bwo[0]])
    dma(qg, wo[:, :, 512:1024], wout[:, 512:1024].rearrange("(k p) c -> p k c", p=128), writes=[Bwo[1]])
    xt = [ar.f32(1024) for _ in range(2)]
    hnb = [ar.bf16(1024) for _ in range(2)]
    Bxt = [Buf(), Buf()]
    Bhnb = [Buf(), Buf()]
    Bh2 = [Buf(f"h2_{t}") for t in range(16)]
    BhnT = [Buf(f"hnT{t}") for t in range(16)]
    def X2a(t):
        i = t % 2
        pb_ = (t % 2) * 2
        dma(qs, xt[i], xin[(t + 1) * 128:(t + 2) * 128, :], writes=[Bxt[i]])
        for half in range(2):
            for kk in range(KT):
                op(pe, lambda kk=kk, half=half: nc.tensor.matmul(ps[pb_ + half][:, :], lhsT=mergedT[:, kk, t * 128:(t + 1) * 128],
                                                                 rhs=wo[:, kk, half * 512:(half + 1) * 512],
                                                                 start=(kk == 0), stop=(kk == KT - 1)),
                   reads=[Bwo[half], BmT[t // 4]], writes=[psb[pb_ + half]])
            op(dve, lambda half=half: nc.vector.tensor_tensor(out=h2[:, t, half * 512:(half + 1) * 512], in0=ps[pb_ + half][:, :],
                                                              in1=xt[i][:, half * 512:(half + 1) * 512], op=ALU.add),
               reads=[psb[pb_ + half], Bxt[i]], writes=[Bh2[t]])

    def X2b(t):
        i = t % 2
        op(act, lambda: nc.scalar.activation(out=junk[:], in_=h2[:, t, :], func=AF.Square, accum_out=ssA[:, t:t + 1]),
           reads=[Bh2[t]], writes=[Bss])
        r = rstd(ssA[:, t:t + 1], D)
        op(act, lambda: nc.scalar.activation(out=hnb[i], in_=h2[:, t, :], func=AF.Copy, scale=r), reads=[Bh2[t], Bss], writes=[Bhnb[i]])

    def X2c(t):
        i = t % 2
        for kk in range(KT):
            op(pe, lambda kk=kk: nc.tensor.transpose(psT[:, kk * 128:(kk + 1) * 128], hnb[i][:, kk * 128:(kk + 1) * 128], identb[:]),
               reads=[Bhnb[i], B_c], writes=[BpsT])
        op(dve, lambda: nc.vector.tensor_tensor(out=hnT[:, :, t * 128:(t + 1) * 128], in0=psT.rearrange("p (k t) -> p k t", k=KT),
                                                in1=g2[:, :].unsqueeze(2).to_broadcast([128, KT, 128]), op=ALU.mult),
           reads=[BpsT, B_c], writes=[BhnT[t]])

    X2a(0)
    for t in range(16):
        if t + 1 < 16:
            X2a(t + 1)
        X2b(t)
        if t >= 1:
            X2c(t - 1)
    X2c(15)
    k.barrier()

    ar = Arena()
    JB = [(0, 4), (4, 4), (8, 4), (12, 4), (16, 4), (20, 2)]
    wg_ = [ar.bf16(KT * 512).rearrange("p (k c) -> p k c", k=KT) for _ in range(2)]
    wu_ = [ar.bf16(KT * 512).rearrange("p (k c) -> p k c", k=KT) for _ in range(2)]
    wd_ = [ar.bf16(4 * 1024).rearrange("p (j c) -> p j c", j=4) for _ in range(2)]
    Bwf = [[Buf() for _ in range(3)] for _ in range(2)]
    ffT = [ar.bf16(4 * 512).rearrange("p (j c) -> p j c", j=4) for _ in range(2)]
    Bff = [Buf(), Buf()]
    fa = ar.f32(512)
    fb = ar.f32(512)
    fsg = ar.f32(512)
    fgs = ar.f32(512)
    Bfa, Bfsg, Bfgs = Buf(), Buf(), Buf()
    ot = [ar.f32(1024) for _ in range(2)]
    Bot = [Buf(), Buf()]

    def load_wf(J, slot):
        j0, nj = JB[J]
        n = nj * 128
        dma(qg, wg_[slot][:, :, 0:n], wfg[:, j0 * 128:j0 * 128 + n].rearrange("(k p) c -> p k c", p=128), writes=[Bwf[slot][0]])
        dma(qg, wu_[slot][:, :, 0:n], wfu[:, j0 * 128:j0 * 128 + n].rearrange("(k p) c -> p k c", p=128), writes=[Bwf[slot][1]])
        dma(qg, wd_[slot][:, 0:nj, :], wfd[j0 * 128:j0 * 128 + n, :].rearrange("(j p) c -> p j c", p=128), writes=[Bwf[slot][2]])

    load_wf(0, 0)
    itg = 0
    ito = 0
    for J in range(len(JB)):
        if J + 1 < len(JB):
            load_wf(J + 1, (J + 1) % 2)
        s = J % 2
        j0, nj = JB[J]
        for b in range(4):
            fi = (J * 4 + b) % 2
            for j in range(nj):
                pg = (itg % 2) * 2
                itg += 1
                for which, wsrc in ((0, wg_[s]), (1, wu_[s])):
                    for kk in range(KT):
                        op(pe, lambda kk=kk, which=which, wsrc=wsrc: nc.tensor.matmul(
                            ps[pg + which][:, :], lhsT=wsrc[:, kk, j * 128:(j + 1) * 128], rhs=hnT[:, kk, b * 512:(b + 1) * 512],
                            start=(kk == 0), stop=(kk == KT - 1)), reads=[Bwf[s][which]] + BhnT[b * 4:(b + 1) * 4], writes=[psb[pg + which]])
                sigmoid_into(fsg, ps[pg][:, :], fa, fb, [psb[pg]], [Bfsg])
                op(dve, lambda: nc.vector.tensor_tensor(out=fgs, in0=ps[pg][:, :], in1=fsg, op=ALU.mult), reads=[psb[pg], Bfsg], writes=[Bfgs])
                op(dve, lambda j=j: nc.vector.tensor_tensor(out=ffT[fi][:, j, :], in0=ps[pg + 1][:, :], in1=fgs, op=ALU.mult),
                   reads=[psb[pg + 1], Bfgs], writes=[Bff[fi]])
            for tt in range(4):
                t = b * 4 + tt
                po = 4 + (ito % 2) * 2
                ito += 1
                for half in range(2):
                    for j in range(nj):
                        op(pe, lambda j=j, half=half: nc.tensor.matmul(
                            ps[po + half][:, :], lhsT=ffT[fi][:, j, tt * 128:(tt + 1) * 128], rhs=wd_[s][:, j, half * 512:(half + 1) * 512],
                            start=(j == 0), stop=(j == nj - 1)), reads=[Bff[fi], Bwf[s][2]], writes=[psb[po + half]])
                    op(dve, lambda half=half: nc.vector.tensor_tensor(out=h2[:, t, half * 512:(half + 1) * 512], in0=ps[po + half][:, :],
                                                                      in1=h2[:, t, half * 512:(half + 1) * 512], op=ALU.add),
                       reads=[psb[po + half], Bh2[t]], writes=[Bh2[t]])
    for t in range(16):
        i = t % 2
        op(act, lambda: nc.scalar.activation(out=junk[:], in_=h2[:, t, :], func=AF.Square, accum_out=ssA[:, 16 + t:17 + t]),
           reads=[Bh2[t]], writes=[Bss])
        r = rstd(ssA[:, 16 + t:17 + t], D)
        op(dve, lambda: nc.vector.scalar_tensor_tensor(out=ot[i], in0=h2[:, t, :], scalar=r, in1=fg[:, :], op0=ALU.mult, op1=ALU.mult),
           reads=[Bh2[t], Bss, B_c], writes=[Bot[i]])
        dma(qs, out_d[t * 128:(t + 1) * 128, :], ot[i], reads=[Bot[i]])
    for q in (qs, qg):
        for t_ in q.last:
            k.sp.wait_tok(t_)
    k.barrier()
    es.close()
    return nc


_CACHE = {}


def _host_inputs(x, meta_tokens, norm1_g, w_in, conv_w, conv_b, m_gate_b, g_a2, g_a2_b, m_head_g, g_head_g,
                 w_branch_m, w_branch_g, w_out, norm2_g, w_ff_gate, w_ff_up, w_ff_down, final_g):
    f = np.float32
    x2 = np.asarray(x, f)[0]
    pre0 = np.zeros((128, D), f)
    pre0[112:128] = np.asarray(meta_tokens, f)

    def pk(v):
        return np.ascontiguousarray(np.asarray(v, f).reshape(KT, 128).T)
    ident = np.eye(128, dtype=f)
    idx = np.arange(128)
    cmask = ((idx[:, None] // 64 == idx[None, :] // 64) & (idx[:, None] <= idx[None, :])).astype(f)
    rst = np.ones((128, 512), f)
    rst[:, ::64] = 0.0
    selh = np.zeros((4, 4, 128), f)
    for h in range(4):
        selh[h, h, :] = 1.0
    selh = selh.reshape(4, 512)
    cw = np.asarray(conv_w, f)[0]
    cw_p = np.ascontiguousarray(cw.reshape(4, KT, 128).transpose(2, 1, 0).reshape(128, 32))
    mgb = np.ascontiguousarray(np.asarray(m_gate_b, f)[0].T)
    a2b = np.ascontiguousarray(np.asarray(g_a2_b, f)[0].reshape(4, 128).T)
    hgrow = np.concatenate([np.asarray(m_head_g, f)[0].reshape(-1), np.asarray(g_head_g, f)[0].reshape(-1)])
    hg = np.ascontiguousarray(np.broadcast_to(hgrow[None, :], (128, 2048)))
    fgb = np.ascontiguousarray(np.broadcast_to(np.asarray(final_g, f)[None, :], (128, 1024)))
    shared = dict(
        ident=ident, cmask=cmask, rst=rst, selh=selh, g1=pk(np.asarray(norm1_g)[0]), g2=pk(np.asarray(norm2_g)[0]),
        cw=cw_p, cb=pk(np.asarray(conv_b)[0]), mgb=mgb, a2b=a2b, hg=hg, fg=fgb,
        w_in=np.ascontiguousarray(np.asarray(w_in, f)[0]), g_a2=np.ascontiguousarray(np.asarray(g_a2, f)[0]),
        wbm=np.ascontiguousarray(np.asarray(w_branch_m, f)[0]), wbg=np.ascontiguousarray(np.asarray(w_branch_g, f)[0]),
        wout=np.ascontiguousarray(np.asarray(w_out, f)[0]), wfg=np.ascontiguousarray(np.asarray(w_ff_gate, f)[0]),
        wfu=np.ascontiguousarray(np.asarray(w_ff_up, f)[0]), wfd=np.ascontiguousarray(np.asarray(w_ff_down, f)[0]),
    )
    maps = []
    for c in range(NCORES):
        if c == 0:
            xin = np.concatenate([pre0, x2[0:TL]], axis=0)
            valid = (np.arange(128) >= 112).astype(f)
        else:
            xin = x2[c * TL - 128:(c + 1) * TL]
            valid = np.zeros(128, f)
        maskrow = np.ascontiguousarray(np.broadcast_to(valid[None, :], (128, 128)))
        negrow = np.ascontiguousarray(np.broadcast_to(((valid - 1.0) * 1e30)[None, :], (128, 128))).astype(f)
        selv = (np.arange(8) < c).astype(f)
        sel = np.ascontiguousarray(np.broadcast_to(selv[None, :], (128, 8)))
        onem = np.ascontiguousarray(1.0 - sel).astype(f)
        m = dict(shared)
        m.update(xin=np.ascontiguousarray(xin, dtype=f), maskrow=maskrow, negrow=negrow, sel=sel, onem=onem)
        maps.append(m)
    return maps


def kernel(**inputs):
    maps = _host_inputs(**inputs)
    if MODE == "fused":
        if "nc" not in _CACHE:
            _CACHE["nc"] = build_program("fused")
        res = run_bass_kernel_spmd(_CACHE["nc"], maps, core_ids=list(range(NCORES)))
    else:
        if "ncA" not in _CACHE:
            _CACHE["ncA"] = build_program("A")
            _CACHE["ncB"] = build_program("B")
        drop = ("fg", "wbm", "wbg", "wout", "wfg", "wfu", "wfd", "g2", "sel", "onem")
        mapsA = [{k_: v for k_, v in m.items() if k_ not in drop} for m in maps]
        resA = run_bass_kernel_spmd(_CACHE["ncA"], mapsA, core_ids=list(range(NCORES)))
        gath = np.ascontiguousarray(np.concatenate([resA.results[c]["pub"] for c in range(NCORES)], axis=0))
        for m in maps:
            m["gath"] = gath
        res = run_bass_kernel_spmd(_CACHE["ncB"], maps, core_ids=list(range(NCORES)))
    _CACHE["last"] = res
    out = np.concatenate([res.results[c]["out"] for c in range(NCORES)], axis=0)
    return out[None].astype(np.float32)
```

```python
import math
from contextlib import ExitStack

import numpy as np
import concourse.bass as bass
import concourse.mybir as mybir
from concourse.bass_utils import run_bass_kernel_spmd

F32 = mybir.dt.float32
BF16 = mybir.dt.bfloat16
AF = mybir.ActivationFunctionType
ALU = mybir.AluOpType
AX = mybir.AxisListType

NCORES = 8
D = 1024
KT = 8
TL = 2048
NT = 17
TT = NT * 128
NCH = 2 * NT
DFF = 2816
NPROJ = 8216
EPS = 1e-6
BLOCKS = [(0, 4), (4, 4), (8, 4), (12, 4), (16, 1)]
LN_C = -0.5 * math.log(128.0)

DEBUG = {}
MODE = "split"


class Tok:
    __slots__ = ("sem", "val", "eng", "key")

    def __init__(self, sem, val, eng, key):
        self.sem, self.val, self.eng, self.key = sem, val, eng, key


class Buf:
    __slots__ = ("name", "w", "r", "excl")

    def __init__(self, name="", excl=False):
        self.name = name
        self.w = None
        self.r = {}
        self.excl = excl


class Eng:
    def __init__(self, e, name, sems, is_pe=False):
        self.e = e
        self.name = name
        self.sems = sems
        self.ep = 0
        self.cnt = 0
        self.is_pe = is_pe
        self.waited = {}
        self.last = None

    def wait_tok(self, tok):
        if tok is None:
            return
        if tok.eng is self and self.is_pe:
            return
        if self.waited.get(tok.key, 0) >= tok.val:
            return
        self.e.wait_ge(tok.sem, tok.val)
        self.waited[tok.key] = tok.val

    def bump(self, ins):
        sem = self.sems[self.ep]
        ins.then_inc(sem, 1)
        self.cnt += 1
        t = Tok(sem, self.cnt, self, (self.name, self.ep))
        self.last = t
        return t

    def new_epoch(self):
        self.ep += 1
        self.cnt = 0


class DmaQ:
    def __init__(self, eng, sems, name):
        self.eng = eng
        self.sems = sems
        self.cnt = [0] * len(sems)
        self.last = [None] * len(sems)
        self.i = 0
        self.name = name

    def issue(self, out, in_, **kw):
        i = self.i
        self.i = (self.i + 1) % len(self.sems)
        self.eng.wait_tok(self.last[i])
        ins = self.eng.e.dma_start(out=out, in_=in_, **kw)
        ins.then_inc(self.sems[i], 16)
        self.cnt[i] += 16
        t = Tok(self.sems[i], self.cnt[i], None, (self.name, i))
        self.last[i] = t
        return t


class K:
    def op(self, eng, fn, reads=(), writes=()):
        ex = [b for b in reads if b.excl]
        if ex:
            reads = [b for b in reads if not b.excl]
            writes = list(writes) + [b for b in ex if b not in writes]
        for b in reads:
            eng.wait_tok(b.w)
        for b in writes:
            eng.wait_tok(b.w)
            for t in b.r.values():
                eng.wait_tok(t)
        tok = eng.bump(fn())
        for b in reads:
            b.r[tok.key] = tok
        for b in writes:
            b.w = tok
            b.r = {}
        return tok

    def dma(self, q, out, in_, reads=(), writes=(), **kw):
        for b in reads:
            q.eng.wait_tok(b.w)
        for b in writes:
            q.eng.wait_tok(b.w)
            for t in b.r.values():
                q.eng.wait_tok(t)
        tok = q.issue(out, in_, **kw)
        for b in reads:
            b.r[tok.key] = tok
        for b in writes:
            b.w = tok
            b.r = {}
        return tok

    def barrier(self):
        toks = []
        for e in (self.pe, self.act, self.dve, self.pool):
            if e.last is not None:
                toks.append(e.last)
        for q in (self.qs, self.qg):
            for t in q.last:
                if t is not None:
                    toks.append(t)
        for e in (self.pe, self.act, self.dve, self.pool, self.sp):
            for t in toks:
                if t.eng is e:
                    continue
                e.wait_tok(t)


def build_program(mode="fused"):
    nc = bass.Bass("TRN2", target_bir_lowering=False)
    k = K()
    k.nc = nc
    es = ExitStack()
    k.es = es

    A_DROP = ("fg", "wbm", "wbg", "wout", "wfg", "wfu", "wfd", "g2", "sel", "onem")

    def din(name, shape, dt=F32):
        if mode == "A" and name in A_DROP:
            return None
        return nc.dram_tensor(name, list(shape), dt, kind="ExternalInput").ap()

    xin = din("xin", [TT, D])
    maskrow_d = din("maskrow", [128, 128])
    negrow_d = din("negrow", [128, 128])
    sel_d = din("sel", [128, 8])
    onem_d = din("onem", [128, 8])
    ident_d = din("ident", [128, 128])
    cmask_d = din("cmask", [128, 128])
    rst_d = din("rst", [128, 512])
    selh_d = din("selh", [4, 512])
    g1_d = din("g1", [128, 8])
    g2_d = din("g2", [128, 8])
    cw_d = din("cw", [128, 32])
    cb_d = din("cb", [128, 8])
    mgb_d = din("mgb", [4, 2])
    a2b_d = din("a2b", [128, 4])
    hg_d = din("hg", [128, 2048])
    fg_d = din("fg", [128, 1024])
    w_in = din("w_in", [D, NPROJ])
    g_a2 = din("g_a2", [16, 512])
    wbm = din("wbm", [D, D])
    wbg = din("wbg", [D, D])
    wout = din("wout", [D, D])
    wfg = din("wfg", [D, DFF])
    wfu = din("wfu", [D, DFF])
    wfd = din("wfd", [DFF, D])
    if mode != "A":
        out_d = nc.dram_tensor("out", [TL, D], F32, kind="ExternalOutput").ap()
    dbg_d = {}
    for name, shape in DEBUG.items():
        dbg_d[name] = nc.dram_tensor("dbg_" + name, list(shape), F32, kind="ExternalOutput").ap()

    if mode == "fused":
        pub_t = nc.dram_tensor("pub", [128, 8 * 258], F32)
        gath_t = nc.dram_tensor("gath", [8 * 128, 8 * 258], F32)
        pub = pub_t.ap()
        gath = gath_t.ap()
    elif mode == "A":
        pub = nc.dram_tensor("pub", [128, 8 * 258], F32, kind="ExternalOutput").ap()
        gath = None
    else:
        pub = None
        gath = nc.dram_tensor("gath", [8 * 128, 8 * 258], F32, kind="ExternalInput").ap()

    def sem(name):
        return es.enter_context(nc.semaphore(name))

    NEP = 3
    k.pe = Eng(nc.tensor, "pe", [sem(f"pe{i}") for i in range(NEP)], is_pe=True)
    k.act = Eng(nc.scalar, "act", [sem(f"act{i}") for i in range(NEP)])
    k.dve = Eng(nc.vector, "dve", [sem(f"dve{i}") for i in range(NEP)])
    k.pool = Eng(nc.gpsimd, "pool", [sem(f"pool{i}") for i in range(NEP)])
    k.sp = Eng(nc.sync, "sp", [sem("sp0")])
    k.qs = DmaQ(k.sp, [sem(f"qs{i}") for i in range(8)], "qs")
    k.qg = DmaQ(k.pool, [sem(f"qg{i}") for i in range(8)], "qg")
    ccsem = sem("cc")
    pe, act, dve, pool, qs, qg = k.pe, k.act, k.dve, k.pool, k.qs, k.qg
    op, dma = k.op, k.dma

    def sb(name, shape, dt=F32):
        return es.enter_context(nc.sbuf_tensor(name, list(shape), dt))

    cst = sb("cst", [128, 8])
    maskrow = sb("maskrow_s", [128, 128])
    negrow = sb("negrow_s", [128, 128])
    sel = sb("sel_s", [128, 8])
    onem = sb("onem_s", [128, 8])
    identb = sb("identb", [128, 128], BF16)
    identf = sb("identf", [128, 128])
    cmask = sb("cmask_s", [128, 128])
    rst = sb("rst_s", [128, 512])
    selh = sb("selh_s", [4, 512])
    g1 = sb("g1_s", [128, 8])
    g2 = sb("g2_s", [128, 8])
    cw = sb("cw_s", [128, 32])
    cb = sb("cb_s", [128, 8])
    mgb = sb("mgb_s", [4, 2])
    nbf = sb("nbf_s", [4, 1])
    a2b = sb("a2b_s", [128, 4])
    na2b = sb("na2b_s", [128, 4])
    hg = sb("hg_s", [128, 2048])
    fg = sb("fg_s", [128, 1024])
    etok = sb("etok", [128, NT, 8])
    abc = sb("abc", [128, 4, 36])
    ssA = sb("ssA", [128, 64])
    rsA = sb("rsA", [128, 64])
    tmpA = sb("tmpA", [128, 64])
    junk = sb("junk", [128, 1024], BF16)
    wa2 = sb("wa2", [16, 512], BF16)
    wif = sb("wif", [128, 8, 8], BF16)

    R1 = sb("R1", [128, KT * TT], BF16)
    R2 = sb("R2", [128, 32768], BF16)
    R3 = sb("R3", [128, 20480])

    xnT = R1[:, :].rearrange("p (k t) -> p k t", k=KT)
    hnT = R1[:, 0:KT * TL].rearrange("p (k t) -> p k t", k=KT)
    yT = R2[:, :].rearrange("p (k t) -> p k t", k=16)
    h2 = R2[:, :].bitcast(F32).rearrange("p (t d) -> p t d", t=16)

    class Arena:
        def __init__(self):
            self.off = 0

        def f32(self, n):
            a = R3[:, self.off:self.off + n]
            self.off += n
            assert self.off <= 20480, self.off
            return a

        def bf16(self, n):
            n2 = (n + 1) // 2
            a = R3[:, self.off:self.off + n2].bitcast(BF16)
            self.off += n2
            assert self.off <= 20480, self.off
            return a[:, 0:n]

    ps = [es.enter_context(nc.psum_tensor(f"ps{i}", [128, 512], F32)) for i in range(8)]
    psb = [Buf(f"ps{i}", excl=True) for i in range(8)]

    B_c = Buf("consts")
    for dst, src in ((maskrow, maskrow_d), (negrow, negrow_d), (sel, sel_d), (onem, onem_d),
                     (identf, ident_d), (cmask, cmask_d), (rst, rst_d), (selh, selh_d),
                     (g1, g1_d), (g2, g2_d), (cw, cw_d), (cb, cb_d), (mgb, mgb_d),
                     (a2b, a2b_d), (hg, hg_d), (fg, fg_d)):
        if src is not None:
            dma(qs, dst[:], src[:, :], writes=[B_c])
    dma(qg, identb[:], ident_d[:, :], writes=[B_c])
    dma(qg, wa2[:], g_a2[:, :], writes=[B_c])
    dma(qg, wif[:], w_in[:, 2048:2056].rearrange("(k p) c -> p k c", p=128), writes=[B_c])
    B_cst = Buf("cst")
    for col, val in enumerate((EPS, 1.0, -LN_C, LN_C, 0.0, -0.5)):
        op(pool, lambda col=col, val=val: nc.gpsimd.memset(cst[:, col:col + 1], float(val)), writes=[B_cst])
    c_eps, c_one, c_lnsq, c_lnc, c_zero = (cst[:, i:i + 1] for i in range(5))
    k.barrier()
    B_small = Buf("small")
    op(dve, lambda: nc.vector.tensor_scalar(nbf[:], mgb[:, 1:2], -1.0, None, ALU.mult), reads=[B_c], writes=[B_small])
    op(dve, lambda: nc.vector.tensor_scalar(na2b[:], a2b[:], -1.0, None, ALU.mult), reads=[B_c], writes=[B_small])

    Bss = Buf("ss")
    rs_idx = [0]

    def rstd(ss_ap, n, npart=128, width=1):
        i = rs_idx[0] % (64 // width)
        rs_idx[0] += 1
        t1 = tmpA[0:npart, i * width:(i + 1) * width]
        o = rsA[0:npart, i * width:(i + 1) * width]
        op(act, lambda: nc.scalar.activation(out=t1, in_=ss_ap, func=AF.Ln, bias=c_eps[0:npart], scale=1.0 / n),
           reads=[Bss, B_cst], writes=[Bss])
        op(act, lambda: nc.scalar.activation(out=o, in_=t1, func=AF.Exp, scale=-0.5), reads=[Bss], writes=[Bss])
        return o

    def sigmoid_into(out_ap, in_ap, ta, tb, rbufs, wbufs, neg_bias=None):
        if neg_bias is None:
            op(act, lambda: nc.scalar.activation(out=ta, in_=in_ap, func=AF.Exp, scale=-1.0), reads=rbufs, writes=wbufs)
        else:
            op(act, lambda: nc.scalar.activation(out=ta, in_=in_ap, func=AF.Exp, scale=-1.0, bias=neg_bias),
               reads=rbufs, writes=wbufs)
        op(act, lambda: nc.scalar.activation(out=tb, in_=ta, func=AF.Ln, bias=c_one[0:ta.shape[0]], scale=1.0),
           reads=list(wbufs) + [B_cst], writes=wbufs)
        op(act, lambda: nc.scalar.activation(out=out_ap, in_=tb, func=AF.Exp, scale=-1.0), reads=wbufs, writes=wbufs)

    def dbg(name, src_ap, rbufs):
        if name in dbg_d:
            dma(qs, dbg_d[name], src_ap, reads=rbufs)

    ar = Arena()
    xt = [ar.f32(1024) for _ in range(2)]
    xnb = [ar.bf16(1024) for _ in range(2)]
    Bxt = [Buf("xt0"), Buf("xt1")]
    Bxnb = [Buf("xnb0"), Buf("xnb1")]
    BxnT = [Buf(f"xnT{t}") for t in range(NT)]
    psT = ps[7][:, :].bitcast(BF16)
    BpsT = psb[7]
    for t in range(NT):
        i = t % 2
        dma(qs, xt[i], xin[t * 128:(t + 1) * 128, :], writes=[Bxt[i]])
        op(act, lambda: nc.scalar.activation(out=junk[:], in_=xt[i], func=AF.Square, accum_out=ssA[:, t:t + 1]),
           reads=[Bxt[i]], writes=[Bss])
        r = rstd(ssA[:, t:t + 1], D)
        op(act, lambda: nc.scalar.activation(out=xnb[i], in_=xt[i], func=AF.Copy, scale=r),
           reads=[Bxt[i], Bss], writes=[Bxnb[i]])
        for kk in range(KT):
            op(pe, lambda kk=kk: nc.tensor.transpose(psT[:, kk * 128:(kk + 1) * 128], xnb[i][:, kk * 128:(kk + 1) * 128], identb[:]),
               reads=[Bxnb[i], B_c], writes=[BpsT])
        op(dve, lambda: nc.vector.tensor_tensor(
            out=xnT[:, :, t * 128:(t + 1) * 128], in0=psT.rearrange("p (k t) -> p k t", k=KT),
            in1=g1[:, :].unsqueeze(2).to_broadcast([128, KT, 128]), op=ALU.mult),
           reads=[BpsT, B_c], writes=[BxnT[t]])
    if "xnT" in dbg_d:
        xf = ar.f32(TT)
        Bxf = Buf()
        op(dve, lambda: nc.vector.tensor_copy(out=xf, in_=xnT[:, 0, :]), reads=BxnT, writes=[Bxf])
        dbg("xnT", xf, [Bxf])
    k.barrier()

    ar = Arena()
    li = ar.f32(TT)[0:4]
    t1g = ar.f32(TT)[0:4]
    t2g = ar.f32(TT)[0:4]
    nbg = ar.f32(TT)[0:4]
    rg = ar.f32(TT)[0:4]
    eg = ar.f32(TT)[0:4]
    flg = ar.f32(TT)[0:4]
    smallg = ar.f32(256)[0:4]
    Rg = smallg[:, 0:34]
    gg_ = smallg[:, 34:68]
    marr = smallg[:, 68:103]
    Mg = smallg[:, 103:137]
    dg = smallg[:, 137:171]
    cat = smallg[:, 171:207]
    sumnb = smallg[:, 207:208]
    gaT = sb("gaT", [16, TT], BF16)
    Bg = Buf("gates")
    BgaT = Buf("gaT")
    for (t0, ntl) in BLOCKS:
        n = ntl * 128
        c0 = t0 * 128
        for j, (pst, pb) in enumerate(((ps[0], psb[0]), (ps[1], psb[1]))):
            for kk in range(KT):
                op(pe, lambda kk=kk, j=j, pst=pst: nc.tensor.matmul(
                    pst[0:4, 0:n], lhsT=wif[:, kk, j * 4:(j + 1) * 4], rhs=xnT[:, kk, c0:c0 + n],
                    start=(kk == 0), stop=(kk == KT - 1)), reads=[B_c] + BxnT[t0:t0 + ntl], writes=[pb])
        op(act, lambda: nc.scalar.activation(out=li[:, c0:c0 + n], in_=ps[0][0:4, 0:n], func=AF.Identity,
                                             bias=mgb[:, 0:1], scale=1.0), reads=[psb[0], B_c], writes=[Bg])
        op(act, lambda: nc.scalar.activation(out=t1g[:, c0:c0 + n], in_=ps[1][0:4, 0:n], func=AF.Exp,
                                             bias=nbf[:, 0:1], scale=-1.0), reads=[psb[1], B_small], writes=[Bg])
        op(act, lambda: nc.scalar.activation(out=t2g[:, c0:c0 + n], in_=t1g[:, c0:c0 + n], func=AF.Ln,
                                             bias=c_one[0:4], scale=1.0), reads=[Bg, B_cst], writes=[Bg])
    op(dve, lambda: nc.vector.tensor_tensor(out=t2g[:, 0:128], in0=t2g[:, 0:128], in1=maskrow[0:4, :], op=ALU.mult),
       reads=[Bg, B_c], writes=[Bg])
    for (t0, ntl) in BLOCKS:
        n = ntl * 128
        c0 = t0 * 128
        op(dve, lambda: nc.vector.tensor_tensor_scan(out=nbg[:, c0:c0 + n], data0=rst[0:4, 0:n], data1=t2g[:, c0:c0 + n],
                                                     initial=0.0, op0=ALU.mult, op1=ALU.add), reads=[Bg, B_c], writes=[Bg])
    op(dve, lambda: nc.vector.tensor_tensor(out=rg, in0=li, in1=nbg, op=ALU.add), reads=[Bg], writes=[Bg])
    op(dve, lambda: nc.vector.tensor_tensor(out=rg[:, 0:128], in0=rg[:, 0:128], in1=maskrow[0:4, :], op=ALU.mult),
       reads=[Bg, B_c], writes=[Bg])
    op(dve, lambda: nc.vector.tensor_tensor(out=rg[:, 0:128], in0=rg[:, 0:128], in1=negrow[0:4, :], op=ALU.add),
       reads=[Bg, B_c], writes=[Bg])
    rg3 = rg.rearrange("p (c l) -> p c l", l=64)
    nb3 = nbg.rearrange("p (c l) -> p c l", l=64)
    op(dve, lambda: nc.vector.tensor_reduce(out=Rg, in_=rg3, axis=AX.X, op=ALU.max), reads=[Bg], writes=[Bg])
    op(dve, lambda: nc.vector.tensor_scalar(gg_, nb3[:, :, 63], -1.0, None, ALU.mult), reads=[Bg], writes=[Bg])
    op(dve, lambda: nc.vector.memset(marr[:, 0:1], 0.0), writes=[Bg])
    op(dve, lambda: nc.vector.tensor_tensor_scan(out=marr[:, 1:35], data0=Rg, data1=gg_, initial=0.0,
                                                 op0=ALU.max, op1=ALU.add), reads=[Bg], writes=[Bg])
    op(dve, lambda: nc.vector.tensor_tensor(out=Mg, in0=marr[:, 0:34], in1=Rg, op=ALU.max), reads=[Bg], writes=[Bg])
    op(dve, lambda: nc.vector.tensor_tensor(out=dg, in0=marr[:, 0:34], in1=Mg, op=ALU.subtract), reads=[Bg], writes=[Bg])
    op(act, lambda: nc.scalar.activation(out=cat[:, 0:34], in_=dg, func=AF.Exp), reads=[Bg], writes=[Bg])
    op(act, lambda: nc.scalar.activation(out=cat[:, 34:35], in_=marr[:, 34:35], func=AF.Exp), reads=[Bg], writes=[Bg])
    op(dve, lambda: nc.vector.tensor_reduce(out=sumnb, in_=nb3[:, :, 63], axis=AX.X, op=ALU.add), reads=[Bg], writes=[Bg])
    op(act, lambda: nc.scalar.activation(out=cat[:, 35:36], in_=sumnb, func=AF.Exp, scale=-1.0), reads=[Bg], writes=[Bg])
    Mb = Mg.unsqueeze(2).to_broadcast([4, NCH, 64])
    op(dve, lambda: nc.vector.tensor_tensor(out=rg3, in0=rg3, in1=Mb, op=ALU.subtract), reads=[Bg], writes=[Bg])
    op(act, lambda: nc.scalar.activation(out=eg, in_=rg, func=AF.Exp), reads=[Bg], writes=[Bg])
    op(dve, lambda: nc.vector.tensor_tensor(out=nb3, in0=nb3, in1=Mb, op=ALU.subtract), reads=[Bg], writes=[Bg])
    op(act, lambda: nc.scalar.activation(out=flg, in_=nbg, func=AF.Exp, bias=c_lnsq[0:4], scale=1.0),
       reads=[Bg, B_cst], writes=[Bg])
    for h in range(4):
        op(pe, lambda h=h: nc.tensor.matmul(ps[0][:, h * 36:(h + 1) * 36], lhsT=selh[0:4, h * 128:(h + 1) * 128],
                                            rhs=cat, start=True, stop=True), reads=[Bg, B_c], writes=[psb[0]])
    Babc = Buf("abc")
    op(dve, lambda: nc.vector.tensor_copy(out=abc[:, :, :], in_=ps[0][:, 0:144].rearrange("p (h c) -> p h c", h=4)),
       reads=[psb[0]], writes=[Babc])
    for t in range(NT):
        op(pe, lambda t=t: nc.tensor.matmul(ps[1][:, t * 8:t * 8 + 4], lhsT=eg[:, t * 128:(t + 1) * 128],
                                            rhs=identf[0:4, 0:4], start=True, stop=True), reads=[Bg, B_c], writes=[psb[1]])
        op(pe, lambda t=t: nc.tensor.matmul(ps[1][:, t * 8 + 4:t * 8 + 8], lhsT=flg[:, t * 128:(t + 1) * 128],
                                            rhs=identf[0:4, 0:4], start=True, stop=True), reads=[Bg, B_c], writes=[psb[1]])
    Betok = Buf("etok")
    op(dve, lambda: nc.vector.tensor_copy(out=etok[:, :, :], in_=ps[1][:, 0:NT * 8].rearrange("p (t c) -> p t c", c=8)),
       reads=[psb[1]], writes=[Betok])
    w16 = ar.bf16(KT * 16).rearrange("p (k c) -> p k c", k=KT)
    Bw16 = Buf("w16")
    dma(qg, w16, w_in[:, 5128:5144].rearrange("(k p) c -> p k c", p=128), writes=[Bw16])
    for (t0, ntl) in BLOCKS:
        n = ntl * 128
        c0 = t0 * 128
        for kk in range(KT):
            op(pe, lambda kk=kk: nc.tensor.matmul(ps[2][0:16, 0:n], lhsT=w16[:, kk, :], rhs=xnT[:, kk, c0:c0 + n],
                                                  start=(kk == 0), stop=(kk == KT - 1)),
               reads=[Bw16] + BxnT[t0:t0 + ntl], writes=[psb[2]])
        op(act, lambda: nc.scalar.activation(out=gaT[:, c0:c0 + n], in_=ps[2][0:16, 0:n], func=AF.Copy),
           reads=[psb[2]], writes=[BgaT])
    dbg("etok", etok[:, :, :].rearrange("p t c -> p (t c)"), [Betok])
    dbg("abc", abc[:, :, :].rearrange("p h c -> p (h c)"), [Babc])
    k.barrier()

    ar = Arena()
    wbuf = [ar.bf16(KT * 768).rearrange("p (k c) -> p k c", k=KT) for _ in range(2)]
    Bw = [[Buf(f"w{s_}_{j}") for j in range(4)] for s_ in range(2)]
    kpre = ar.f32(516)
    qpre = ar.f32(516)
    kc = ar.f32(512)
    qc = ar.f32(512)
    ta = ar.f32(512)
    tb = ar.f32(512)
    sgt = ar.f32(512)
    nbt = ar.f32(512)
    ekt = ar.f32(512)
    kTb = [ar.bf16(512) for _ in range(2)]
    qTb = [ar.bf16(512) for _ in range(2)]
    ktok = ar.bf16(8 * 128).rearrange("p (t c) -> p t c", t=8)
    vtok = ar.bf16(8 * 258).rearrange("p (t c) -> p t c", t=8)
    Gt = ar.f32(8 * 256).rearrange("p (t c) -> p t c", t=8)
    ga_ = ar.f32(256)
    gb_ = ar.f32(256)
    gc_ = ar.f32(256)
    go_ = ar.f32(256)
    attb = ar.bf16(128)
    Sbf = [ar.bf16(258) for _ in range(4)]
    Xs = [ar.f32(258) for _ in range(2)]
    ybuf = [ar.bf16(256) for _ in range(2)]
    gsb = ar.f32(8 * 258).rearrange("p (r c) -> p r c", r=8)
    pubb = ar.f32(258)
    avec = ar.f32(40)
    nbend = ar.f32(40)
    decp = ar.f32(8)
    tiny = ar.f32(16)
    Bkpre, Bqpre, Bkc, Bqc = Buf(), Buf(), Buf(), Buf()
    Bta, Bnb, Bek = Buf(), Buf(), Buf()
    BkT = [Buf(), Buf()]
    BqT = [Buf(), Buf()]
    Bktok = [Buf() for _ in range(8)]
    Bvtok = [Buf() for _ in range(8)]
    BG = [Buf() for _ in range(8)]
    Bgt = Buf()
    Batt = Buf()
    BS = [Buf() for _ in range(4)]
    BX = [Buf(), Buf()]
    By = [Buf(), Buf()]
    Bgs, Bpub, Btiny = Buf(), Buf(), Buf()
    Bav = [Buf() for _ in range(len(BLOCKS) + 1)]
    ByT = [Buf(f"yT{t}") for t in range(16)]

    def load_unit_weights(u, phase, slot):
        m = u < 4
        h = u % 4
        if m:
            cq, ck, cv, co = h * 128, 512 + h * 128, 1024 + h * 256, 2056 + h * 256
        else:
            cq, ck, cv, co = 3080 + h * 128, 3592 + h * 128, 4104 + h * 256, 5144 + h * 256
        w = wbuf[slot]

        def ld(dst0, c, n, bi_):
            dma(qg, w[:, :, dst0:dst0 + n], w_in[:, c:c + n].rearrange("(k p) c -> p k c", p=128), writes=[Bw[slot][bi_]])
        ld(128, ck, 128, 1)
        ld(256, cv, 256, 2)
        if phase == 2:
            ld(0, cq, 128, 0)
            ld(512, co, 256, 3)

    psQ, psK, psV, psA, psO = ps[0], ps[1], ps[2], ps[3], ps[6]
    BpsQ, BpsK, BpsV, BpsA, BpsO = psb[0], psb[1], psb[2], psb[3], psb[6]
    psP = [ps[4], ps[5]]
    BpsP = [psb[4], psb[5]]
    TILES = [(bi, tt) for bi, (t0, ntl) in enumerate(BLOCKS) for tt in range(ntl)]

    def mixer_unit(u, phase, slot):
        m = u < 4
        h = u % 4
        NV = 257 if m else 256
        w = wbuf[slot]
        full = phase == 2
        st = {"xi": 0, "sidx": {}}
        op(dve, lambda: nc.vector.memset(Xs[0][:, 0:NV], 0.0), writes=[BX[0]])
        if full:
            dma(qs, gsb[:, :, :], gath[:, u * 258:(u + 1) * 258].rearrange("(r p) c -> p r c", p=128), writes=[Bgs])
            op(dve, lambda: nc.vector.tensor_tensor(out=decp[:, 0:8], in0=gsb[:, :, 257], in1=sel[:, :], op=ALU.mult),
               reads=[Bgs, B_c], writes=[Btiny])
            op(dve, lambda: nc.vector.tensor_tensor(out=decp[:, 0:8], in0=decp[:, 0:8], in1=onem[:, :], op=ALU.add),
               reads=[Btiny, B_c], writes=[Btiny])
            op(dve, lambda: nc.vector.tensor_tensor(out=gsb[:, 0:7, 0:NV], in0=gsb[:, 0:7, 0:NV],
                                                    in1=sel[:, 0:7].unsqueeze(2).to_broadcast([128, 7, NV]), op=ALU.mult),
               reads=[B_c, Bgs], writes=[Bgs])
            for j in range(7):
                op(dve, lambda j=j: nc.vector.scalar_tensor_tensor(out=Xs[0][:, 0:NV], in0=Xs[0][:, 0:NV],
                                                                   scalar=decp[:, j:j + 1], in1=gsb[:, j, 0:NV],
                                                                   op0=ALU.mult, op1=ALU.add),
                   reads=[Bgs, Btiny], writes=[BX[0]])
        if m:
            op(pool, lambda: nc.gpsimd.memset(kpre[:, 0:3], 0.0), writes=[Bkpre])
            if full:
                op(pool, lambda: nc.gpsimd.memset(qpre[:, 0:3], 0.0), writes=[Bqpre])
        else:
            op(pool, lambda: nc.gpsimd.memset(avec[:, 0:1], 1.0), writes=[Bav[0]])

        def S0_stages(bi):
            t0, ntl = BLOCKS[bi]
            par = bi % 2
            n = ntl * 128
            c0 = t0 * 128
            rx = BxnT[t0:t0 + ntl]
            kT_, qT_ = kTb[par], qTb[par]

            def proj_k():
                for kk in range(KT):
                    op(pe, lambda kk=kk: nc.tensor.matmul(psK[:, 0:n], lhsT=w[:, kk, 128:256], rhs=xnT[:, kk, c0:c0 + n],
                                                          start=(kk == 0), stop=(kk == KT - 1)), reads=[Bw[slot][1]] + rx, writes=[BpsK])

            def proj_q():
                for kk in range(KT):
                    op(pe, lambda kk=kk: nc.tensor.matmul(psQ[:, 0:n], lhsT=w[:, kk, 0:128], rhs=xnT[:, kk, c0:c0 + n],
                                                          start=(kk == 0), stop=(kk == KT - 1)), reads=[Bw[slot][0]] + rx, writes=[BpsQ])

            if m:
                chains = [(psK, BpsK, kpre, Bkpre, kc, Bkc, kT_, BkT[par], 4 + h)]
                if full:
                    chains.append((psQ, BpsQ, qpre, Bqpre, qc, Bqc, qT_, BqT[par], h))

                def f1():
                    proj_k()
                    if full:
                        proj_q()
                    for (psx, Bpsx, pre, Bpre, cx, Bcx, outb, Bout, ktile) in chains:
                        op(act, lambda psx=psx, pre=pre: nc.scalar.activation(out=pre[:, 3:3 + n], in_=psx[:, 0:n], func=AF.Copy),
                           reads=[Bpsx], writes=[Bpre])

                def f2(chain):
                    for (psx, Bpsx, pre, Bpre, cx, Bcx, outb, Bout, ktile) in [chain]:
                        wv = cw[:, ktile * 4:(ktile + 1) * 4]
                        op(dve, lambda cx=cx, pre=pre, wv=wv, ktile=ktile: nc.vector.tensor_scalar(
                            cx[:, 0:n], pre[:, 0:n], wv[:, 0:1], cb[:, ktile:ktile + 1], ALU.mult, ALU.add),
                           reads=[Bpre, B_c], writes=[Bcx])
                        for i in range(1, 4):
                            op(dve, lambda i=i, cx=cx, pre=pre, wv=wv: nc.vector.scalar_tensor_tensor(
                                out=cx[:, 0:n], in0=pre[:, i:i + n], scalar=wv[:, i:i + 1], in1=cx[:, 0:n], op0=ALU.mult, op1=ALU.add),
                               reads=[Bpre, B_c, Bcx], writes=[Bcx])
                        op(pool, lambda pre=pre: nc.gpsimd.tensor_copy(out=pre[:, 0:3], in_=pre[:, n:n + 3]), reads=[Bpre], writes=[Bpre])

                def f3(chain):
                    for (psx, Bpsx, pre, Bpre, cx, Bcx, outb, Bout, ktile) in [chain]:
                        sigmoid_into(sgt[:, 0:n], cx[:, 0:n], ta[:, 0:n], tb[:, 0:n], [Bcx], [Bta])
                        op(pool, lambda outb=outb, cx=cx: nc.gpsimd.tensor_tensor(out=outb[:, 0:n], in0=cx[:, 0:n], in1=sgt[:, 0:n], op=ALU.mult),
                           reads=[Bcx, Bta], writes=[Bout])
            else:
                nch = 2 * ntl
                nb3_ = nbt[:, 0:n].rearrange("p (c l) -> p c l", l=64)

                def f1():
                    op(pe, lambda: nc.tensor.matmul(psQ[:, 0:n], lhsT=wa2[:, h * 128:(h + 1) * 128], rhs=gaT[:, c0:c0 + n],
                                                    start=True, stop=True), reads=[B_c, BgaT], writes=[BpsQ])
                    op(act, lambda: nc.scalar.activation(out=ta[:, 0:n], in_=psQ[:, 0:n], func=AF.Exp, bias=na2b[:, h:h + 1], scale=-1.0),
                       reads=[BpsQ, B_small], writes=[Bta])
                    proj_k()
                    if full:
                        proj_q()
                    op(act, lambda: nc.scalar.activation(out=tb[:, 0:n], in_=ta[:, 0:n], func=AF.Ln, bias=c_one, scale=1.0),
                       reads=[Bta, B_cst], writes=[Bta])

                def f2():
                    if bi == 0:
                        op(dve, lambda: nc.vector.tensor_tensor(out=tb[:, 0:128], in0=tb[:, 0:128], in1=maskrow[:, :], op=ALU.mult),
                           reads=[Bta, B_c], writes=[Bta])
                    op(dve, lambda: nc.vector.tensor_tensor_scan(out=nbt[:, 0:n], data0=rst[:, 0:n], data1=tb[:, 0:n], initial=0.0,
                                                                 op0=ALU.mult, op1=ALU.add), reads=[Bta, B_c], writes=[Bnb])

                def f2b():
                    op(act, lambda: nc.scalar.activation(out=avec[:, 2 * t0 + 1:2 * t0 + 1 + nch], in_=nb3_[:, :, 63], func=AF.Exp,
                                                         scale=-1.0 / 16.0), reads=[Bnb], writes=[Bav[bi + 1]])
                    if not full:
                        op(dve, lambda: nc.vector.tensor_copy(out=nbend[:, 2 * t0:2 * t0 + nch], in_=nb3_[:, :, 63]), reads=[Bnb],
                           writes=[Bav[bi + 1]])
                    op(act, lambda: nc.scalar.activation(out=ekt[:, 0:n], in_=nbt[:, 0:n], func=AF.Exp, scale=1.0 / 16.0),
                       reads=[Bnb], writes=[Bek])

                def f3():
                    if bi == 0:
                        op(dve, lambda: nc.vector.tensor_tensor(out=ekt[:, 0:128], in0=ekt[:, 0:128], in1=maskrow[:, :], op=ALU.mult),
                           reads=[Bek, B_c], writes=[Bek])
                    op(dve, lambda: nc.vector.tensor_tensor(out=kT_[:, 0:n], in0=psK[:, 0:n], in1=ekt[:, 0:n], op=ALU.mult),
                       reads=[BpsK, Bek], writes=[BkT[par]])

                def f3b():
                    if full:
                        op(act, lambda: nc.scalar.activation(out=ekt[:, 0:n], in_=nbt[:, 0:n], func=AF.Exp, bias=c_lnc, scale=-1.0 / 16.0),
                           reads=[Bnb, B_cst], writes=[Bek])
                        op(dve, lambda: nc.vector.tensor_tensor(out=qT_[:, 0:n], in0=psQ[:, 0:n], in1=ekt[:, 0:n], op=ALU.mult),
                           reads=[BpsQ, Bek], writes=[BqT[par]])
            if m:
                stages = [f1] + [(lambda c=c: f2(c)) for c in chains] + [(lambda c=c: f3(c)) for c in chains]
            else:
                stages = [f1, f2, f2b, f3, f3b]
            return stages + [lambda: S0b(bi)]

        def S0b(bi):
            t0, ntl = BLOCKS[bi]
            par = bi % 2
            n = ntl * 128
            kT_ = kTb[par]
            for tt in range(ntl):
                op(pe, lambda tt=tt: nc.tensor.transpose(psT[:, tt * 128:(tt + 1) * 128], kT_[:, tt * 128:(tt + 1) * 128], identb[:]),
                   reads=[BkT[par], B_c], writes=[BpsT])
            op(act, lambda: nc.scalar.activation(out=ktok[:, par * 4:par * 4 + ntl, :],
                                                 in_=psT[:, 0:n].rearrange("p (t c) -> p t c", t=ntl),
                                                 func=AF.Copy), reads=[BpsT], writes=Bktok[par * 4:par * 4 + ntl])

        def S1(bi, tt):
            t0, ntl = BLOCKS[bi]
            t = t0 + tt
            sl = (bi % 2) * 4 + tt
            outp = full and t >= 1
            ncol = 512 if outp else 256
            for kk in range(KT):
                op(pe, lambda kk=kk: nc.tensor.matmul(psV[:, 0:ncol], lhsT=xnT[:, kk, t * 128:(t + 1) * 128],
                                                      rhs=w[:, kk, 256:256 + ncol], start=(kk == 0), stop=(kk == KT - 1)),
                   reads=[Bw[slot][2], Bw[slot][3], BxnT[t]], writes=[BpsV])
            if m:
                op(act, lambda: nc.scalar.activation(out=vtok[:, sl, 0:256], in_=psV[:, 0:256], func=AF.Copy,
                                                     scale=etok[:, t, h:h + 1]), reads=[BpsV, Betok], writes=[Bvtok[sl]])
                op(pool, lambda: nc.gpsimd.tensor_copy(out=vtok[:, sl, 256:257], in_=etok[:, t, h:h + 1]),
                   reads=[Betok, Bvtok[sl]], writes=[Bvtok[sl]])
            else:
                op(act, lambda: nc.scalar.activation(out=vtok[:, sl, 0:256], in_=psV[:, 0:256], func=AF.Copy),
                   reads=[BpsV], writes=[Bvtok[sl]])
            if outp:
                hgv = hg[:, u * 256:(u + 1) * 256]
                if m:
                    sigmoid_into(gc_, psV[:, 256:512], ga_, gb_, [BpsV], [Bgt])
                    op(pool, lambda: nc.gpsimd.tensor_tensor(out=Gt[:, sl, :], in0=gc_, in1=hgv, op=ALU.mult),
                       reads=[Bgt, B_c], writes=[BG[sl]])
                else:
                    op(act, lambda: nc.scalar.activation(out=go_, in_=psV[:, 256:512], func=AF.Copy), reads=[BpsV], writes=[Bgt])
                    sigmoid_into(gc_, go_, ga_, gb_, [Bgt], [Bgt])
                    op(pool, lambda: nc.gpsimd.tensor_tensor(out=gc_, in0=gc_, in1=hgv, op=ALU.mult),
                       reads=[Bgt, B_c], writes=[Bgt])
                    op(pool, lambda: nc.gpsimd.tensor_tensor(out=Gt[:, sl, :], in0=go_, in1=gc_, op=ALU.mult),
                       reads=[Bgt], writes=[BG[sl]])

        def S2a(bi, tt):
            t0, ntl = BLOCKS[bi]
            t = t0 + tt
            par = bi % 2
            sl = par * 4 + tt
            outp = full and t >= 1
            sidx = []
            st["sidx"][t] = sidx
            for half in range(2):
                c = 2 * t + half
                rows = slice(half * 64, (half + 1) * 64)
                a_c = abc[:, h, c:c + 1] if m else avec[:, c:c + 1]
                Ba = Babc if m else (Bav[0] if c == 0 else Bav[(c - 1) // 8 + 1])
                xi = st["xi"]
                xin_, xout_ = Xs[xi], Xs[1 - xi]
                if outp:
                    si = c % 4
                    sidx.append(si)
                    op(act, lambda: nc.scalar.activation(out=Sbf[si][:, 0:NV], in_=xin_[:, 0:NV], func=AF.Copy, scale=a_c),
                       reads=[BX[xi], Ba], writes=[BS[si]])
                pp = c % 2
                op(pe, lambda: nc.tensor.matmul(psP[pp][:, 0:NV], lhsT=ktok[rows, sl, :], rhs=vtok[rows, sl, 0:NV],
                                                start=True, stop=True), reads=[Bktok[sl], Bvtok[sl]], writes=[BpsP[pp]])
                op(dve, lambda: nc.vector.scalar_tensor_tensor(out=xout_[:, 0:NV], in0=xin_[:, 0:NV], scalar=a_c,
                                                               in1=psP[pp][:, 0:NV], op0=ALU.mult, op1=ALU.add),
                   reads=[BX[xi], Ba, BpsP[pp]], writes=[BX[1 - xi]])
                st["xi"] = 1 - xi

        def S2b(bi, tt):
            t0, ntl = BLOCKS[bi]
            t = t0 + tt
            par = bi % 2
            sl = par * 4 + tt
            outp = full and t >= 1
            if not outp:
                return
            kT_, qT_ = kTb[par], qTb[par]
            sidx = st["sidx"][t]
            tq = slice(tt * 128, (tt + 1) * 128)
            op(pe, lambda: nc.tensor.matmul(psA[:, 0:128], lhsT=kT_[:, tq], rhs=qT_[:, tq], start=True, stop=True),
               reads=[BkT[par], BqT[par]], writes=[BpsA])
            op(dve, lambda: nc.vector.tensor_tensor(out=attb, in0=psA[:, 0:128], in1=cmask[:, :], op=ALU.mult),
               reads=[BpsA, B_c], writes=[Batt])
            op(pe, lambda: nc.tensor.matmul(psO[:, 0:NV], lhsT=attb, rhs=vtok[:, sl, 0:NV], start=True, stop=False),
               reads=[Batt, Bvtok[sl]], writes=[BpsO])
            op(pe, lambda: nc.tensor.matmul(psO[0:64, 0:NV], lhsT=qT_[:, tt * 128:tt * 128 + 64], rhs=Sbf[sidx[0]][:, 0:NV],
                                            start=False, stop=False), reads=[BqT[par], BS[sidx[0]]], writes=[BpsO])
            op(pe, lambda: nc.tensor.matmul(psO[64:128, 0:NV], lhsT=qT_[:, tt * 128 + 64:tt * 128 + 128], rhs=Sbf[sidx[1]][:, 0:NV],
                                            start=False, stop=True), reads=[BqT[par], BS[sidx[1]]], writes=[BpsO])

        def S2c(bi, tt):
            t0, ntl = BLOCKS[bi]
            t = t0 + tt
            sl = (bi % 2) * 4 + tt
            if not (full and t >= 1):
                return
            ci = (u * 16 + (t - 1)) % 16
            ssc = ssA[:, 32 + ci:33 + ci]
            if m:
                d1 = tiny[:, 0:1]
                rden = tiny[:, 1:2]
                op(dve, lambda: nc.vector.scalar_tensor_tensor(out=tiny[:, 3:4], in0=psO[:, 256:257], scalar=-1.0,
                                                               in1=etok[:, t, 4 + h:5 + h], op0=ALU.mult, op1=ALU.max),
                   reads=[BpsO, Betok], writes=[Btiny])
                op(dve, lambda: nc.vector.tensor_tensor(out=d1, in0=psO[:, 256:257], in1=tiny[:, 3:4], op=ALU.max),
                   reads=[BpsO, Btiny], writes=[Btiny])
                op(dve, lambda: nc.vector.reciprocal(out=rden, in_=d1), reads=[Btiny], writes=[Btiny])
                op(act, lambda: nc.scalar.activation(out=junk[:, 0:256], in_=psO[:, 0:256], func=AF.Square, scale=rden,
                                                     accum_out=ssc), reads=[BpsO, Btiny], writes=[Bss])
            else:
                op(act, lambda: nc.scalar.activation(out=junk[:, 0:256], in_=psO[:, 0:256], func=AF.Square, accum_out=ssc),
                   reads=[BpsO], writes=[Bss])
            r = rstd(ssc, 256)
            if m:
                sc = tiny[:, 2:3]
                op(dve, lambda: nc.vector.tensor_tensor(out=sc, in0=rden, in1=r, op=ALU.mult), reads=[Btiny, Bss], writes=[Btiny])
                rb = [Btiny]
            else:
                sc = r
                rb = [Bss]
            yi = t % 2
            op(dve, lambda: nc.vector.scalar_tensor_tensor(out=ybuf[yi], in0=psO[:, 0:256], scalar=sc, in1=Gt[:, sl, :],
                                                           op0=ALU.mult, op1=ALU.mult), reads=[BpsO, BG[sl]] + rb, writes=[By[yi]])

        def S3(bi, tt):
            t0, ntl = BLOCKS[bi]
            t = t0 + tt
            if not (full and t >= 1):
                return
            yi = t % 2
            for j in range(2):
                op(pe, lambda j=j: nc.tensor.transpose(psT[:, 512 + j * 128:512 + (j + 1) * 128], ybuf[yi][:, j * 128:(j + 1) * 128],
                                                       identb[:]), reads=[By[yi], B_c], writes=[BpsT])
            op(dve, lambda: nc.vector.tensor_copy(out=yT[:, 2 * u:2 * u + 2, (t - 1) * 128:t * 128],
                                                  in_=psT[:, 512:768].rearrange("p (j c) -> p j c", j=2)),
               reads=[BpsT], writes=[ByT[t - 1]])

        for f in S0_stages(0):
            f()
        S1(*TILES[0])
        sched = {}
        for i, (bi, tt) in enumerate(TILES):
            ntl_b = BLOCKS[bi][1]
            if tt == 0 and bi + 1 < len(BLOCKS):
                stages = S0_stages(bi + 1)
                L = len(stages) - 1
                nslot = 2 * ntl_b - 2
                for j, f in enumerate(stages[:-1]):
                    slot_ = (j * nslot) // L if nslot > 0 else 0
                    sched.setdefault((i + slot_ // 2, slot_ % 2), []).append(f)
                sched.setdefault((i + ntl_b - 1, 0), []).append(stages[-1])
            for f in sched.pop((i, 0), []):
                f()
            if full:
                if i == 0:
                    S2a(bi, tt)
                if i + 1 < len(TILES):
                    S1(*TILES[i + 1])
                S2b(bi, tt)
                if i + 1 < len(TILES):
                    S2a(*TILES[i + 1])
                S2c(bi, tt)
            else:
                if i + 1 < len(TILES):
                    S1(*TILES[i + 1])
                S2a(bi, tt)
            if i >= 1:
                S3(*TILES[i - 1])
            for f in sched.pop((i, 1), []):
                f()
        assert not sched, sched.keys()
        S3(*TILES[-1])
        xi = st["xi"]
        if not full:
            xf_ = Xs[xi]
            if m:
                fin = abc[:, h, 34:35]
                dec = abc[:, h, 35:36]
                Bf = [Babc]
            else:
                fin = avec[:, NCH:NCH + 1]
                op(dve, lambda: nc.vector.tensor_reduce(out=tiny[:, 4:5], in_=nbend[:, 0:NCH], axis=AX.X, op=ALU.add),
                   reads=Bav, writes=[Btiny])
                op(act, lambda: nc.scalar.activation(out=tiny[:, 5:6], in_=tiny[:, 4:5], func=AF.Exp, scale=-1.0 / 16.0),
                   reads=[Btiny], writes=[Btiny])
                dec = tiny[:, 5:6]
                Bf = Bav
            op(dve, lambda: nc.vector.memset(pubb[:, 0:258], 0.0), writes=[Bpub])
            op(dve, lambda: nc.vector.tensor_scalar(pubb[:, 0:NV], xf_[:, 0:NV], fin, None, ALU.mult),
               reads=[BX[xi]] + Bf, writes=[Bpub])
            op(dve, lambda: nc.vector.tensor_copy(out=pubb[:, 257:258], in_=dec), reads=Bf + [Btiny], writes=[Bpub])
            dma(qs, pub[:, u * 258:(u + 1) * 258], pubb[:, 0:258], reads=[Bpub])

    for phase in {"fused": (1, 2), "A": (1,), "B": (2,)}[mode]:
        load_unit_weights(0, phase, 0)
        for u in range(8):
            if u + 1 < 8:
                load_unit_weights(u + 1, phase, (u + 1) % 2)
            mixer_unit(u, phase, u % 2)
        if phase == 1 and mode == "A":
            for q in (qs, qg):
                for t_ in q.last:
                    k.sp.wait_tok(t_)
            k.barrier()
            es.close()
            return nc
        if phase == 1:
            for t_ in qs.last:
                pool.wait_tok(t_)
            import os
            if os.environ.get("NOCC"):
                nc.gpsimd.dma_start(out=gath[0:128, :], in_=pub[:, :]).then_inc(ccsem, 16)
                k.sp.e.wait_ge(ccsem, 16)
                nc.gpsimd.wait_ge(ccsem, 16)
            else:
                nc.gpsimd.collective_compute("AllGather", ALU.bypass, replica_groups=[list(range(NCORES))],
                                             ins=[pub_t.ap().opt()], outs=[gath_t.ap().opt()]).then_inc(ccsem)
                k.sp.e.wait_ge(ccsem, 1)
                nc.gpsimd.wait_ge(ccsem, 1)
            for e in (pe, act, dve, pool):
                e.new_epoch()
    if "yT" in dbg_d:
        yf = R3[:, 0:2048]
        Byf = Buf()
        k.barrier()
        op(dve, lambda: nc.vector.tensor_copy(out=yf, in_=yT[:, 0, :]), reads=ByT, writes=[Byf])
        dbg("yT", yf, [Byf])
    k.barrier()

    ar = Arena()
    mergedT = ar.bf16(KT * TL).rearrange("p (k t) -> p k t", k=KT)
    x1_mark = ar.off
    wx = [ar.bf16(KT * 512).rearrange("p (k j c) -> p k j c", k=KT, j=4) for _ in range(2)]
    Bwx = [[Buf() for _ in range(4)] for _ in range(2)]
    sa = ar.f32(512)
    sb_ = ar.f32(512)
    sgm = ar.f32(512)
    sgg = ar.f32(512)
    m1 = ar.f32(512)
    m2 = ar.f32(512)
    Bsa, Bsgm, Bsgg, Bm1, Bm2 = Buf(), Buf(), Buf(), Buf(), Buf()
    BmT = [Buf() for _ in range(4)]

    def load_wx(o, slot):
        cs = slice(o * 128, (o + 1) * 128)
        dma(qg, wx[slot][:, :, 0, :], wbm[:, cs].rearrange("(k p) c -> p k c", p=128), writes=[Bwx[slot][0]])
        dma(qg, wx[slot][:, :, 1, :], wbg[:, cs].rearrange("(k p) c -> p k c", p=128), writes=[Bwx[slot][1]])
        dma(qg, wx[slot][:, :, 2, :], w_in[:, 6168 + o * 128:6168 + (o + 1) * 128].rearrange("(k p) c -> p k c", p=128), writes=[Bwx[slot][2]])
        dma(qg, wx[slot][:, :, 3, :], w_in[:, 7192 + o * 128:7192 + (o + 1) * 128].rearrange("(k p) c -> p k c", p=128), writes=[Bwx[slot][3]])

    load_wx(0, 0)
    it = 0
    for o in range(8):
        if o + 1 < 8:
            load_wx(o + 1, (o + 1) % 2)
        s = o % 2
        for b in range(4):
            pb = (it % 2) * 4
            it += 1
            y0 = b * 512
            c0 = 128 + b * 512
            for j in range(4):
                for kk in range(KT):
                    if j == 0:
                        rhs = yT[:, kk, y0:y0 + 512]
                    elif j == 1:
                        rhs = yT[:, 8 + kk, y0:y0 + 512]
                    else:
                        rhs = xnT[:, kk, c0:c0 + 512]
                    op(pe, lambda kk=kk, j=j, rhs=rhs: nc.tensor.matmul(ps[pb + j][:, :], lhsT=wx[s][:, kk, j, :], rhs=rhs,
                                                                        start=(kk == 0), stop=(kk == KT - 1)),
                       reads=[Bwx[s][j]], writes=[psb[pb + j]])
            sigmoid_into(sgm, ps[pb + 2][:, :], sa, sb_, [psb[pb + 2]], [Bsgm])
            sigmoid_into(sgg, ps[pb + 3][:, :], sa, sb_, [psb[pb + 3]], [Bsgg])
            op(dve, lambda: nc.vector.tensor_tensor(out=m1, in0=ps[pb][:, :], in1=sgm, op=ALU.mult), reads=[psb[pb], Bsgm], writes=[Bm1])
            op(dve, lambda: nc.vector.tensor_tensor(out=m2, in0=ps[pb + 1][:, :], in1=sgg, op=ALU.mult), reads=[psb[pb + 1], Bsgg], writes=[Bm2])
            op(pool, lambda: nc.gpsimd.tensor_tensor(out=mergedT[:, o, y0:y0 + 512], in0=m1, in1=m2, op=ALU.add),
               reads=[Bm1, Bm2], writes=[BmT[b]])
    k.barrier()

    ar = Arena()
    ar.off = x1_mark
    wo = ar.bf16(KT * 1024).rearrange("p (k c) -> p k c", k=KT)
    Bwo = [Buf(), Buf()]
    dma(qg, wo[:, :, 0:512], wout[:, 0:512].rearrange("(k p) c -> p k c", p=128), writes=[Bwo[0]])
    dma(qg, wo[:, :, 512:1024], wout[:, 512:1024].rearrange("(k p) c -> p k c", p=128), writes=[Bwo[1]])
    xt = [ar.f32(1024) for _ in range(2)]
    hnb = [ar.bf16(1024) for _ in range(2)]
    Bxt = [Buf(), Buf()]
    Bhnb = [Buf(), Buf()]
    Bh2 = [Buf(f"h2_{t}") for t in range(16)]
    BhnT = [Buf(f"hnT{t}") for t in range(16)]
    def X2a(t):
        i = t % 2
        pb_ = (t % 2) * 2
        dma(qs, xt[i], xin[(t + 1) * 128:(t + 2) * 128, :], writes=[Bxt[i]])
        for half in range(2):
            for kk in range(KT):
                op(pe, lambda kk=kk, half=half: nc.tensor.matmul(ps[pb_ + half][:, :], lhsT=mergedT[:, kk, t * 128:(t + 1) * 128],
                                                                 rhs=wo[:, kk, half * 512:(half + 1) * 512],
                                                                 start=(kk == 0), stop=(kk == KT - 1)),
                   reads=[Bwo[half], BmT[t // 4]], writes=[psb[pb_ + half]])
            op(dve, lambda half=half: nc.vector.tensor_tensor(out=h2[:, t, half * 512:(half + 1) * 512], in0=ps[pb_ + half][:, :],
                                                              in1=xt[i][:, half * 512:(half + 1) * 512], op=ALU.add),
               reads=[psb[pb_ + half], Bxt[i]], writes=[Bh2[t]])

    def X2b(t):
        i = t % 2
        op(act, lambda: nc.scalar.activation(out=junk[:], in_=h2[:, t, :], func=AF.Square, accum_out=ssA[:, t:t + 1]),
           reads=[Bh2[t]], writes=[Bss])
        r = rstd(ssA[:, t:t + 1], D)
        op(act, lambda: nc.scalar.activation(out=hnb[i], in_=h2[:, t, :], func=AF.Copy, scale=r), reads=[Bh2[t], Bss], writes=[Bhnb[i]])

    def X2c(t):
        i = t % 2
        for kk in range(KT):
            op(pe, lambda kk=kk: nc.tensor.transpose(psT[:, kk * 128:(kk + 1) * 128], hnb[i][:, kk * 128:(kk + 1) * 128], identb[:]),
               reads=[Bhnb[i], B_c], writes=[BpsT])
        op(dve, lambda: nc.vector.tensor_tensor(out=hnT[:, :, t * 128:(t + 1) * 128], in0=psT.rearrange("p (k t) -> p k t", k=KT),
                                                in1=g2[:, :].unsqueeze(2).to_broadcast([128, KT, 128]), op=ALU.mult),
           reads=[BpsT, B_c], writes=[BhnT[t]])

    X2a(0)
    for t in range(16):
        if t + 1 < 16:
            X2a(t + 1)
        X2b(t)
        if t >= 1:
            X2c(t - 1)
    X2c(15)
    k.barrier()

    ar = Arena()
    JB = [(0, 4), (4, 4), (8, 4), (12, 4), (16, 4), (20, 2)]
    wg_ = [ar.bf16(KT * 512).rearrange("p (k c) -> p k c", k=KT) for _ in range(2)]
    wu_ = [ar.bf16(KT * 512).rearrange("p (k c) -> p k c", k=KT) for _ in range(2)]
    wd_ = [ar.bf16(4 * 1024).rearrange("p (j c) -> p j c", j=4) for _ in range(2)]
    Bwf = [[Buf() for _ in range(3)] for _ in range(2)]
    ffT = [ar.bf16(4 * 512).rearrange("p (j c) -> p j c", j=4) for _ in range(2)]
    Bff = [Buf(), Buf()]
    fa = ar.f32(512)
    fb = ar.f32(512)
    fsg = ar.f32(512)
    fgs = ar.f32(512)
    Bfa, Bfsg, Bfgs = Buf(), Buf(), Buf()
    ot = [ar.f32(1024) for _ in range(2)]
    Bot = [Buf(), Buf()]

    def load_wf(J, slot):
        j0, nj = JB[J]
        n = nj * 128
        dma(qg, wg_[slot][:, :, 0:n], wfg[:, j0 * 128:j0 * 128 + n].rearrange("(k p) c -> p k c", p=128), writes=[Bwf[slot][0]])
        dma(qg, wu_[slot][:, :, 0:n], wfu[:, j0 * 128:j0 * 128 + n].rearrange("(k p) c -> p k c", p=128), writes=[Bwf[slot][1]])
        dma(qg, wd_[slot][:, 0:nj, :], wfd[j0 * 128:j0 * 128 + n, :].rearrange("(j p) c -> p j c", p=128), writes=[Bwf[slot][2]])

    load_wf(0, 0)
    itg = 0
    ito = 0
    for J in range(len(JB)):
        if J + 1 < len(JB):
            load_wf(J + 1, (J + 1) % 2)
        s = J % 2
        j0, nj = JB[J]
        for b in range(4):
            fi = (J * 4 + b) % 2
            for j in range(nj):
                pg = (itg % 2) * 2
                itg += 1
                for which, wsrc in ((0, wg_[s]), (1, wu_[s])):
                    for kk in range(KT):
                        op(pe, lambda kk=kk, which=which, wsrc=wsrc: nc.tensor.matmul(
                            ps[pg + which][:, :], lhsT=wsrc[:, kk, j * 128:(j + 1) * 128], rhs=hnT[:, kk, b * 512:(b + 1) * 512],
                            start=(kk == 0), stop=(kk == KT - 1)), reads=[Bwf[s][which]] + BhnT[b * 4:(b + 1) * 4], writes=[psb[pg + which]])
                sigmoid_into(fsg, ps[pg][:, :], fa, fb, [psb[pg]], [Bfsg])
                op(dve, lambda: nc.vector.tensor_tensor(out=fgs, in0=ps[pg][:, :], in1=fsg, op=ALU.mult), reads=[psb[pg], Bfsg], writes=[Bfgs])
                op(dve, lambda j=j: nc.vector.tensor_tensor(out=ffT[fi][:, j, :], in0=ps[pg + 1][:, :], in1=fgs, op=ALU.mult),
                   reads=[psb[pg + 1], Bfgs], writes=[Bff[fi]])
            for tt in range(4):
                t = b * 4 + tt
                po = 4 + (ito % 2) * 2
                ito += 1
                for half in range(2):
                    for j in range(nj):
                        op(pe, lambda j=j, half=half: nc.tensor.matmul(
                            ps[po + half][:, :], lhsT=ffT[fi][:, j, tt * 128:(tt + 1) * 128], rhs=wd_[s][:, j, half * 512:(half + 1) * 512],
                            start=(j == 0), stop=(j == nj - 1)), reads=[Bff[fi], Bwf[s][2]], writes=[psb[po + half]])
                    op(dve, lambda half=half: nc.vector.tensor_tensor(out=h2[:, t, half * 512:(half + 1) * 512], in0=ps[po + half][:, :],
                                                                      in1=h2[:, t, half * 512:(half + 1) * 512], op=ALU.add),
                       reads=[psb[po + half], Bh2[t]], writes=[Bh2[t]])
    for t in range(16):
        i = t % 2
        op(act, lambda: nc.scalar.activation(out=junk[:], in_=h2[:, t, :], func=AF.Square, accum_out=ssA[:, 16 + t:17 + t]),
           reads=[Bh2[t]], writes=[Bss])
        r = rstd(ssA[:, 16 + t:17 + t], D)
        op(dve, lambda: nc.vector.scalar_tensor_tensor(out=ot[i], in0=h2[:, t, :], scalar=r, in1=fg[:, :], op0=ALU.mult, op1=ALU.mult),
           reads=[Bh2[t], Bss, B_c], writes=[Bot[i]])
        dma(qs, out_d[t * 128:(t + 1) * 128, :], ot[i], reads=[Bot[i]])
    for q in (qs, qg):
        for t_ in q.last:
            k.sp.wait_tok(t_)
    k.barrier()
    es.close()
    return nc


_CACHE = {}


def _host_inputs(x, meta_tokens, norm1_g, w_in, conv_w, conv_b, m_gate_b, g_a2, g_a2_b, m_head_g, g_head_g,
                 w_branch_m, w_branch_g, w_out, norm2_g, w_ff_gate, w_ff_up, w_ff_down, final_g):
    f = np.float32
    x2 = np.asarray(x, f)[0]
    pre0 = np.zeros((128, D), f)
    pre0[112:128] = np.asarray(meta_tokens, f)

    def pk(v):
        return np.ascontiguousarray(np.asarray(v, f).reshape(KT, 128).T)
    ident = np.eye(128, dtype=f)
    idx = np.arange(128)
    cmask = ((idx[:, None] // 64 == idx[None, :] // 64) & (idx[:, None] <= idx[None, :])).astype(f)
    rst = np.ones((128, 512), f)
    rst[:, ::64] = 0.0
    selh = np.zeros((4, 4, 128), f)
    for h in range(4):
        selh[h, h, :] = 1.0
    selh = selh.reshape(4, 512)
    cw = np.asarray(conv_w, f)[0]
    cw_p = np.ascontiguousarray(cw.reshape(4, KT, 128).transpose(2, 1, 0).reshape(128, 32))
    mgb = np.ascontiguousarray(np.asarray(m_gate_b, f)[0].T)
    a2b = np.ascontiguousarray(np.asarray(g_a2_b, f)[0].reshape(4, 128).T)
    hgrow = np.concatenate([np.asarray(m_head_g, f)[0].reshape(-1), np.asarray(g_head_g, f)[0].reshape(-1)])
    hg = np.ascontiguousarray(np.broadcast_to(hgrow[None, :], (128, 2048)))
    fgb = np.ascontiguousarray(np.broadcast_to(np.asarray(final_g, f)[None, :], (128, 1024)))
    shared = dict(
        ident=ident, cmask=cmask, rst=rst, selh=selh, g1=pk(np.asarray(norm1_g)[0]), g2=pk(np.asarray(norm2_g)[0]),
        cw=cw_p, cb=pk(np.asarray(conv_b)[0]), mgb=mgb, a2b=a2b, hg=hg, fg=fgb,
        w_in=np.ascontiguousarray(np.asarray(w_in, f)[0]), g_a2=np.ascontiguousarray(np.asarray(g_a2, f)[0]),
        wbm=np.ascontiguousarray(np.asarray(w_branch_m, f)[0]), wbg=np.ascontiguousarray(np.asarray(w_branch_g, f)[0]),
        wout=np.ascontiguousarray(np.asarray(w_out, f)[0]), wfg=np.ascontiguousarray(np.asarray(w_ff_gate, f)[0]),
        wfu=np.ascontiguousarray(np.asarray(w_ff_up, f)[0]), wfd=np.ascontiguousarray(np.asarray(w_ff_down, f)[0]),
    )
    maps = []
    for c in range(NCORES):
        if c == 0:
            xin = np.concatenate([pre0, x2[0:TL]], axis=0)
            valid = (np.arange(128) >= 112).astype(f)
        else:
            xin = x2[c * TL - 128:(c + 1) * TL]
            valid = np.zeros(128, f)
        maskrow = np.ascontiguousarray(np.broadcast_to(valid[None, :], (128, 128)))
        negrow = np.ascontiguousarray(np.broadcast_to(((valid - 1.0) * 1e30)[None, :], (128, 128))).astype(f)
        selv = (np.arange(8) < c).astype(f)
        sel = np.ascontiguousarray(np.broadcast_to(selv[None, :], (128, 8)))
        onem = np.ascontiguousarray(1.0 - sel).astype(f)
        m = dict(shared)
        m.update(xin=np.ascontiguousarray(xin, dtype=f), maskrow=maskrow, negrow=negrow, sel=sel, onem=onem)
        maps.append(m)
    return maps


def kernel(**inputs):
    maps = _host_inputs(**inputs)
    if MODE == "fused":
        if "nc" not in _CACHE:
            _CACHE["nc"] = build_program("fused")
        res = run_bass_kernel_spmd(_CACHE["nc"], maps, core_ids=list(range(NCORES)))
    else:
        if "ncA" not in _CACHE:
            _CACHE["ncA"] = build_program("A")
            _CACHE["ncB"] = build_program("B")
        drop = ("fg", "wbm", "wbg", "wout", "wfg", "wfu", "wfd", "g2", "sel", "onem")
        mapsA = [{k_: v for k_, v in m.items() if k_ not in drop} for m in maps]
        resA = run_bass_kernel_spmd(_CACHE["ncA"], mapsA, core_ids=list(range(NCORES)))
        gath = np.ascontiguousarray(np.concatenate([resA.results[c]["pub"] for c in range(NCORES)], axis=0))
        for m in maps:
            m["gath"] = gath
        res = run_bass_kernel_spmd(_CACHE["ncB"], maps, core_ids=list(range(NCORES)))
    _CACHE["last"] = res
    out = np.concatenate([res.results[c]["out"] for c in range(NCORES)], axis=0)
    return out[None].astype(np.float32)
```

```python
import math
from contextlib import ExitStack

import numpy as np
import concourse.bass as bass
import concourse.mybir as mybir
from concourse.bass_utils import run_bass_kernel_spmd

F32 = mybir.dt.float32
BF16 = mybir.dt.bfloat16
AF = mybir.ActivationFunctionType
ALU = mybir.AluOpType
AX = mybir.AxisListType

NCORES = 8
D = 1024
KT = 8
TL = 2048
NT = 17
TT = NT * 128
NCH = 2 * NT
DFF = 2816
NPROJ = 8216
EPS = 1e-6
BLOCKS = [(0, 4), (4, 4), (8, 4), (12, 4), (16, 1)]
LN_C = -0.5 * math.log(128.0)

DEBUG = {}
MODE = "split"


class Tok:
    __slots__ = ("sem", "val", "eng", "key")

    def __init__(self, sem, val, eng, key):
        self.sem, self.val, self.eng, self.key = sem, val, eng, key


class Buf:
    __slots__ = ("name", "w", "r", "excl")

    def __init__(self, name="", excl=False):
        self.name = name
        self.w = None
        self.r = {}
        self.excl = excl


class Eng:
    def __init__(self, e, name, sems, is_pe=False):
        self.e = e
        self.name = name
        self.sems = sems
        self.ep = 0
        self.cnt = 0
        self.is_pe = is_pe
        self.waited = {}
        self.last = None

    def wait_tok(self, tok):
        if tok is None:
            return
        if tok.eng is self and self.is_pe:
            return
        if self.waited.get(tok.key, 0) >= tok.val:
            return
        self.e.wait_ge(tok.sem, tok.val)
        self.waited[tok.key] = tok.val

    def bump(self, ins):
        sem = self.sems[self.ep]
        ins.then_inc(sem, 1)
        self.cnt += 1
        t = Tok(sem, self.cnt, self, (self.name, self.ep))
        self.last = t
        return t

    def new_epoch(self):
        self.ep += 1
        self.cnt = 0


class DmaQ:
    def __init__(self, eng, sems, name):
        self.eng = eng
        self.sems = sems
        self.cnt = [0] * len(sems)
        self.last = [None] * len(sems)
        self.i = 0
        self.name = name

    def issue(self, out, in_, **kw):
        i = self.i
        self.i = (self.i + 1) % len(self.sems)
        self.eng.wait_tok(self.last[i])
        ins = self.eng.e.dma_start(out=out, in_=in_, **kw)
        ins.then_inc(self.sems[i], 16)
        self.cnt[i] += 16
        t = Tok(self.sems[i], self.cnt[i], None, (self.name, i))
        self.last[i] = t
        return t


class K:
    def op(self, eng, fn, reads=(), writes=()):
        ex = [b for b in reads if b.excl]
        if ex:
            reads = [b for b in reads if not b.excl]
            writes = list(writes) + [b for b in ex if b not in writes]
        for b in reads:
            eng.wait_tok(b.w)
        for b in writes:
            eng.wait_tok(b.w)
            for t in b.r.values():
                eng.wait_tok(t)
        tok = eng.bump(fn())
        for b in reads:
            b.r[tok.key] = tok
        for b in writes:
            b.w = tok
            b.r = {}
        return tok

    def dma(self, q, out, in_, reads=(), writes=(), **kw):
        for b in reads:
            q.eng.wait_tok(b.w)
        for b in writes:
            q.eng.wait_tok(b.w)
            for t in b.r.values():
                q.eng.wait_tok(t)
        tok = q.issue(out, in_, **kw)
        for b in reads:
            b.r[tok.key] = tok
        for b in writes:
            b.w = tok
            b.r = {}
        return tok

    def barrier(self):
        toks = []
        for e in (self.pe, self.act, self.dve, self.pool):
            if e.last is not None:
                toks.append(e.last)
        for q in (self.qs, self.qg):
            for t in q.last:
                if t is not None:
                    toks.append(t)
        for e in (self.pe, self.act, self.dve, self.pool, self.sp):
            for t in toks:
                if t.eng is e:
                    continue
                e.wait_tok(t)


def build_program(mode="fused"):
    nc = bass.Bass("TRN2", target_bir_lowering=False)
    k = K()
    k.nc = nc
    es = ExitStack()
    k.es = es

    A_DROP = ("fg", "wbm", "wbg", "wout", "wfg", "wfu", "wfd", "g2", "sel", "onem")

    def din(name, shape, dt=F32):
        if mode == "A" and name in A_DROP:
            return None
        return nc.dram_tensor(name, list(shape), dt, kind="ExternalInput").ap()

    xin = din("xin", [TT, D])
    maskrow_d = din("maskrow", [128, 128])
    negrow_d = din("negrow", [128, 128])
    sel_d = din("sel", [128, 8])
    onem_d = din("onem", [128, 8])
    ident_d = din("ident", [128, 128])
    cmask_d = din("cmask", [128, 128])
    rst_d = din("rst", [128, 512])
    selh_d = din("selh", [4, 512])
    g1_d = din("g1", [128, 8])
    g2_d = din("g2", [128, 8])
    cw_d = din("cw", [128, 32])
    cb_d = din("cb", [128, 8])
    mgb_d = din("mgb", [4, 2])
    a2b_d = din("a2b", [128, 4])
    hg_d = din("hg", [128, 2048])
    fg_d = din("fg", [128, 1024])
    w_in = din("w_in", [D, NPROJ])
    g_a2 = din("g_a2", [16, 512])
    wbm = din("wbm", [D, D])
    wbg = din("wbg", [D, D])
    wout = din("wout", [D, D])
    wfg = din("wfg", [D, DFF])
    wfu = din("wfu", [D, DFF])
    wfd = din("wfd", [DFF, D])
    if mode != "A":
        out_d = nc.dram_tensor("out", [TL, D], F32, kind="ExternalOutput").ap()
    dbg_d = {}
    for name, shape in DEBUG.items():
        dbg_d[name] = nc.dram_tensor("dbg_" + name, list(shape), F32, kind="ExternalOutput").ap()

    if mode == "fused":
        pub_t = nc.dram_tensor("pub", [128, 8 * 258], F32)
        gath_t = nc.dram_tensor("gath", [8 * 128, 8 * 258], F32)
        pub = pub_t.ap()
        gath = gath_t.ap()
    elif mode == "A":
        pub = nc.dram_tensor("pub", [128, 8 * 258], F32, kind="ExternalOutput").ap()
        gath = None
    else:
        pub = None
        gath = nc.dram_tensor("gath", [8 * 128, 8 * 258], F32, kind="ExternalInput").ap()

    def sem(name):
        return es.enter_context(nc.semaphore(name))

    NEP = 3
    k.pe = Eng(nc.tensor, "pe", [sem(f"pe{i}") for i in range(NEP)], is_pe=True)
    k.act = Eng(nc.scalar, "act", [sem(f"act{i}") for i in range(NEP)])
    k.dve = Eng(nc.vector, "dve", [sem(f"dve{i}") for i in range(NEP)])
    k.pool = Eng(nc.gpsimd, "pool", [sem(f"pool{i}") for i in range(NEP)])
    k.sp = Eng(nc.sync, "sp", [sem("sp0")])
    k.qs = DmaQ(k.sp, [sem(f"qs{i}") for i in range(8)], "qs")
    k.qg = DmaQ(k.pool, [sem(f"qg{i}") for i in range(8)], "qg")
    ccsem = sem("cc")
    pe, act, dve, pool, qs, qg = k.pe, k.act, k.dve, k.pool, k.qs, k.qg
    op, dma = k.op, k.dma

    def sb(name, shape, dt=F32):
        return es.enter_context(nc.sbuf_tensor(name, list(shape), dt))

    cst = sb("cst", [128, 8])
    maskrow = sb("maskrow_s", [128, 128])
    negrow = sb("negrow_s", [128, 128])
    sel = sb("sel_s", [128, 8])
    onem = sb("onem_s", [128, 8])
    identb = sb("identb", [128, 128], BF16)
    identf = sb("identf", [128, 128])
    cmask = sb("cmask_s", [128, 128])
    rst = sb("rst_s", [128, 512])
    selh = sb("selh_s", [4, 512])
    g1 = sb("g1_s", [128, 8])
    g2 = sb("g2_s", [128, 8])
    cw = sb("cw_s", [128, 32])
    cb = sb("cb_s", [128, 8])
    mgb = sb("mgb_s", [4, 2])
    nbf = sb("nbf_s", [4, 1])
    a2b = sb("a2b_s", [128, 4])
    na2b = sb("na2b_s", [128, 4])
    hg = sb("hg_s", [128, 2048])
    fg = sb("fg_s", [128, 1024])
    etok = sb("etok", [128, NT, 8])
    abc = sb("abc", [128, 4, 36])
    ssA = sb("ssA", [128, 64])
    rsA = sb("rsA", [128, 64])
    tmpA = sb("tmpA", [128, 64])
    junk = sb("junk", [128, 1024], BF16)
    wa2 = sb("wa2", [16, 512], BF16)
    wif = sb("wif", [128, 8, 8], BF16)

    R1 = sb("R1", [128, KT * TT], BF16)
    R2 = sb("R2", [128, 32768], BF16)
    R3 = sb("R3", [128, 20480])

    xnT = R1[:, :].rearrange("p (k t) -> p k t", k=KT)
    hnT = R1[:, 0:KT * TL].rearrange("p (k t) -> p k t", k=KT)
    yT = R2[:, :].rearrange("p (k t) -> p k t", k=16)
    h2 = R2[:, :].bitcast(F32).rearrange("p (t d) -> p t d", t=16)

    class Arena:
        def __init__(self):
            self.off = 0

        def f32(self, n):
            a = R3[:, self.off:self.off + n]
            self.off += n
            assert self.off <= 20480, self.off
            return a

        def bf16(self, n):
            n2 = (n + 1) // 2
            a = R3[:, self.off:self.off + n2].bitcast(BF16)
            self.off += n2
            assert self.off <= 20480, self.off
            return a[:, 0:n]

    ps = [es.enter_context(nc.psum_tensor(f"ps{i}", [128, 512], F32)) for i in range(8)]
    psb = [Buf(f"ps{i}", excl=True) for i in range(8)]

    B_c = Buf("consts")
    for dst, src in ((maskrow, maskrow_d), (negrow, negrow_d), (sel, sel_d), (onem, onem_d),
                     (identf, ident_d), (cmask, cmask_d), (rst, rst_d), (selh, selh_d),
                     (g1, g1_d), (g2, g2_d), (cw, cw_d), (cb, cb_d), (mgb, mgb_d),
                     (a2b, a2b_d), (hg, hg_d), (fg, fg_d)):
        if src is not None:
            dma(qs, dst[:], src[:, :], writes=[B_c])
    dma(qg, identb[:], ident_d[:, :], writes=[B_c])
    dma(qg, wa2[:], g_a2[:, :], writes=[B_c])
    dma(qg, wif[:], w_in[:, 2048:2056].rearrange("(k p) c -> p k c", p=128), writes=[B_c])
    B_cst = Buf("cst")
    for col, val in enumerate((EPS, 1.0, -LN_C, LN_C, 0.0, -0.5)):
        op(pool, lambda col=col, val=val: nc.gpsimd.memset(cst[:, col:col + 1], float(val)), writes=[B_cst])
    c_eps, c_one, c_lnsq, c_lnc, c_zero = (cst[:, i:i + 1] for i in range(5))
    k.barrier()
    B_small = Buf("small")
    op(dve, lambda: nc.vector.tensor_scalar(nbf[:], mgb[:, 1:2], -1.0, None, ALU.mult), reads=[B_c], writes=[B_small])
    op(dve, lambda: nc.vector.tensor_scalar(na2b[:], a2b[:], -1.0, None, ALU.mult), reads=[B_c], writes=[B_small])

    Bss = Buf("ss")
    rs_idx = [0]

    def rstd(ss_ap, n, npart=128, width=1):
        i = rs_idx[0] % (64 // width)
        rs_idx[0] += 1
        t1 = tmpA[0:npart, i * width:(i + 1) * width]
        o = rsA[0:npart, i * width:(i + 1) * width]
        op(act, lambda: nc.scalar.activation(out=t1, in_=ss_ap, func=AF.Ln, bias=c_eps[0:npart], scale=1.0 / n),
           reads=[Bss, B_cst], writes=[Bss])
        op(act, lambda: nc.scalar.activation(out=o, in_=t1, func=AF.Exp, scale=-0.5), reads=[Bss], writes=[Bss])
        return o

    def sigmoid_into(out_ap, in_ap, ta, tb, rbufs, wbufs, neg_bias=None):
        if neg_bias is None:
            op(act, lambda: nc.scalar.activation(out=ta, in_=in_ap, func=AF.Exp, scale=-1.0), reads=rbufs, writes=wbufs)
        else:
            op(act, lambda: nc.scalar.activation(out=ta, in_=in_ap, func=AF.Exp, scale=-1.0, bias=neg_bias),
               reads=rbufs, writes=wbufs)
        op(act, lambda: nc.scalar.activation(out=tb, in_=ta, func=AF.Ln, bias=c_one[0:ta.shape[0]], scale=1.0),
           reads=list(wbufs) + [B_cst], writes=wbufs)
        op(act, lambda: nc.scalar.activation(out=out_ap, in_=tb, func=AF.Exp, scale=-1.0), reads=wbufs, writes=wbufs)

    def dbg(name, src_ap, rbufs):
        if name in dbg_d:
            dma(qs, dbg_d[name], src_ap, reads=rbufs)

    ar = Arena()
    xt = [ar.f32(1024) for _ in range(2)]
    xnb = [ar.bf16(1024) for _ in range(2)]
    Bxt = [Buf("xt0"), Buf("xt1")]
    Bxnb = [Buf("xnb0"), Buf("xnb1")]
    BxnT = [Buf(f"xnT{t}") for t in range(NT)]
    psT = ps[7][:, :].bitcast(BF16)
    BpsT = psb[7]
    def P0a(t):
        i = t % 2
        dma(qs, xt[i], xin[t * 128:(t + 1) * 128, :], writes=[Bxt[i]])
        op(act, lambda: nc.scalar.activation(out=junk[:], in_=xt[i], func=AF.Square, accum_out=ssA[:, t:t + 1]),
           reads=[Bxt[i]], writes=[Bss])
        r = rstd(ssA[:, t:t + 1], D)
        op(act, lambda: nc.scalar.activation(out=xnb[i], in_=xt[i], func=AF.Copy, scale=r),
           reads=[Bxt[i], Bss], writes=[Bxnb[i]])

    def P0b(t):
        i = t % 2
        for kk in range(KT):
            op(pe, lambda kk=kk: nc.tensor.transpose(psT[:, kk * 128:(kk + 1) * 128], xnb[i][:, kk * 128:(kk + 1) * 128], identb[:]),
               reads=[Bxnb[i], B_c], writes=[BpsT])
        op(dve, lambda: nc.vector.tensor_tensor(
            out=xnT[:, :, t * 128:(t + 1) * 128], in0=psT.rearrange("p (k t) -> p k t", k=KT),
            in1=g1[:, :].unsqueeze(2).to_broadcast([128, KT, 128]), op=ALU.mult),
           reads=[BpsT, B_c], writes=[BxnT[t]])

    P0a(0)
    for t in range(NT):
        if t + 1 < NT:
            P0a(t + 1)
        P0b(t)
    if "xnT" in dbg_d:
        xf = ar.f32(TT)
        Bxf = Buf()
        op(dve, lambda: nc.vector.tensor_copy(out=xf, in_=xnT[:, 0, :]), reads=BxnT, writes=[Bxf])
        dbg("xnT", xf, [Bxf])
    k.barrier()

    ar = Arena()
    li = ar.f32(TT)[0:4]
    t1g = ar.f32(TT)[0:4]
    t2g = ar.f32(TT)[0:4]
    nbg = ar.f32(TT)[0:4]
    rg = ar.f32(TT)[0:4]
    eg = ar.f32(TT)[0:4]
    flg = ar.f32(TT)[0:4]
    smallg = ar.f32(256)[0:4]
    Rg = smallg[:, 0:34]
    gg_ = smallg[:, 34:68]
    marr = smallg[:, 68:103]
    Mg = smallg[:, 103:137]
    dg = smallg[:, 137:171]
    cat = smallg[:, 171:207]
    sumnb = smallg[:, 207:208]
    gaT = sb("gaT", [16, TT], BF16)
    Bg = Buf("gates")
    BgaT = Buf("gaT")
    for (t0, ntl) in BLOCKS:
        n = ntl * 128
        c0 = t0 * 128
        for j, (pst, pb) in enumerate(((ps[0], psb[0]), (ps[1], psb[1]))):
            for kk in range(KT):
                op(pe, lambda kk=kk, j=j, pst=pst: nc.tensor.matmul(
                    pst[0:4, 0:n], lhsT=wif[:, kk, j * 4:(j + 1) * 4], rhs=xnT[:, kk, c0:c0 + n],
                    start=(kk == 0), stop=(kk == KT - 1)), reads=[B_c] + BxnT[t0:t0 + ntl], writes=[pb])
        op(act, lambda: nc.scalar.activation(out=li[:, c0:c0 + n], in_=ps[0][0:4, 0:n], func=AF.Identity,
                                             bias=mgb[:, 0:1], scale=1.0), reads=[psb[0], B_c], writes=[Bg])
        op(act, lambda: nc.scalar.activation(out=t1g[:, c0:c0 + n], in_=ps[1][0:4, 0:n], func=AF.Exp,
                                             bias=nbf[:, 0:1], scale=-1.0), reads=[psb[1], B_small], writes=[Bg])
        op(act, lambda: nc.scalar.activation(out=t2g[:, c0:c0 + n], in_=t1g[:, c0:c0 + n], func=AF.Ln,
                                             bias=c_one[0:4], scale=1.0), reads=[Bg, B_cst], writes=[Bg])
    op(dve, lambda: nc.vector.tensor_tensor(out=t2g[:, 0:128], in0=t2g[:, 0:128], in1=maskrow[0:4, :], op=ALU.mult),
       reads=[Bg, B_c], writes=[Bg])
    for (t0, ntl) in BLOCKS:
        n = ntl * 128
        c0 = t0 * 128
        op(dve, lambda: nc.vector.tensor_tensor_scan(out=nbg[:, c0:c0 + n], data0=rst[0:4, 0:n], data1=t2g[:, c0:c0 + n],
                                                     initial=0.0, op0=ALU.mult, op1=ALU.add), reads=[Bg, B_c], writes=[Bg])
    op(dve, lambda: nc.vector.tensor_tensor(out=rg, in0=li, in1=nbg, op=ALU.add), reads=[Bg], writes=[Bg])
    op(dve, lambda: nc.vector.tensor_tensor(out=rg[:, 0:128], in0=rg[:, 0:128], in1=maskrow[0:4, :], op=ALU.mult),
       reads=[Bg, B_c], writes=[Bg])
    op(dve, lambda: nc.vector.tensor_tensor(out=rg[:, 0:128], in0=rg[:, 0:128], in1=negrow[0:4, :], op=ALU.add),
       reads=[Bg, B_c], writes=[Bg])
    rg3 = rg.rearrange("p (c l) -> p c l", l=64)
    nb3 = nbg.rearrange("p (c l) -> p c l", l=64)
    op(dve, lambda: nc.vector.tensor_reduce(out=Rg, in_=rg3, axis=AX.X, op=ALU.max), reads=[Bg], writes=[Bg])
    op(dve, lambda: nc.vector.tensor_scalar(gg_, nb3[:, :, 63], -1.0, None, ALU.mult), reads=[Bg], writes=[Bg])
    op(dve, lambda: nc.vector.memset(marr[:, 0:1], 0.0), writes=[Bg])
    op(dve, lambda: nc.vector.tensor_tensor_scan(out=marr[:, 1:35], data0=Rg, data1=gg_, initial=0.0,
                                                 op0=ALU.max, op1=ALU.add), reads=[Bg], writes=[Bg])
    op(dve, lambda: nc.vector.tensor_tensor(out=Mg, in0=marr[:, 0:34], in1=Rg, op=ALU.max), reads=[Bg], writes=[Bg])
    op(dve, lambda: nc.vector.tensor_tensor(out=dg, in0=marr[:, 0:34], in1=Mg, op=ALU.subtract), reads=[Bg], writes=[Bg])
    op(act, lambda: nc.scalar.activation(out=cat[:, 0:34], in_=dg, func=AF.Exp), reads=[Bg], writes=[Bg])
    op(act, lambda: nc.scalar.activation(out=cat[:, 34:35], in_=marr[:, 34:35], func=AF.Exp), reads=[Bg], writes=[Bg])
    op(dve, lambda: nc.vector.tensor_reduce(out=sumnb, in_=nb3[:, :, 63], axis=AX.X, op=ALU.add), reads=[Bg], writes=[Bg])
    op(act, lambda: nc.scalar.activation(out=cat[:, 35:36], in_=sumnb, func=AF.Exp, scale=-1.0), reads=[Bg], writes=[Bg])
    Mb = Mg.unsqueeze(2).to_broadcast([4, NCH, 64])
    op(dve, lambda: nc.vector.tensor_tensor(out=rg3, in0=rg3, in1=Mb, op=ALU.subtract), reads=[Bg], writes=[Bg])
    op(act, lambda: nc.scalar.activation(out=eg, in_=rg, func=AF.Exp), reads=[Bg], writes=[Bg])
    op(dve, lambda: nc.vector.tensor_tensor(out=nb3, in0=nb3, in1=Mb, op=ALU.subtract), reads=[Bg], writes=[Bg])
    op(act, lambda: nc.scalar.activation(out=flg, in_=nbg, func=AF.Exp, bias=c_lnsq[0:4], scale=1.0),
       reads=[Bg, B_cst], writes=[Bg])
    for h in range(4):
        op(pe, lambda h=h: nc.tensor.matmul(ps[0][:, h * 36:(h + 1) * 36], lhsT=selh[0:4, h * 128:(h + 1) * 128],
                                            rhs=cat, start=True, stop=True), reads=[Bg, B_c], writes=[psb[0]])
    Babc = Buf("abc")
    op(dve, lambda: nc.vector.tensor_copy(out=abc[:, :, :], in_=ps[0][:, 0:144].rearrange("p (h c) -> p h c", h=4)),
       reads=[psb[0]], writes=[Babc])
    for t in range(NT):
        op(pe, lambda t=t: nc.tensor.matmul(ps[1][:, t * 8:t * 8 + 4], lhsT=eg[:, t * 128:(t + 1) * 128],
                                            rhs=identf[0:4, 0:4], start=True, stop=True), reads=[Bg, B_c], writes=[psb[1]])
        op(pe, lambda t=t: nc.tensor.matmul(ps[1][:, t * 8 + 4:t * 8 + 8], lhsT=flg[:, t * 128:(t + 1) * 128],
                                            rhs=identf[0:4, 0:4], start=True, stop=True), reads=[Bg, B_c], writes=[psb[1]])
    Betok = Buf("etok")
    op(dve, lambda: nc.vector.tensor_copy(out=etok[:, :, :], in_=ps[1][:, 0:NT * 8].rearrange("p (t c) -> p t c", c=8)),
       reads=[psb[1]], writes=[Betok])
    w16 = ar.bf16(KT * 16).rearrange("p (k c) -> p k c", k=KT)
    Bw16 = Buf("w16")
    dma(qg, w16, w_in[:, 5128:5144].rearrange("(k p) c -> p k c", p=128), writes=[Bw16])
    for (t0, ntl) in BLOCKS:
        n = ntl * 128
        c0 = t0 * 128
        for kk in range(KT):
            op(pe, lambda kk=kk: nc.tensor.matmul(ps[2][0:16, 0:n], lhsT=w16[:, kk, :], rhs=xnT[:, kk, c0:c0 + n],
                                                  start=(kk == 0), stop=(kk == KT - 1)),
               reads=[Bw16] + BxnT[t0:t0 + ntl], writes=[psb[2]])
        op(act, lambda: nc.scalar.activation(out=gaT[:, c0:c0 + n], in_=ps[2][0:16, 0:n], func=AF.Copy),
           reads=[psb[2]], writes=[BgaT])
    dbg("etok", etok[:, :, :].rearrange("p t c -> p (t c)"), [Betok])
    dbg("abc", abc[:, :, :].rearrange("p h c -> p (h c)"), [Babc])
    k.barrier()

    ar = Arena()
    wbuf = [ar.bf16(KT * 768).rearrange("p (k c) -> p k c", k=KT) for _ in range(2)]
    Bw = [[Buf(f"w{s_}_{j}") for j in range(4)] for s_ in range(2)]
    kpre = ar.f32(516)
    qpre = ar.f32(516)
    kc = ar.f32(512)
    qc = ar.f32(512)
    ta = ar.f32(512)
    tb = ar.f32(512)
    sgt = ar.f32(512)
    nbt = ar.f32(512)
    ekt = ar.f32(512)
    kTb = [ar.bf16(512) for _ in range(2)]
    qTb = [ar.bf16(512) for _ in range(2)]
    ktok = ar.bf16(8 * 128).rearrange("p (t c) -> p t c", t=8)
    vtok = ar.bf16(8 * 258).rearrange("p (t c) -> p t c", t=8)
    Gt = ar.f32(8 * 256).rearrange("p (t c) -> p t c", t=8)
    ga_ = ar.f32(256)
    gb_ = ar.f32(256)
    gc_ = ar.f32(256)
    go_ = ar.f32(256)
    attb = ar.bf16(128)
    Sbf = [ar.bf16(258) for _ in range(4)]
    Xs = [ar.f32(258) for _ in range(2)]
    ybuf = [ar.bf16(256) for _ in range(2)]
    gsb = ar.f32(8 * 258).rearrange("p (r c) -> p r c", r=8)
    pubb = ar.f32(258)
    avec = ar.f32(40)
    nbend = ar.f32(40)
    decp = ar.f32(8)
    tiny = ar.f32(16)
    Bkpre, Bqpre, Bkc, Bqc = Buf(), Buf(), Buf(), Buf()
    Bta, Bnb, Bek = Buf(), Buf(), Buf()
    BkT = [Buf(), Buf()]
    BqT = [Buf(), Buf()]
    Bktok = [Buf() for _ in range(8)]
    Bvtok = [Buf() for _ in range(8)]
    BG = [Buf() for _ in range(8)]
    Bgt = Buf()
    Batt = Buf()
    BS = [Buf() for _ in range(4)]
    BX = [Buf(), Buf()]
    By = [Buf(), Buf()]
    Bgs, Bpub, Btiny = Buf(), Buf(), Buf()
    Bav = [Buf() for _ in range(len(BLOCKS) + 1)]
    ByT = [Buf(f"yT{t}") for t in range(16)]

    def load_unit_weights(u, phase, slot):
        m = u < 4
        h = u % 4
        if m:
            cq, ck, cv, co = h * 128, 512 + h * 128, 1024 + h * 256, 2056 + h * 256
        else:
            cq, ck, cv, co = 3080 + h * 128, 3592 + h * 128, 4104 + h * 256, 5144 + h * 256
        w = wbuf[slot]

        def ld(dst0, c, n, bi_):
            dma(qg, w[:, :, dst0:dst0 + n], w_in[:, c:c + n].rearrange("(k p) c -> p k c", p=128), writes=[Bw[slot][bi_]])
        ld(128, ck, 128, 1)
        ld(256, cv, 256, 2)
        if phase == 2:
            ld(0, cq, 128, 0)
            ld(512, co, 256, 3)

    psQ, psK, psV, psA, psO = ps[0], ps[1], ps[2], ps[3], ps[6]
    BpsQ, BpsK, BpsV, BpsA, BpsO = psb[0], psb[1], psb[2], psb[3], psb[6]
    psP = [ps[4], ps[5]]
    BpsP = [psb[4], psb[5]]
    TILES = [(bi, tt) for bi, (t0, ntl) in enumerate(BLOCKS) for tt in range(ntl)]

    def mixer_unit(u, phase, slot):
        m = u < 4
        h = u % 4
        NV = 257 if m else 256
        w = wbuf[slot]
        full = phase == 2
        st = {"xi": 0, "sidx": {}}
        op(dve, lambda: nc.vector.memset(Xs[0][:, 0:NV], 0.0), writes=[BX[0]])
        if full:
            dma(qs, gsb[:, :, :], gath[:, u * 258:(u + 1) * 258].rearrange("(r p) c -> p r c", p=128), writes=[Bgs])
            op(dve, lambda: nc.vector.tensor_tensor(out=decp[:, 0:8], in0=gsb[:, :, 257], in1=sel[:, :], op=ALU.mult),
               reads=[Bgs, B_c], writes=[Btiny])
            op(dve, lambda: nc.vector.tensor_tensor(out=decp[:, 0:8], in0=decp[:, 0:8], in1=onem[:, :], op=ALU.add),
               reads=[Btiny, B_c], writes=[Btiny])
            op(dve, lambda: nc.vector.tensor_tensor(out=gsb[:, 0:7, 0:NV], in0=gsb[:, 0:7, 0:NV],
                                                    in1=sel[:, 0:7].unsqueeze(2).to_broadcast([128, 7, NV]), op=ALU.mult),
               reads=[B_c, Bgs], writes=[Bgs])
            for j in range(7):
                op(dve, lambda j=j: nc.vector.scalar_tensor_tensor(out=Xs[0][:, 0:NV], in0=Xs[0][:, 0:NV],
                                                                   scalar=decp[:, j:j + 1], in1=gsb[:, j, 0:NV],
                                                                   op0=ALU.mult, op1=ALU.add),
                   reads=[Bgs, Btiny], writes=[BX[0]])
        if m:
            op(pool, lambda: nc.gpsimd.memset(kpre[:, 0:3], 0.0), writes=[Bkpre])
            if full:
                op(pool, lambda: nc.gpsimd.memset(qpre[:, 0:3], 0.0), writes=[Bqpre])
        else:
            op(pool, lambda: nc.gpsimd.memset(avec[:, 0:1], 1.0), writes=[Bav[0]])

        def S0_stages(bi):
            t0, ntl = BLOCKS[bi]
            par = bi % 2
            n = ntl * 128
            c0 = t0 * 128
            rx = BxnT[t0:t0 + ntl]
            kT_, qT_ = kTb[par], qTb[par]

            def proj_k():
                for kk in range(KT):
                    op(pe, lambda kk=kk: nc.tensor.matmul(psK[:, 0:n], lhsT=w[:, kk, 128:256], rhs=xnT[:, kk, c0:c0 + n],
                                                          start=(kk == 0), stop=(kk == KT - 1)), reads=[Bw[slot][1]] + rx, writes=[BpsK])

            def proj_q():
                for kk in range(KT):
                    op(pe, lambda kk=kk: nc.tensor.matmul(psQ[:, 0:n], lhsT=w[:, kk, 0:128], rhs=xnT[:, kk, c0:c0 + n],
                                                          start=(kk == 0), stop=(kk == KT - 1)), reads=[Bw[slot][0]] + rx, writes=[BpsQ])

            if m:
                chains = [(psK, BpsK, kpre, Bkpre, kc, Bkc, kT_, BkT[par], 4 + h)]
                if full:
                    chains.append((psQ, BpsQ, qpre, Bqpre, qc, Bqc, qT_, BqT[par], h))

                def f1():
                    proj_k()
                    if full:
                        proj_q()
                    for (psx, Bpsx, pre, Bpre, cx, Bcx, outb, Bout, ktile) in chains:
                        op(act, lambda psx=psx, pre=pre: nc.scalar.activation(out=pre[:, 3:3 + n], in_=psx[:, 0:n], func=AF.Copy),
                           reads=[Bpsx], writes=[Bpre])

                def f2(chain):
                    for (psx, Bpsx, pre, Bpre, cx, Bcx, outb, Bout, ktile) in [chain]:
                        wv = cw[:, ktile * 4:(ktile + 1) * 4]
                        op(dve, lambda cx=cx, pre=pre, wv=wv, ktile=ktile: nc.vector.tensor_scalar(
                            cx[:, 0:n], pre[:, 0:n], wv[:, 0:1], cb[:, ktile:ktile + 1], ALU.mult, ALU.add),
                           reads=[Bpre, B_c], writes=[Bcx])
                        for i in range(1, 4):
                            op(dve, lambda i=i, cx=cx, pre=pre, wv=wv: nc.vector.scalar_tensor_tensor(
                                out=cx[:, 0:n], in0=pre[:, i:i + n], scalar=wv[:, i:i + 1], in1=cx[:, 0:n], op0=ALU.mult, op1=ALU.add),
                               reads=[Bpre, B_c, Bcx], writes=[Bcx])
                        op(pool, lambda pre=pre: nc.gpsimd.tensor_copy(out=pre[:, 0:3], in_=pre[:, n:n + 3]), reads=[Bpre], writes=[Bpre])

                def f3(chain):
                    for (psx, Bpsx, pre, Bpre, cx, Bcx, outb, Bout, ktile) in [chain]:
                        sigmoid_into(sgt[:, 0:n], cx[:, 0:n], ta[:, 0:n], tb[:, 0:n], [Bcx], [Bta])
                        op(pool, lambda outb=outb, cx=cx: nc.gpsimd.tensor_tensor(out=outb[:, 0:n], in0=cx[:, 0:n], in1=sgt[:, 0:n], op=ALU.mult),
                           reads=[Bcx, Bta], writes=[Bout])
            else:
                nch = 2 * ntl
                nb3_ = nbt[:, 0:n].rearrange("p (c l) -> p c l", l=64)

                def f1():
                    op(pe, lambda: nc.tensor.matmul(psQ[:, 0:n], lhsT=wa2[:, h * 128:(h + 1) * 128], rhs=gaT[:, c0:c0 + n],
                                                    start=True, stop=True), reads=[B_c, BgaT], writes=[BpsQ])
                    op(act, lambda: nc.scalar.activation(out=ta[:, 0:n], in_=psQ[:, 0:n], func=AF.Exp, bias=na2b[:, h:h + 1], scale=-1.0),
                       reads=[BpsQ, B_small], writes=[Bta])
                    proj_k()
                    if full:
                        proj_q()
                    op(act, lambda: nc.scalar.activation(out=tb[:, 0:n], in_=ta[:, 0:n], func=AF.Ln, bias=c_one, scale=1.0),
                       reads=[Bta, B_cst], writes=[Bta])

                def f2():
                    if bi == 0:
                        op(dve, lambda: nc.vector.tensor_tensor(out=tb[:, 0:128], in0=tb[:, 0:128], in1=maskrow[:, :], op=ALU.mult),
                           reads=[Bta, B_c], writes=[Bta])
                    op(dve, lambda: nc.vector.tensor_tensor_scan(out=nbt[:, 0:n], data0=rst[:, 0:n], data1=tb[:, 0:n], initial=0.0,
                                                                 op0=ALU.mult, op1=ALU.add), reads=[Bta, B_c], writes=[Bnb])

                def f2b():
                    op(act, lambda: nc.scalar.activation(out=avec[:, 2 * t0 + 1:2 * t0 + 1 + nch], in_=nb3_[:, :, 63], func=AF.Exp,
                                                         scale=-1.0 / 16.0), reads=[Bnb], writes=[Bav[bi + 1]])
                    if not full:
                        op(dve, lambda: nc.vector.tensor_copy(out=nbend[:, 2 * t0:2 * t0 + nch], in_=nb3_[:, :, 63]), reads=[Bnb],
                           writes=[Bav[bi + 1]])
                    op(act, lambda: nc.scalar.activation(out=ekt[:, 0:n], in_=nbt[:, 0:n], func=AF.Exp, scale=1.0 / 16.0),
                       reads=[Bnb], writes=[Bek])

                def f3():
                    if bi == 0:
                        op(dve, lambda: nc.vector.tensor_tensor(out=ekt[:, 0:128], in0=ekt[:, 0:128], in1=maskrow[:, :], op=ALU.mult),
                           reads=[Bek, B_c], writes=[Bek])
                    op(dve, lambda: nc.vector.tensor_tensor(out=kT_[:, 0:n], in0=psK[:, 0:n], in1=ekt[:, 0:n], op=ALU.mult),
                       reads=[BpsK, Bek], writes=[BkT[par]])

                def f3b():
                    if full:
                        op(act, lambda: nc.scalar.activation(out=ekt[:, 0:n], in_=nbt[:, 0:n], func=AF.Exp, bias=c_lnc, scale=-1.0 / 16.0),
                           reads=[Bnb, B_cst], writes=[Bek])
                        op(dve, lambda: nc.vector.tensor_tensor(out=qT_[:, 0:n], in0=psQ[:, 0:n], in1=ekt[:, 0:n], op=ALU.mult),
                           reads=[BpsQ, Bek], writes=[BqT[par]])
            if m:
                stages = [f1] + [(lambda c=c: f2(c)) for c in chains] + [(lambda c=c: f3(c)) for c in chains]
            else:
                stages = [f1, f2, f2b, f3, f3b]
            return stages + [lambda: S0b(bi)]

        def S0b(bi):
            t0, ntl = BLOCKS[bi]
            par = bi % 2
            n = ntl * 128
            kT_ = kTb[par]
            for tt in range(ntl):
                op(pe, lambda tt=tt: nc.tensor.transpose(psT[:, tt * 128:(tt + 1) * 128], kT_[:, tt * 128:(tt + 1) * 128], identb[:]),
                   reads=[BkT[par], B_c], writes=[BpsT])
            op(act, lambda: nc.scalar.activation(out=ktok[:, par * 4:par * 4 + ntl, :],
                                                 in_=psT[:, 0:n].rearrange("p (t c) -> p t c", t=ntl),
                                                 func=AF.Copy), reads=[BpsT], writes=Bktok[par * 4:par * 4 + ntl])

        def S1(bi, tt):
            t0, ntl = BLOCKS[bi]
            t = t0 + tt
            sl = (bi % 2) * 4 + tt
            outp = full and t >= 1
            ncol = 512 if outp else 256
            for kk in range(KT):
                op(pe, lambda kk=kk: nc.tensor.matmul(psV[:, 0:ncol], lhsT=xnT[:, kk, t * 128:(t + 1) * 128],
                                                      rhs=w[:, kk, 256:256 + ncol], start=(kk == 0), stop=(kk == KT - 1)),
                   reads=[Bw[slot][2], Bw[slot][3], BxnT[t]], writes=[BpsV])
            if m:
                op(act, lambda: nc.scalar.activation(out=vtok[:, sl, 0:256], in_=psV[:, 0:256], func=AF.Copy,
                                                     scale=etok[:, t, h:h + 1]), reads=[BpsV, Betok], writes=[Bvtok[sl]])
                op(pool, lambda: nc.gpsimd.tensor_copy(out=vtok[:, sl, 256:257], in_=etok[:, t, h:h + 1]),
                   reads=[Betok, Bvtok[sl]], writes=[Bvtok[sl]])
            else:
                op(act, lambda: nc.scalar.activation(out=vtok[:, sl, 0:256], in_=psV[:, 0:256], func=AF.Copy),
                   reads=[BpsV], writes=[Bvtok[sl]])
            if outp:
                hgv = hg[:, u * 256:(u + 1) * 256]
                if m:
                    sigmoid_into(gc_, psV[:, 256:512], ga_, gb_, [BpsV], [Bgt])
                    op(pool, lambda: nc.gpsimd.tensor_tensor(out=Gt[:, sl, :], in0=gc_, in1=hgv, op=ALU.mult),
                       reads=[Bgt, B_c], writes=[BG[sl]])
                else:
                    op(act, lambda: nc.scalar.activation(out=go_, in_=psV[:, 256:512], func=AF.Copy), reads=[BpsV], writes=[Bgt])
                    sigmoid_into(gc_, go_, ga_, gb_, [Bgt], [Bgt])
                    op(pool, lambda: nc.gpsimd.tensor_tensor(out=gc_, in0=gc_, in1=hgv, op=ALU.mult),
                       reads=[Bgt, B_c], writes=[Bgt])
                    op(pool, lambda: nc.gpsimd.tensor_tensor(out=Gt[:, sl, :], in0=go_, in1=gc_, op=ALU.mult),
                       reads=[Bgt], writes=[BG[sl]])

        def S2a(bi, tt):
            t0, ntl = BLOCKS[bi]
            t = t0 + tt
            par = bi % 2
            sl = par * 4 + tt
            outp = full and t >= 1
            sidx = []
            st["sidx"][t] = sidx
            for half in range(2):
                c = 2 * t + half
                rows = slice(half * 64, (half + 1) * 64)
                a_c = abc[:, h, c:c + 1] if m else avec[:, c:c + 1]
                Ba = Babc if m else (Bav[0] if c == 0 else Bav[(c - 1) // 8 + 1])
                xi = st["xi"]
                xin_, xout_ = Xs[xi], Xs[1 - xi]
                if outp:
                    si = c % 4
                    sidx.append(si)
                    op(act, lambda: nc.scalar.activation(out=Sbf[si][:, 0:NV], in_=xin_[:, 0:NV], func=AF.Copy, scale=a_c),
                       reads=[BX[xi], Ba], writes=[BS[si]])
                pp = c % 2
                op(pe, lambda: nc.tensor.matmul(psP[pp][:, 0:NV], lhsT=ktok[rows, sl, :], rhs=vtok[rows, sl, 0:NV],
                                                start=True, stop=True), reads=[Bktok[sl], Bvtok[sl]], writes=[BpsP[pp]])
                op(dve, lambda: nc.vector.scalar_tensor_tensor(out=xout_[:, 0:NV], in0=xin_[:, 0:NV], scalar=a_c,
                                                               in1=psP[pp][:, 0:NV], op0=ALU.mult, op1=ALU.add),
                   reads=[BX[xi], Ba, BpsP[pp]], writes=[BX[1 - xi]])
                st["xi"] = 1 - xi

        def S2b(bi, tt):
            t0, ntl = BLOCKS[bi]
            t = t0 + tt
            par = bi % 2
            sl = par * 4 + tt
            outp = full and t >= 1
            if not outp:
                return
            kT_, qT_ = kTb[par], qTb[par]
            sidx = st["sidx"][t]
            tq = slice(tt * 128, (tt + 1) * 128)
            op(pe, lambda: nc.tensor.matmul(psA[:, 0:128], lhsT=kT_[:, tq], rhs=qT_[:, tq], start=True, stop=True),
               reads=[BkT[par], BqT[par]], writes=[BpsA])
            op(dve, lambda: nc.vector.tensor_tensor(out=attb, in0=psA[:, 0:128], in1=cmask[:, :], op=ALU.mult),
               reads=[BpsA, B_c], writes=[Batt])
            op(pe, lambda: nc.tensor.matmul(psO[:, 0:NV], lhsT=attb, rhs=vtok[:, sl, 0:NV], start=True, stop=False),
               reads=[Batt, Bvtok[sl]], writes=[BpsO])
            op(pe, lambda: nc.tensor.matmul(psO[0:64, 0:NV], lhsT=qT_[:, tt * 128:tt * 128 + 64], rhs=Sbf[sidx[0]][:, 0:NV],
                                            start=False, stop=False), reads=[BqT[par], BS[sidx[0]]], writes=[BpsO])
            op(pe, lambda: nc.tensor.matmul(psO[64:128, 0:NV], lhsT=qT_[:, tt * 128 + 64:tt * 128 + 128], rhs=Sbf[sidx[1]][:, 0:NV],
                                            start=False, stop=True), reads=[BqT[par], BS[sidx[1]]], writes=[BpsO])

        def S2c(bi, tt):
            t0, ntl = BLOCKS[bi]
            t = t0 + tt
            sl = (bi % 2) * 4 + tt
            if not (full and t >= 1):
                return
            ci = (u * 16 + (t - 1)) % 16
            ssc = ssA[:, 32 + ci:33 + ci]
            if m:
                d1 = tiny[:, 0:1]
                rden = tiny[:, 1:2]
                op(dve, lambda: nc.vector.scalar_tensor_tensor(out=tiny[:, 3:4], in0=psO[:, 256:257], scalar=-1.0,
                                                               in1=etok[:, t, 4 + h:5 + h], op0=ALU.mult, op1=ALU.max),
                   reads=[BpsO, Betok], writes=[Btiny])
                op(dve, lambda: nc.vector.tensor_tensor(out=d1, in0=psO[:, 256:257], in1=tiny[:, 3:4], op=ALU.max),
                   reads=[BpsO, Btiny], writes=[Btiny])
                op(dve, lambda: nc.vector.reciprocal(out=rden, in_=d1), reads=[Btiny], writes=[Btiny])
                op(act, lambda: nc.scalar.activation(out=junk[:, 0:256], in_=psO[:, 0:256], func=AF.Square, scale=rden,
                                                     accum_out=ssc), reads=[BpsO, Btiny], writes=[Bss])
            else:
                op(act, lambda: nc.scalar.activation(out=junk[:, 0:256], in_=psO[:, 0:256], func=AF.Square, accum_out=ssc),
                   reads=[BpsO], writes=[Bss])
            r = rstd(ssc, 256)
            if m:
                sc = tiny[:, 2:3]
                op(dve, lambda: nc.vector.tensor_tensor(out=sc, in0=rden, in1=r, op=ALU.mult), reads=[Btiny, Bss], writes=[Btiny])
                rb = [Btiny]
            else:
                sc = r
                rb = [Bss]
            yi = t % 2
            op(dve, lambda: nc.vector.scalar_tensor_tensor(out=ybuf[yi], in0=psO[:, 0:256], scalar=sc, in1=Gt[:, sl, :],
                                                           op0=ALU.mult, op1=ALU.mult), reads=[BpsO, BG[sl]] + rb, writes=[By[yi]])

        def S3(bi, tt):
            t0, ntl = BLOCKS[bi]
            t = t0 + tt
            if not (full and t >= 1):
                return
            yi = t % 2
            for j in range(2):
                op(pe, lambda j=j: nc.tensor.transpose(psT[:, 512 + j * 128:512 + (j + 1) * 128], ybuf[yi][:, j * 128:(j + 1) * 128],
                                                       identb[:]), reads=[By[yi], B_c], writes=[BpsT])
            op(dve, lambda: nc.vector.tensor_copy(out=yT[:, 2 * u:2 * u + 2, (t - 1) * 128:t * 128],
                                                  in_=psT[:, 512:768].rearrange("p (j c) -> p j c", j=2)),
               reads=[BpsT], writes=[ByT[t - 1]])

        for f in S0_stages(0):
            f()
        S1(*TILES[0])
        sched = {}
        for i, (bi, tt) in enumerate(TILES):
            ntl_b = BLOCKS[bi][1]
            if tt == 0 and bi + 1 < len(BLOCKS):
                stages = S0_stages(bi + 1)
                L = len(stages) - 1
                nslot = 2 * ntl_b - 2
                for j, f in enumerate(stages[:-1]):
                    slot_ = (j * nslot) // L if nslot > 0 else 0
                    sched.setdefault((i + slot_ // 2, slot_ % 2), []).append(f)
                sched.setdefault((i + ntl_b - 1, 0), []).append(stages[-1])
            for f in sched.pop((i, 0), []):
                f()
            if full:
                S2a(bi, tt)
                if i + 1 < len(TILES):
                    S1(*TILES[i + 1])
                S2b(bi, tt)
                S2c(bi, tt)
            else:
                if i + 1 < len(TILES):
                    S1(*TILES[i + 1])
                S2a(bi, tt)
            if i >= 1:
                S3(*TILES[i - 1])
            for f in sched.pop((i, 1), []):
                f()
        assert not sched, sched.keys()
        S3(*TILES[-1])
        xi = st["xi"]
        if not full:
            xf_ = Xs[xi]
            if m:
                fin = abc[:, h, 34:35]
                dec = abc[:, h, 35:36]
                Bf = [Babc]
            else:
                fin = avec[:, NCH:NCH + 1]
                op(dve, lambda: nc.vector.tensor_reduce(out=tiny[:, 4:5], in_=nbend[:, 0:NCH], axis=AX.X, op=ALU.add),
                   reads=Bav, writes=[Btiny])
                op(act, lambda: nc.scalar.activation(out=tiny[:, 5:6], in_=tiny[:, 4:5], func=AF.Exp, scale=-1.0 / 16.0),
                   reads=[Btiny], writes=[Btiny])
                dec = tiny[:, 5:6]
                Bf = Bav
            op(dve, lambda: nc.vector.memset(pubb[:, 0:258], 0.0), writes=[Bpub])
            op(dve, lambda: nc.vector.tensor_scalar(pubb[:, 0:NV], xf_[:, 0:NV], fin, None, ALU.mult),
               reads=[BX[xi]] + Bf, writes=[Bpub])
            op(dve, lambda: nc.vector.tensor_copy(out=pubb[:, 257:258], in_=dec), reads=Bf + [Btiny], writes=[Bpub])
            dma(qs, pub[:, u * 258:(u + 1) * 258], pubb[:, 0:258], reads=[Bpub])

    for phase in {"fused": (1, 2), "A": (1,), "B": (2,)}[mode]:
        load_unit_weights(0, phase, 0)
        for u in range(8):
            if u + 1 < 8:
                load_unit_weights(u + 1, phase, (u + 1) % 2)
            mixer_unit(u, phase, u % 2)
        if phase == 1 and mode == "A":
            for q in (qs, qg):
                for t_ in q.last:
                    k.sp.wait_tok(t_)
            k.barrier()
            es.close()
            return nc
        if phase == 1:
            for t_ in qs.last:
                pool.wait_tok(t_)
            import os
            if os.environ.get("NOCC"):
                nc.gpsimd.dma_start(out=gath[0:128, :], in_=pub[:, :]).then_inc(ccsem, 16)
                k.sp.e.wait_ge(ccsem, 16)
                nc.gpsimd.wait_ge(ccsem, 16)
            else:
                nc.gpsimd.collective_compute("AllGather", ALU.bypass, replica_groups=[list(range(NCORES))],
                                             ins=[pub_t.ap().opt()], outs=[gath_t.ap().opt()]).then_inc(ccsem)
                k.sp.e.wait_ge(ccsem, 1)
                nc.gpsimd.wait_ge(ccsem, 1)
            for e in (pe, act, dve, pool):
                e.new_epoch()
    if "yT" in dbg_d:
        yf = R3[:, 0:2048]
        Byf = Buf()
        k.barrier()
        op(dve, lambda: nc.vector.tensor_copy(out=yf, in_=yT[:, 0, :]), reads=ByT, writes=[Byf])
        dbg("yT", yf, [Byf])
    k.barrier()

    ar = Arena()
    mergedT = ar.bf16(KT * TL).rearrange("p (k t) -> p k t", k=KT)
    x1_mark = ar.off
    wx = [ar.bf16(KT * 512).rearrange("p (k j c) -> p k j c", k=KT, j=4) for _ in range(2)]
    Bwx = [[Buf() for _ in range(4)] for _ in range(2)]
    sa = ar.f32(512)
    sb_ = ar.f32(512)
    sgm = ar.f32(512)
    sgg = ar.f32(512)
    m1 = ar.f32(512)
    m2 = ar.f32(512)
    Bsa, Bsgm, Bsgg, Bm1, Bm2 = Buf(), Buf(), Buf(), Buf(), Buf()
    BmT = [Buf() for _ in range(4)]

    def load_wx(o, slot):
        cs = slice(o * 128, (o + 1) * 128)
        dma(qg, wx[slot][:, :, 0, :], wbm[:, cs].rearrange("(k p) c -> p k c", p=128), writes=[Bwx[slot][0]])
        dma(qg, wx[slot][:, :, 1, :], wbg[:, cs].rearrange("(k p) c -> p k c", p=128), writes=[Bwx[slot][1]])
        dma(qg, wx[slot][:, :, 2, :], w_in[:, 6168 + o * 128:6168 + (o + 1) * 128].rearrange("(k p) c -> p k c", p=128), writes=[Bwx[slot][2]])
        dma(qg, wx[slot][:, :, 3, :], w_in[:, 7192 + o * 128:7192 + (o + 1) * 128].rearrange("(k p) c -> p k c", p=128), writes=[Bwx[slot][3]])

    load_wx(0, 0)
    it = 0
    for o in range(8):
        if o + 1 < 8:
            load_wx(o + 1, (o + 1) % 2)
        s = o % 2
        for b in range(4):
            pb = (it % 2) * 4
            it += 1
            y0 = b * 512
            c0 = 128 + b * 512
            for j in range(4):
                for kk in range(KT):
                    if j == 0:
                        rhs = yT[:, kk, y0:y0 + 512]
                    elif j == 1:
                        rhs = yT[:, 8 + kk, y0:y0 + 512]
                    else:
                        rhs = xnT[:, kk, c0:c0 + 512]
                    op(pe, lambda kk=kk, j=j, rhs=rhs: nc.tensor.matmul(ps[pb + j][:, :], lhsT=wx[s][:, kk, j, :], rhs=rhs,
                                                                        start=(kk == 0), stop=(kk == KT - 1)),
                       reads=[Bwx[s][j]], writes=[psb[pb + j]])
            sigmoid_into(sgm, ps[pb + 2][:, :], sa, sb_, [psb[pb + 2]], [Bsgm])
            sigmoid_into(sgg, ps[pb + 3][:, :], sa, sb_, [psb[pb + 3]], [Bsgg])
            op(dve, lambda: nc.vector.tensor_tensor(out=m1, in0=ps[pb][:, :], in1=sgm, op=ALU.mult), reads=[psb[pb], Bsgm], writes=[Bm1])
            op(dve, lambda: nc.vector.tensor_tensor(out=m2, in0=ps[pb + 1][:, :], in1=sgg, op=ALU.mult), reads=[psb[pb + 1], Bsgg], writes=[Bm2])
            op(pool, lambda: nc.gpsimd.tensor_tensor(out=mergedT[:, o, y0:y0 + 512], in0=m1, in1=m2, op=ALU.add),
               reads=[Bm1, Bm2], writes=[BmT[b]])
    k.barrier()

    ar = Arena()
    ar.off = x1_mark
    wo = ar.bf16(KT * 1024).rearrange("p (k c) -> p k c", k=KT)
    Bwo = [Buf(), Buf()]
    dma(qg, wo[:, :, 0:512], wout[:, 0:512].rearrange("(k p) c -> p k c", p=128), writes=[Bwo[0]])
    dma(qg, wo[:, :, 512:1024], wout[:, 512:1024].rearrange("(k p) c -> p k c", p=128), writes=[Bwo[1]])
    xt = [ar.f32(1024) for _ in range(2)]
    hnb = [ar.bf16(1024) for _ in range(2)]
    Bxt = [Buf(), Buf()]
    Bhnb = [Buf(), Buf()]
    Bh2 = [Buf(f"h2_{t}") for t in range(16)]
    BhnT = [Buf(f"hnT{t}") for t in range(16)]
    def X2a(t):
        i = t % 2
        pb_ = (t % 2) * 2
        dma(qs, xt[i], xin[(t + 1) * 128:(t + 2) * 128, :], writes=[Bxt[i]])
        for half in range(2):
            for kk in range(KT):
                op(pe, lambda kk=kk, half=half: nc.tensor.matmul(ps[pb_ + half][:, :], lhsT=mergedT[:, kk, t * 128:(t + 1) * 128],
                                                                 rhs=wo[:, kk, half * 512:(half + 1) * 512],
                                                                 start=(kk == 0), stop=(kk == KT - 1)),
                   reads=[Bwo[half], BmT[t // 4]], writes=[psb[pb_ + half]])
            op(dve, lambda half=half: nc.vector.tensor_tensor(out=h2[:, t, half * 512:(half + 1) * 512], in0=ps[pb_ + half][:, :],
                                                              in1=xt[i][:, half * 512:(half + 1) * 512], op=ALU.add),
               reads=[psb[pb_ + half], Bxt[i]], writes=[Bh2[t]])

    def X2b(t):
        i = t % 2
        op(act, lambda: nc.scalar.activation(out=junk[:], in_=h2[:, t, :], func=AF.Square, accum_out=ssA[:, t:t + 1]),
           reads=[Bh2[t]], writes=[Bss])
        r = rstd(ssA[:, t:t + 1], D)
        op(act, lambda: nc.scalar.activation(out=hnb[i], in_=h2[:, t, :], func=AF.Copy, scale=r), reads=[Bh2[t], Bss], writes=[Bhnb[i]])

    def X2c(t):
        i = t % 2
        for kk in range(KT):
            op(pe, lambda kk=kk: nc.tensor.transpose(psT[:, kk * 128:(kk + 1) * 128], hnb[i][:, kk * 128:(kk + 1) * 128], identb[:]),
               reads=[Bhnb[i], B_c], writes=[BpsT])
        op(dve, lambda: nc.vector.tensor_tensor(out=hnT[:, :, t * 128:(t + 1) * 128], in0=psT.rearrange("p (k t) -> p k t", k=KT),
                                                in1=g2[:, :].unsqueeze(2).to_broadcast([128, KT, 128]), op=ALU.mult),
           reads=[BpsT, B_c], writes=[BhnT[t]])

    X2a(0)
    for t in range(16):
        if t + 1 < 16:
            X2a(t + 1)
        X2b(t)
        if t >= 1:
            X2c(t - 1)
    X2c(15)
    k.barrier()

    ar = Arena()
    JB = [(0, 4), (4, 4), (8, 4), (12, 4), (16, 4), (20, 2)]
    wg_ = [ar.bf16(KT * 512).rearrange("p (k c) -> p k c", k=KT) for _ in range(2)]
    wu_ = [ar.bf16(KT * 512).rearrange("p (k c) -> p k c", k=KT) for _ in range(2)]
    wd_ = [ar.bf16(4 * 1024).rearrange("p (j c) -> p j c", j=4) for _ in range(2)]
    Bwf = [[Buf() for _ in range(3)] for _ in range(2)]
    ffT = [ar.bf16(4 * 512).rearrange("p (j c) -> p j c", j=4) for _ in range(2)]
    Bff = [Buf(), Buf()]
    fa = ar.f32(512)
    fb = ar.f32(512)
    fsg = ar.f32(512)
    fgs = ar.f32(512)
    Bfa, Bfsg, Bfgs = Buf(), Buf(), Buf()
    ot = [ar.f32(1024) for _ in range(2)]
    Bot = [Buf(), Buf()]

    def load_wf(J, slot):
        j0, nj = JB[J]
        n = nj * 128
        dma(qg, wg_[slot][:, :, 0:n], wfg[:, j0 * 128:j0 * 128 + n].rearrange("(k p) c -> p k c", p=128), writes=[Bwf[slot][0]])
        dma(qg, wu_[slot][:, :, 0:n], wfu[:, j0 * 128:j0 * 128 + n].rearrange("(k p) c -> p k c", p=128), writes=[Bwf[slot][1]])
        dma(qg, wd_[slot][:, 0:nj, :], wfd[j0 * 128:j0 * 128 + n, :].rearrange("(j p) c -> p j c", p=128), writes=[Bwf[slot][2]])

    load_wf(0, 0)
    itg = 0
    ito = 0
    for J in range(len(JB)):
        if J + 1 < len(JB):
            load_wf(J + 1, (J + 1) % 2)
        s = J % 2
        j0, nj = JB[J]
        for b in range(4):
            fi = (J * 4 + b) % 2
            for j in range(nj):
                pg = (itg % 2) * 2
                itg += 1
                for which, wsrc in ((0, wg_[s]), (1, wu_[s])):
                    for kk in range(KT):
                        op(pe, lambda kk=kk, which=which, wsrc=wsrc: nc.tensor.matmul(
                            ps[pg + which][:, :], lhsT=wsrc[:, kk, j * 128:(j + 1) * 128], rhs=hnT[:, kk, b * 512:(b + 1) * 512],
                            start=(kk == 0), stop=(kk == KT - 1)), reads=[Bwf[s][which]] + BhnT[b * 4:(b + 1) * 4], writes=[psb[pg + which]])
                sigmoid_into(fsg, ps[pg][:, :], fa, fb, [psb[pg]], [Bfsg])
                op(dve, lambda: nc.vector.tensor_tensor(out=fgs, in0=ps[pg][:, :], in1=fsg, op=ALU.mult), reads=[psb[pg], Bfsg], writes=[Bfgs])
                op(dve, lambda j=j: nc.vector.tensor_tensor(out=ffT[fi][:, j, :], in0=ps[pg + 1][:, :], in1=fgs, op=ALU.mult),
                   reads=[psb[pg + 1], Bfgs], writes=[Bff[fi]])
            for tt in range(4):
                t = b * 4 + tt
                po = 4 + (ito % 2) * 2
                ito += 1
                for half in range(2):
                    for j in range(nj):
                        op(pe, lambda j=j, half=half: nc.tensor.matmul(
                            ps[po + half][:, :], lhsT=ffT[fi][:, j, tt * 128:(tt + 1) * 128], rhs=wd_[s][:, j, half * 512:(half + 1) * 512],
                            start=(j == 0), stop=(j == nj - 1)), reads=[Bff[fi], Bwf[s][2]], writes=[psb[po + half]])
                    op(dve, lambda half=half: nc.vector.tensor_tensor(out=h2[:, t, half * 512:(half + 1) * 512], in0=ps[po + half][:, :],
                                                                      in1=h2[:, t, half * 512:(half + 1) * 512], op=ALU.add),
                       reads=[psb[po + half], Bh2[t]], writes=[Bh2[t]])
    for t in range(16):
        i = t % 2
        op(act, lambda: nc.scalar.activation(out=junk[:], in_=h2[:, t, :], func=AF.Square, accum_out=ssA[:, 16 + t:17 + t]),
           reads=[Bh2[t]], writes=[Bss])
        r = rstd(ssA[:, 16 + t:17 + t], D)
        op(dve, lambda: nc.vector.scalar_tensor_tensor(out=ot[i], in0=h2[:, t, :], scalar=r, in1=fg[:, :], op0=ALU.mult, op1=ALU.mult),
           reads=[Bh2[t], Bss, B_c], writes=[Bot[i]])
        dma(qs, out_d[t * 128:(t + 1) * 128, :], ot[i], reads=[Bot[i]])
    for q in (qs, qg):
        for t_ in q.last:
            k.sp.wait_tok(t_)
    k.barrier()
    es.close()
    return nc


_CACHE = {}


def _host_inputs(x, meta_tokens, norm1_g, w_in, conv_w, conv_b, m_gate_b, g_a2, g_a2_b, m_head_g, g_head_g,
                 w_branch_m, w_branch_g, w_out, norm2_g, w_ff_gate, w_ff_up, w_ff_down, final_g):
    f = np.float32
    x2 = np.asarray(x, f)[0]
    pre0 = np.zeros((128, D), f)
    pre0[112:128] = np.asarray(meta_tokens, f)

    def pk(v):
        return np.ascontiguousarray(np.asarray(v, f).reshape(KT, 128).T)
    ident = np.eye(128, dtype=f)
    idx = np.arange(128)
    cmask = ((idx[:, None] // 64 == idx[None, :] // 64) & (idx[:, None] <= idx[None, :])).astype(f)
    rst = np.ones((128, 512), f)
    rst[:, ::64] = 0.0
    selh = np.zeros((4, 4, 128), f)
    for h in range(4):
        selh[h, h, :] = 1.0
    selh = selh.reshape(4, 512)
    cw = np.asarray(conv_w, f)[0]
    cw_p = np.ascontiguousarray(cw.reshape(4, KT, 128).transpose(2, 1, 0).reshape(128, 32))
    mgb = np.ascontiguousarray(np.asarray(m_gate_b, f)[0].T)
    a2b = np.ascontiguousarray(np.asarray(g_a2_b, f)[0].reshape(4, 128).T)
    hgrow = np.concatenate([np.asarray(m_head_g, f)[0].reshape(-1), np.asarray(g_head_g, f)[0].reshape(-1)])
    hg = np.ascontiguousarray(np.broadcast_to(hgrow[None, :], (128, 2048)))
    fgb = np.ascontiguousarray(np.broadcast_to(np.asarray(final_g, f)[None, :], (128, 1024)))
    shared = dict(
        ident=ident, cmask=cmask, rst=rst, selh=selh, g1=pk(np.asarray(norm1_g)[0]), g2=pk(np.asarray(norm2_g)[0]),
        cw=cw_p, cb=pk(np.asarray(conv_b)[0]), mgb=mgb, a2b=a2b, hg=hg, fg=fgb,
        w_in=np.ascontiguousarray(np.asarray(w_in, f)[0]), g_a2=np.ascontiguousarray(np.asarray(g_a2, f)[0]),
        wbm=np.ascontiguousarray(np.asarray(w_branch_m, f)[0]), wbg=np.ascontiguousarray(np.asarray(w_branch_g, f)[0]),
        wout=np.ascontiguousarray(np.asarray(w_out, f)[0]), wfg=np.ascontiguousarray(np.asarray(w_ff_gate, f)[0]),
        wfu=np.ascontiguousarray(np.asarray(w_ff_up, f)[0]), wfd=np.ascontiguousarray(np.asarray(w_ff_down, f)[0]),
    )
    maps = []
    for c in range(NCORES):
        if c == 0:
            xin = np.concatenate([pre0, x2[0:TL]], axis=0)
            valid = (np.arange(128) >= 112).astype(f)
        else:
            xin = x2[c * TL - 128:(c + 1) * TL]
            valid = np.zeros(128, f)
        maskrow = np.ascontiguousarray(np.broadcast_to(valid[None, :], (128, 128)))
        negrow = np.ascontiguousarray(np.broadcast_to(((valid - 1.0) * 1e30)[None, :], (128, 128))).astype(f)
        selv = (np.arange(8) < c).astype(f)
        sel = np.ascontiguousarray(np.broadcast_to(selv[None, :], (128, 8)))
        onem = np.ascontiguousarray(1.0 - sel).astype(f)
        m = dict(shared)
        m.update(xin=np.ascontiguousarray(xin, dtype=f), maskrow=maskrow, negrow=negrow, sel=sel, onem=onem)
        maps.append(m)
    return maps


def kernel(**inputs):
    maps = _host_inputs(**inputs)
    if MODE == "fused":
        if "nc" not in _CACHE:
            _CACHE["nc"] = build_program("fused")
        res = run_bass_kernel_spmd(_CACHE["nc"], maps, core_ids=list(range(NCORES)))
    else:
        if "ncA" not in _CACHE:
            _CACHE["ncA"] = build_program("A")
            _CACHE["ncB"] = build_program("B")
        drop = ("fg", "wbm", "wbg", "wout", "wfg", "wfu", "wfd", "g2", "sel", "onem")
        mapsA = [{k_: v for k_, v in m.items() if k_ not in drop} for m in maps]
        resA = run_bass_kernel_spmd(_CACHE["ncA"], mapsA, core_ids=list(range(NCORES)))
        gath = np.ascontiguousarray(np.concatenate([resA.results[c]["pub"] for c in range(NCORES)], axis=0))
        for m in maps:
            m["gath"] = gath
        res = run_bass_kernel_spmd(_CACHE["ncB"], maps, core_ids=list(range(NCORES)))
    _CACHE["last"] = res
    out = np.concatenate([res.results[c]["out"] for c in range(NCORES)], axis=0)
    return out[None].astype(np.float32)
```

```python
import math
from contextlib import ExitStack

import numpy as np
import concourse.bass as bass
import concourse.mybir as mybir
from concourse.bass_utils import run_bass_kernel_spmd

F32 = mybir.dt.float32
BF16 = mybir.dt.bfloat16
AF = mybir.ActivationFunctionType
ALU = mybir.AluOpType
AX = mybir.AxisListType

NCORES = 8
D = 1024
KT = 8
TL = 2048
NT = 17
TT = NT * 128
NCH = 2 * NT
DFF = 2816
NPROJ = 8216
EPS = 1e-6
BLOCKS = [(0, 4), (4, 4), (8, 4), (12, 4), (16, 1)]
LN_C = -0.5 * math.log(128.0)

DEBUG = {}
MODE = "split"


class Tok:
    __slots__ = ("sem", "val", "eng", "key")

    def __init__(self, sem, val, eng, key):
        self.sem, self.val, self.eng, self.key = sem, val, eng, key


class Buf:
    __slots__ = ("name", "w", "r", "excl")

    def __init__(self, name="", excl=False):
        self.name = name
        self.w = None
        self.r = {}
        self.excl = excl


class Eng:
    def __init__(self, e, name, sems, is_pe=False):
        self.e = e
        self.name = name
        self.sems = sems
        self.ep = 0
        self.cnt = 0
        self.is_pe = is_pe
        self.waited = {}
        self.last = None

    def wait_tok(self, tok):
        if tok is None:
            return
        if tok.eng is self and self.is_pe:
            return
        if self.waited.get(tok.key, 0) >= tok.val:
            return
        self.e.wait_ge(tok.sem, tok.val)
        self.waited[tok.key] = tok.val

    def bump(self, ins):
        sem = self.sems[self.ep]
        ins.then_inc(sem, 1)
        self.cnt += 1
        t = Tok(sem, self.cnt, self, (self.name, self.ep))
        self.last = t
        return t

    def new_epoch(self):
        self.ep += 1
        self.cnt = 0


class DmaQ:
    def __init__(self, eng, sems, name):
        self.eng = eng
        self.sems = sems
        self.cnt = [0] * len(sems)
        self.last = [None] * len(sems)
        self.i = 0
        self.name = name

    def issue(self, out, in_, **kw):
        i = self.i
        self.i = (self.i + 1) % len(self.sems)
        self.eng.wait_tok(self.last[i])
        ins = self.eng.e.dma_start(out=out, in_=in_, **kw)
        ins.then_inc(self.sems[i], 16)
        self.cnt[i] += 16
        t = Tok(self.sems[i], self.cnt[i], None, (self.name, i))
        self.last[i] = t
        return t


class K:
    def op(self, eng, fn, reads=(), writes=()):
        ex = [b for b in reads if b.excl]
        if ex:
            reads = [b for b in reads if not b.excl]
            writes = list(writes) + [b for b in ex if b not in writes]
        for b in reads:
            eng.wait_tok(b.w)
        for b in writes:
            eng.wait_tok(b.w)
            for t in b.r.values():
                eng.wait_tok(t)
        tok = eng.bump(fn())
        for b in reads:
            b.r[tok.key] = tok
        for b in writes:
            b.w = tok
            b.r = {}
        return tok

    def dma(self, q, out, in_, reads=(), writes=(), **kw):
        for b in reads:
            q.eng.wait_tok(b.w)
        for b in writes:
            q.eng.wait_tok(b.w)
            for t in b.r.values():
                q.eng.wait_tok(t)
        tok = q.issue(out, in_, **kw)
        for b in reads:
            b.r[tok.key] = tok
        for b in writes:
            b.w = tok
            b.r = {}
        return tok

    def barrier(self):
        toks = []
        for e in (self.pe, self.act, self.dve, self.pool):
            if e.last is not None:
                toks.append(e.last)
        for q in (self.qs, self.qg):
            for t in q.last:
                if t is not None:
                    toks.append(t)
        for e in (self.pe, self.act, self.dve, self.pool, self.sp):
            for t in toks:
                if t.eng is e:
                    continue
                e.wait_tok(t)


def build_program(mode="fused"):
    nc = bass.Bass("TRN2", target_bir_lowering=False)
    k = K()
    k.nc = nc
    es = ExitStack()
    k.es = es

    A_DROP = ("fg", "wbm", "wbg", "wout", "wfg", "wfu", "wfd", "g2", "sel", "onem")

    def din(name, shape, dt=F32):
        if mode == "A" and name in A_DROP:
            return None
        return nc.dram_tensor(name, list(shape), dt, kind="ExternalInput").ap()

    xin = din("xin", [TT, D])
    maskrow_d = din("maskrow", [128, 128])
    negrow_d = din("negrow", [128, 128])
    sel_d = din("sel", [128, 8])
    onem_d = din("onem", [128, 8])
    ident_d = din("ident", [128, 128])
    cmask_d = din("cmask", [128, 128])
    rst_d = din("rst", [128, 512])
    selh_d = din("selh", [4, 512])
    g1_d = din("g1", [128, 8])
    g2_d = din("g2", [128, 8])
    cw_d = din("cw", [128, 32])
    cb_d = din("cb", [128, 8])
    mgb_d = din("mgb", [4, 2])
    a2b_d = din("a2b", [128, 4])
    hg_d = din("hg", [128, 2048])
    fg_d = din("fg", [128, 1024])
    w_in = din("w_in", [D, NPROJ])
    g_a2 = din("g_a2", [16, 512])
    wbm = din("wbm", [D, D])
    wbg = din("wbg", [D, D])
    wout = din("wout", [D, D])
    wfg = din("wfg", [D, DFF])
    wfu = din("wfu", [D, DFF])
    wfd = din("wfd", [DFF, D])
    if mode != "A":
        out_d = nc.dram_tensor("out", [TL, D], F32, kind="ExternalOutput").ap()
    dbg_d = {}
    for name, shape in DEBUG.items():
        dbg_d[name] = nc.dram_tensor("dbg_" + name, list(shape), F32, kind="ExternalOutput").ap()

    if mode == "fused":
        pub_t = nc.dram_tensor("pub", [128, 8 * 258], F32)
        gath_t = nc.dram_tensor("gath", [8 * 128, 8 * 258], F32)
        pub = pub_t.ap()
        gath = gath_t.ap()
    elif mode == "A":
        pub = nc.dram_tensor("pub", [128, 8 * 258], F32, kind="ExternalOutput").ap()
        gath = None
    else:
        pub = None
        gath = nc.dram_tensor("gath", [8 * 128, 8 * 258], F32, kind="ExternalInput").ap()

    def sem(name):
        return es.enter_context(nc.semaphore(name))

    NEP = 3
    k.pe = Eng(nc.tensor, "pe", [sem(f"pe{i}") for i in range(NEP)], is_pe=True)
    k.act = Eng(nc.scalar, "act", [sem(f"act{i}") for i in range(NEP)])
    k.dve = Eng(nc.vector, "dve", [sem(f"dve{i}") for i in range(NEP)])
    k.pool = Eng(nc.gpsimd, "pool", [sem(f"pool{i}") for i in range(NEP)])
    k.sp = Eng(nc.sync, "sp", [sem("sp0")])
    k.qs = DmaQ(k.sp, [sem(f"qs{i}") for i in range(8)], "qs")
    k.qg = DmaQ(k.pool, [sem(f"qg{i}") for i in range(8)], "qg")
    ccsem = sem("cc")
    pe, act, dve, pool, qs, qg = k.pe, k.act, k.dve, k.pool, k.qs, k.qg
    op, dma = k.op, k.dma

    def sb(name, shape, dt=F32):
        return es.enter_context(nc.sbuf_tensor(name, list(shape), dt))

    cst = sb("cst", [128, 8])
    maskrow = sb("maskrow_s", [128, 128])
    negrow = sb("negrow_s", [128, 128])
    sel = sb("sel_s", [128, 8])
    onem = sb("onem_s", [128, 8])
    identb = sb("identb", [128, 128], BF16)
    identf = sb("identf", [128, 128])
    cmask = sb("cmask_s", [128, 128])
    rst = sb("rst_s", [128, 512])
    selh = sb("selh_s", [4, 512])
    g1 = sb("g1_s", [128, 8])
    g2 = sb("g2_s", [128, 8])
    cw = sb("cw_s", [128, 32])
    cb = sb("cb_s", [128, 8])
    mgb = sb("mgb_s", [4, 2])
    nbf = sb("nbf_s", [4, 1])
    a2b = sb("a2b_s", [128, 4])
    na2b = sb("na2b_s", [128, 4])
    hg = sb("hg_s", [128, 2048])
    fg = sb("fg_s", [128, 1024])
    etok = sb("etok", [128, NT, 8])
    abc = sb("abc", [128, 4, 36])
    ssA = sb("ssA", [128, 64])
    rsA = sb("rsA", [128, 64])
    tmpA = sb("tmpA", [128, 64])
    junk = sb("junk", [128, 1024], BF16)
    wa2 = sb("wa2", [16, 512], BF16)
    wif = sb("wif", [128, 8, 8], BF16)

    R1 = sb("R1", [128, KT * TT], BF16)
    R2 = sb("R2", [128, 32768], BF16)
    R3 = sb("R3", [128, 20480])

    xnT = R1[:, :].rearrange("p (k t) -> p k t", k=KT)
    hnT = R1[:, 0:KT * TL].rearrange("p (k t) -> p k t", k=KT)
    yT = R2[:, :].rearrange("p (k t) -> p k t", k=16)
    h2 = R2[:, :].bitcast(F32).rearrange("p (t d) -> p t d", t=16)

    class Arena:
        def __init__(self):
            self.off = 0

        def f32(self, n):
            a = R3[:, self.off:self.off + n]
            self.off += n
            assert self.off <= 20480, self.off
            return a

        def bf16(self, n):
            n2 = (n + 1) // 2
            a = R3[:, self.off:self.off + n2].bitcast(BF16)
            self.off += n2
            assert self.off <= 20480, self.off
            return a[:, 0:n]

    ps = [es.enter_context(nc.psum_tensor(f"ps{i}", [128, 512], F32)) for i in range(8)]
    psb = [Buf(f"ps{i}", excl=True) for i in range(8)]

    B_c = Buf("consts")
    for dst, src in ((maskrow, maskrow_d), (negrow, negrow_d), (sel, sel_d), (onem, onem_d),
                     (identf, ident_d), (cmask, cmask_d), (rst, rst_d), (selh, selh_d),
                     (g1, g1_d), (g2, g2_d), (cw, cw_d), (cb, cb_d), (mgb, mgb_d),
                     (a2b, a2b_d), (hg, hg_d), (fg, fg_d)):
        if src is not None:
            dma(qs, dst[:], src[:, :], writes=[B_c])
    dma(qg, identb[:], ident_d[:, :], writes=[B_c])
    dma(qg, wa2[:], g_a2[:, :], writes=[B_c])
    dma(qg, wif[:], w_in[:, 2048:2056].rearrange("(k p) c -> p k c", p=128), writes=[B_c])
    B_cst = Buf("cst")
    for col, val in enumerate((EPS, 1.0, -LN_C, LN_C, 0.0, -0.5)):
        op(pool, lambda col=col, val=val: nc.gpsimd.memset(cst[:, col:col + 1], float(val)), writes=[B_cst])
    c_eps, c_one, c_lnsq, c_lnc, c_zero = (cst[:, i:i + 1] for i in range(5))
    k.barrier()
    B_small = Buf("small")
    op(dve, lambda: nc.vector.tensor_scalar(nbf[:], mgb[:, 1:2], -1.0, None, ALU.mult), reads=[B_c], writes=[B_small])
    op(dve, lambda: nc.vector.tensor_scalar(na2b[:], a2b[:], -1.0, None, ALU.mult), reads=[B_c], writes=[B_small])

    Bss = Buf("ss")
    rs_idx = [0]

    def rstd(ss_ap, n, npart=128, width=1):
        i = rs_idx[0] % (64 // width)
        rs_idx[0] += 1
        t1 = tmpA[0:npart, i * width:(i + 1) * width]
        o = rsA[0:npart, i * width:(i + 1) * width]
        op(act, lambda: nc.scalar.activation(out=t1, in_=ss_ap, func=AF.Ln, bias=c_eps[0:npart], scale=1.0 / n),
           reads=[Bss, B_cst], writes=[Bss])
        op(act, lambda: nc.scalar.activation(out=o, in_=t1, func=AF.Exp, scale=-0.5), reads=[Bss], writes=[Bss])
        return o

    def sigmoid_into(out_ap, in_ap, ta, tb, rbufs, wbufs, neg_bias=None):
        if neg_bias is None:
            op(act, lambda: nc.scalar.activation(out=ta, in_=in_ap, func=AF.Exp, scale=-1.0), reads=rbufs, writes=wbufs)
        else:
            op(act, lambda: nc.scalar.activation(out=ta, in_=in_ap, func=AF.Exp, scale=-1.0, bias=neg_bias),
               reads=rbufs, writes=wbufs)
        op(act, lambda: nc.scalar.activation(out=tb, in_=ta, func=AF.Ln, bias=c_one[0:ta.shape[0]], scale=1.0),
           reads=list(wbufs) + [B_cst], writes=wbufs)
        op(act, lambda: nc.scalar.activation(out=out_ap, in_=tb, func=AF.Exp, scale=-1.0), reads=wbufs, writes=wbufs)

    def dbg(name, src_ap, rbufs):
        if name in dbg_d:
            dma(qs, dbg_d[name], src_ap, reads=rbufs)

    ar = Arena()
    xt = [ar.f32(1024) for _ in range(2)]
    xnb = [ar.bf16(1024) for _ in range(2)]
    Bxt = [Buf("xt0"), Buf("xt1")]
    Bxnb = [Buf("xnb0"), Buf("xnb1")]
    BxnT = [Buf(f"xnT{t}") for t in range(NT)]
    psT = ps[7][:, :].bitcast(BF16)
    BpsT = psb[7]
    def P0a(t):
        i = t % 2
        dma(qs, xt[i], xin[t * 128:(t + 1) * 128, :], writes=[Bxt[i]])
        op(act, lambda: nc.scalar.activation(out=junk[:], in_=xt[i], func=AF.Square, accum_out=ssA[:, t:t + 1]),
           reads=[Bxt[i]], writes=[Bss])
        r = rstd(ssA[:, t:t + 1], D)
        op(act, lambda: nc.scalar.activation(out=xnb[i], in_=xt[i], func=AF.Copy, scale=r),
           reads=[Bxt[i], Bss], writes=[Bxnb[i]])

    def P0b(t):
        i = t % 2
        for kk in range(KT):
            op(pe, lambda kk=kk: nc.tensor.transpose(psT[:, kk * 128:(kk + 1) * 128], xnb[i][:, kk * 128:(kk + 1) * 128], identb[:]),
               reads=[Bxnb[i], B_c], writes=[BpsT])
        op(dve, lambda: nc.vector.tensor_tensor(
            out=xnT[:, :, t * 128:(t + 1) * 128], in0=psT.rearrange("p (k t) -> p k t", k=KT),
            in1=g1[:, :].unsqueeze(2).to_broadcast([128, KT, 128]), op=ALU.mult),
           reads=[BpsT, B_c], writes=[BxnT[t]])

    P0a(0)
    for t in range(NT):
        if t + 1 < NT:
            P0a(t + 1)
        P0b(t)
    if "xnT" in dbg_d:
        xf = ar.f32(TT)
        Bxf = Buf()
        op(dve, lambda: nc.vector.tensor_copy(out=xf, in_=xnT[:, 0, :]), reads=BxnT, writes=[Bxf])
        dbg("xnT", xf, [Bxf])
    k.barrier()

    ar = Arena()
    li = ar.f32(TT)[0:4]
    t1g = ar.f32(TT)[0:4]
    t2g = ar.f32(TT)[0:4]
    nbg = ar.f32(TT)[0:4]
    rg = ar.f32(TT)[0:4]
    eg = ar.f32(TT)[0:4]
    flg = ar.f32(TT)[0:4]
    smallg = ar.f32(256)[0:4]
    Rg = smallg[:, 0:34]
    gg_ = smallg[:, 34:68]
    marr = smallg[:, 68:103]
    Mg = smallg[:, 103:137]
    dg = smallg[:, 137:171]
    cat = smallg[:, 171:207]
    sumnb = smallg[:, 207:208]
    gaT = sb("gaT", [16, TT], BF16)
    Bg = Buf("gates")
    BgaT = Buf("gaT")
    for (t0, ntl) in BLOCKS:
        n = ntl * 128
        c0 = t0 * 128
        for j, (pst, pb) in enumerate(((ps[0], psb[0]), (ps[1], psb[1]))):
            for kk in range(KT):
                op(pe, lambda kk=kk, j=j, pst=pst: nc.tensor.matmul(
                    pst[0:4, 0:n], lhsT=wif[:, kk, j * 4:(j + 1) * 4], rhs=xnT[:, kk, c0:c0 + n],
                    start=(kk == 0), stop=(kk == KT - 1)), reads=[B_c] + BxnT[t0:t0 + ntl], writes=[pb])
        op(act, lambda: nc.scalar.activation(out=li[:, c0:c0 + n], in_=ps[0][0:4, 0:n], func=AF.Identity,
                                             bias=mgb[:, 0:1], scale=1.0), reads=[psb[0], B_c], writes=[Bg])
        op(act, lambda: nc.scalar.activation(out=t1g[:, c0:c0 + n], in_=ps[1][0:4, 0:n], func=AF.Exp,
                                             bias=nbf[:, 0:1], scale=-1.0), reads=[psb[1], B_small], writes=[Bg])
        op(act, lambda: nc.scalar.activation(out=t2g[:, c0:c0 + n], in_=t1g[:, c0:c0 + n], func=AF.Ln,
                                             bias=c_one[0:4], scale=1.0), reads=[Bg, B_cst], writes=[Bg])
    op(dve, lambda: nc.vector.tensor_tensor(out=t2g[:, 0:128], in0=t2g[:, 0:128], in1=maskrow[0:4, :], op=ALU.mult),
       reads=[Bg, B_c], writes=[Bg])
    for (t0, ntl) in BLOCKS:
        n = ntl * 128
        c0 = t0 * 128
        op(dve, lambda: nc.vector.tensor_tensor_scan(out=nbg[:, c0:c0 + n], data0=rst[0:4, 0:n], data1=t2g[:, c0:c0 + n],
                                                     initial=0.0, op0=ALU.mult, op1=ALU.add), reads=[Bg, B_c], writes=[Bg])
    op(dve, lambda: nc.vector.tensor_tensor(out=rg, in0=li, in1=nbg, op=ALU.add), reads=[Bg], writes=[Bg])
    op(dve, lambda: nc.vector.tensor_tensor(out=rg[:, 0:128], in0=rg[:, 0:128], in1=maskrow[0:4, :], op=ALU.mult),
       reads=[Bg, B_c], writes=[Bg])
    op(dve, lambda: nc.vector.tensor_tensor(out=rg[:, 0:128], in0=rg[:, 0:128], in1=negrow[0:4, :], op=ALU.add),
       reads=[Bg, B_c], writes=[Bg])
    rg3 = rg.rearrange("p (c l) -> p c l", l=64)
    nb3 = nbg.rearrange("p (c l) -> p c l", l=64)
    op(dve, lambda: nc.vector.tensor_reduce(out=Rg, in_=rg3, axis=AX.X, op=ALU.max), reads=[Bg], writes=[Bg])
    op(dve, lambda: nc.vector.tensor_scalar(gg_, nb3[:, :, 63], -1.0, None, ALU.mult), reads=[Bg], writes=[Bg])
    op(dve, lambda: nc.vector.memset(marr[:, 0:1], 0.0), writes=[Bg])
    op(dve, lambda: nc.vector.tensor_tensor_scan(out=marr[:, 1:35], data0=Rg, data1=gg_, initial=0.0,
                                                 op0=ALU.max, op1=ALU.add), reads=[Bg], writes=[Bg])
    op(dve, lambda: nc.vector.tensor_tensor(out=Mg, in0=marr[:, 0:34], in1=Rg, op=ALU.max), reads=[Bg], writes=[Bg])
    op(dve, lambda: nc.vector.tensor_tensor(out=dg, in0=marr[:, 0:34], in1=Mg, op=ALU.subtract), reads=[Bg], writes=[Bg])
    op(act, lambda: nc.scalar.activation(out=cat[:, 0:34], in_=dg, func=AF.Exp), reads=[Bg], writes=[Bg])
    op(act, lambda: nc.scalar.activation(out=cat[:, 34:35], in_=marr[:, 34:35], func=AF.Exp), reads=[Bg], writes=[Bg])
    op(dve, lambda: nc.vector.tensor_reduce(out=sumnb, in_=nb3[:, :, 63], axis=AX.X, op=ALU.add), reads=[Bg], writes=[Bg])
    op(act, lambda: nc.scalar.activation(out=cat[:, 35:36], in_=sumnb, func=AF.Exp, scale=-1.0), reads=[Bg], writes=[Bg])
    Mb = Mg.unsqueeze(2).to_broadcast([4, NCH, 64])
    op(dve, lambda: nc.vector.tensor_tensor(out=rg3, in0=rg3, in1=Mb, op=ALU.subtract), reads=[Bg], writes=[Bg])
    op(act, lambda: nc.scalar.activation(out=eg, in_=rg, func=AF.Exp), reads=[Bg], writes=[Bg])
    op(dve, lambda: nc.vector.tensor_tensor(out=nb3, in0=nb3, in1=Mb, op=ALU.subtract), reads=[Bg], writes=[Bg])
    op(act, lambda: nc.scalar.activation(out=flg, in_=nbg, func=AF.Exp, bias=c_lnsq[0:4], scale=1.0),
       reads=[Bg, B_cst], writes=[Bg])
    for h in range(4):
        op(pe, lambda h=h: nc.tensor.matmul(ps[0][:, h * 36:(h + 1) * 36], lhsT=selh[0:4, h * 128:(h + 1) * 128],
                                            rhs=cat, start=True, stop=True), reads=[Bg, B_c], writes=[psb[0]])
    Babc = Buf("abc")
    op(dve, lambda: nc.vector.tensor_copy(out=abc[:, :, :], in_=ps[0][:, 0:144].rearrange("p (h c) -> p h c", h=4)),
       reads=[psb[0]], writes=[Babc])
    for t in range(NT):
        op(pe, lambda t=t: nc.tensor.matmul(ps[1][:, t * 8:t * 8 + 4], lhsT=eg[:, t * 128:(t + 1) * 128],
                                            rhs=identf[0:4, 0:4], start=True, stop=True), reads=[Bg, B_c], writes=[psb[1]])
        op(pe, lambda t=t: nc.tensor.matmul(ps[1][:, t * 8 + 4:t * 8 + 8], lhsT=flg[:, t * 128:(t + 1) * 128],
                                            rhs=identf[0:4, 0:4], start=True, stop=True), reads=[Bg, B_c], writes=[psb[1]])
    Betok = Buf("etok")
    op(dve, lambda: nc.vector.tensor_copy(out=etok[:, :, :], in_=ps[1][:, 0:NT * 8].rearrange("p (t c) -> p t c", c=8)),
       reads=[psb[1]], writes=[Betok])
    w16 = ar.bf16(KT * 16).rearrange("p (k c) -> p k c", k=KT)
    Bw16 = Buf("w16")
    dma(qg, w16, w_in[:, 5128:5144].rearrange("(k p) c -> p k c", p=128), writes=[Bw16])
    for (t0, ntl) in BLOCKS:
        n = ntl * 128
        c0 = t0 * 128
        for kk in range(KT):
            op(pe, lambda kk=kk: nc.tensor.matmul(ps[2][0:16, 0:n], lhsT=w16[:, kk, :], rhs=xnT[:, kk, c0:c0 + n],
                                                  start=(kk == 0), stop=(kk == KT - 1)),
               reads=[Bw16] + BxnT[t0:t0 + ntl], writes=[psb[2]])
        op(act, lambda: nc.scalar.activation(out=gaT[:, c0:c0 + n], in_=ps[2][0:16, 0:n], func=AF.Copy),
           reads=[psb[2]], writes=[BgaT])
    dbg("etok", etok[:, :, :].rearrange("p t c -> p (t c)"), [Betok])
    dbg("abc", abc[:, :, :].rearrange("p h c -> p (h c)"), [Babc])
    k.barrier()

    ar = Arena()
    wbuf = [ar.bf16(KT * 768).rearrange("p (k c) -> p k c", k=KT) for _ in range(2)]
    Bw = [[Buf(f"w{s_}_{j}") for j in range(4)] for s_ in range(2)]
    kpre = ar.f32(516)
    qpre = ar.f32(516)
    kc = ar.f32(512)
    qc = ar.f32(512)
    ta = ar.f32(512)
    tb = ar.f32(512)
    sgt = ar.f32(512)
    nbt = ar.f32(512)
    ekt = ar.f32(512)
    kTb = [ar.bf16(512) for _ in range(2)]
    qTb = [ar.bf16(512) for _ in range(2)]
    ktok = ar.bf16(8 * 128).rearrange("p (t c) -> p t c", t=8)
    vtok = ar.bf16(8 * 258).rearrange("p (t c) -> p t c", t=8)
    Gt = ar.f32(8 * 256).rearrange("p (t c) -> p t c", t=8)
    ga_ = ar.f32(256)
    gb_ = ar.f32(256)
    gc_ = ar.f32(256)
    go_ = ar.f32(256)
    attb = ar.bf16(128)
    Sbf = [ar.bf16(258) for _ in range(4)]
    Xs = [ar.f32(258) for _ in range(2)]
    ybuf = [ar.bf16(256) for _ in range(2)]
    gsb = ar.f32(7 * 258).rearrange("p (r c) -> p r c", r=7)
    pubb = ar.f32(258)
    avec2 = [ar.f32(40), ar.f32(40)]
    nbend2 = [ar.f32(40), ar.f32(40)]
    Xh = ar.f32(258)
    BXh = Buf()
    decp = ar.f32(8)
    tiny = ar.f32(16)
    Bkpre, Bqpre, Bkc, Bqc = Buf(), Buf(), Buf(), Buf()
    Bta, Bnb, Bek = Buf(), Buf(), Buf()
    BkT = [Buf(), Buf()]
    BqT = [Buf(), Buf()]
    Bktok = [Buf() for _ in range(8)]
    Bvtok = [Buf() for _ in range(8)]
    BG = [Buf() for _ in range(8)]
    Bgt = Buf()
    Batt = Buf()
    BS = [Buf() for _ in range(4)]
    BX = [Buf(), Buf()]
    By = [Buf(), Buf()]
    Bgs, Bpub, Btiny = Buf(), Buf(), Buf()
    Bav2 = [[Buf() for _ in range(len(BLOCKS) + 1)] for _ in range(2)]
    ByT = [Buf(f"yT{t}") for t in range(16)]

    def load_unit_weights(u, phase, slot):
        m = u < 4
        h = u % 4
        if m:
            cq, ck, cv, co = h * 128, 512 + h * 128, 1024 + h * 256, 2056 + h * 256
        else:
            cq, ck, cv, co = 3080 + h * 128, 3592 + h * 128, 4104 + h * 256, 5144 + h * 256
        w = wbuf[slot]

        def ld(dst0, c, n, bi_):
            dma(qg, w[:, :, dst0:dst0 + n], w_in[:, c:c + n].rearrange("(k p) c -> p k c", p=128), writes=[Bw[slot][bi_]])
        ld(128, ck, 128, 1)
        ld(256, cv, 256, 2)
        if phase == 2:
            ld(0, cq, 128, 0)
            ld(512, co, 256, 3)

    psQ, psK, psV, psA, psO = ps[0], ps[1], ps[2], ps[3], ps[6]
    BpsQ, BpsK, BpsV, BpsA, BpsO = psb[0], psb[1], psb[2], psb[3], psb[6]
    psP = [ps[4], ps[5]]
    BpsP = [psb[4], psb[5]]
    TILES = [(bi, tt) for bi, (t0, ntl) in enumerate(BLOCKS) for tt in range(ntl)]

    def make_unit(u, phase, slot):
        m = u < 4
        h = u % 4
        NV = 257 if m else 256
        w = wbuf[slot]
        full = phase == 2
        st = {"cur": (Xh, BXh), "xo": 0, "sidx": {}}
        avec = avec2[u % 2]
        nbend = nbend2[u % 2]
        Bav = Bav2[u % 2]

        def gpar(bi):
            return (5 * u + bi) % 2

        def preamble():
          op(dve, lambda: nc.vector.memset(Xh[:, 0:NV], 0.0), writes=[BXh])
          if full:
              dma(qs, gsb[:, :, :], gath[0:896, u * 258:(u + 1) * 258].rearrange("(r p) c -> p r c", p=128), writes=[Bgs])
              op(dve, lambda: nc.vector.tensor_tensor(out=decp[:, 0:7], in0=gsb[:, :, 257], in1=sel[:, 0:7], op=ALU.mult),
                 reads=[Bgs, B_c], writes=[Btiny])
              op(dve, lambda: nc.vector.tensor_tensor(out=decp[:, 0:7], in0=decp[:, 0:7], in1=onem[:, 0:7], op=ALU.add),
                 reads=[Btiny, B_c], writes=[Btiny])
              op(dve, lambda: nc.vector.tensor_tensor(out=gsb[:, 0:7, 0:NV], in0=gsb[:, 0:7, 0:NV],
                                                      in1=sel[:, 0:7].unsqueeze(2).to_broadcast([128, 7, NV]), op=ALU.mult),
                 reads=[B_c, Bgs], writes=[Bgs])
              for j in range(7):
                  op(dve, lambda j=j: nc.vector.scalar_tensor_tensor(out=Xh[:, 0:NV], in0=Xh[:, 0:NV],
                                                                     scalar=decp[:, j:j + 1], in1=gsb[:, j, 0:NV],
                                                                     op0=ALU.mult, op1=ALU.add),
                     reads=[Bgs, Btiny], writes=[BXh])
          if m:
              op(pool, lambda: nc.gpsimd.memset(kpre[:, 0:3], 0.0), writes=[Bkpre])
              if full:
                  op(pool, lambda: nc.gpsimd.memset(qpre[:, 0:3], 0.0), writes=[Bqpre])
          else:
              op(pool, lambda: nc.gpsimd.memset(avec[:, 0:1], 1.0), writes=[Bav[0]])

        def S0_stages(bi):
            t0, ntl = BLOCKS[bi]
            par = gpar(bi)
            n = ntl * 128
            c0 = t0 * 128
            rx = BxnT[t0:t0 + ntl]
            kT_, qT_ = kTb[par], qTb[par]

            def proj_k():
                for kk in range(KT):
                    op(pe, lambda kk=kk: nc.tensor.matmul(psK[:, 0:n], lhsT=w[:, kk, 128:256], rhs=xnT[:, kk, c0:c0 + n],
                                                          start=(kk == 0), stop=(kk == KT - 1)), reads=[Bw[slot][1]] + rx, writes=[BpsK])

            def proj_q():
                for kk in range(KT):
                    op(pe, lambda kk=kk: nc.tensor.matmul(psQ[:, 0:n], lhsT=w[:, kk, 0:128], rhs=xnT[:, kk, c0:c0 + n],
                                                          start=(kk == 0), stop=(kk == KT - 1)), reads=[Bw[slot][0]] + rx, writes=[BpsQ])

            if m:
                chains = [(psK, BpsK, kpre, Bkpre, kc, Bkc, kT_, BkT[par], 4 + h)]
                if full:
                    chains.append((psQ, BpsQ, qpre, Bqpre, qc, Bqc, qT_, BqT[par], h))

                def f1():
                    proj_k()
                    if full:
                        proj_q()
                    for (psx, Bpsx, pre, Bpre, cx, Bcx, outb, Bout, ktile) in chains:
                        op(act, lambda psx=psx, pre=pre: nc.scalar.activation(out=pre[:, 3:3 + n], in_=psx[:, 0:n], func=AF.Copy),
                           reads=[Bpsx], writes=[Bpre])

                def f2(chain):
                    for (psx, Bpsx, pre, Bpre, cx, Bcx, outb, Bout, ktile) in [chain]:
                        wv = cw[:, ktile * 4:(ktile + 1) * 4]
                        op(dve, lambda cx=cx, pre=pre, wv=wv, ktile=ktile: nc.vector.tensor_scalar(
                            cx[:, 0:n], pre[:, 0:n], wv[:, 0:1], cb[:, ktile:ktile + 1], ALU.mult, ALU.add),
                           reads=[Bpre, B_c], writes=[Bcx])
                        for i in range(1, 4):
                            op(dve, lambda i=i, cx=cx, pre=pre, wv=wv: nc.vector.scalar_tensor_tensor(
                                out=cx[:, 0:n], in0=pre[:, i:i + n], scalar=wv[:, i:i + 1], in1=cx[:, 0:n], op0=ALU.mult, op1=ALU.add),
                               reads=[Bpre, B_c, Bcx], writes=[Bcx])
                        op(pool, lambda pre=pre: nc.gpsimd.tensor_copy(out=pre[:, 0:3], in_=pre[:, n:n + 3]), reads=[Bpre], writes=[Bpre])

                def f3(chain):
                    for (psx, Bpsx, pre, Bpre, cx, Bcx, outb, Bout, ktile) in [chain]:
                        sigmoid_into(sgt[:, 0:n], cx[:, 0:n], ta[:, 0:n], tb[:, 0:n], [Bcx], [Bta])
                        op(pool, lambda outb=outb, cx=cx: nc.gpsimd.tensor_tensor(out=outb[:, 0:n], in0=cx[:, 0:n], in1=sgt[:, 0:n], op=ALU.mult),
                           reads=[Bcx, Bta], writes=[Bout])
            else:
                nch = 2 * ntl
                nb3_ = nbt[:, 0:n].rearrange("p (c l) -> p c l", l=64)

                def f1():
                    op(pe, lambda: nc.tensor.matmul(psQ[:, 0:n], lhsT=wa2[:, h * 128:(h + 1) * 128], rhs=gaT[:, c0:c0 + n],
                                                    start=True, stop=True), reads=[B_c, BgaT], writes=[BpsQ])
                    op(act, lambda: nc.scalar.activation(out=ta[:, 0:n], in_=psQ[:, 0:n], func=AF.Exp, bias=na2b[:, h:h + 1], scale=-1.0),
                       reads=[BpsQ, B_small], writes=[Bta])
                    proj_k()
                    if full:
                        proj_q()
                    op(act, lambda: nc.scalar.activation(out=tb[:, 0:n], in_=ta[:, 0:n], func=AF.Ln, bias=c_one, scale=1.0),
                       reads=[Bta, B_cst], writes=[Bta])

                def f2():
                    if bi == 0:
                        op(dve, lambda: nc.vector.tensor_tensor(out=tb[:, 0:128], in0=tb[:, 0:128], in1=maskrow[:, :], op=ALU.mult),
                           reads=[Bta, B_c], writes=[Bta])
                    op(dve, lambda: nc.vector.tensor_tensor_scan(out=nbt[:, 0:n], data0=rst[:, 0:n], data1=tb[:, 0:n], initial=0.0,
                                                                 op0=ALU.mult, op1=ALU.add), reads=[Bta, B_c], writes=[Bnb])

                def f2b():
                    op(act, lambda: nc.scalar.activation(out=avec[:, 2 * t0 + 1:2 * t0 + 1 + nch], in_=nb3_[:, :, 63], func=AF.Exp,
                                                         scale=-1.0 / 16.0), reads=[Bnb], writes=[Bav[bi + 1]])
                    if not full:
                        op(dve, lambda: nc.vector.tensor_copy(out=nbend[:, 2 * t0:2 * t0 + nch], in_=nb3_[:, :, 63]), reads=[Bnb],
                           writes=[Bav[bi + 1]])
                    op(act, lambda: nc.scalar.activation(out=ekt[:, 0:n], in_=nbt[:, 0:n], func=AF.Exp, scale=1.0 / 16.0),
                       reads=[Bnb], writes=[Bek])

                def f3():
                    if bi == 0:
                        op(dve, lambda: nc.vector.tensor_tensor(out=ekt[:, 0:128], in0=ekt[:, 0:128], in1=maskrow[:, :], op=ALU.mult),
                           reads=[Bek, B_c], writes=[Bek])
                    op(dve, lambda: nc.vector.tensor_tensor(out=kT_[:, 0:n], in0=psK[:, 0:n], in1=ekt[:, 0:n], op=ALU.mult),
                       reads=[BpsK, Bek], writes=[BkT[par]])

                def f3b():
                    if full:
                        op(act, lambda: nc.scalar.activation(out=ekt[:, 0:n], in_=nbt[:, 0:n], func=AF.Exp, bias=c_lnc, scale=-1.0 / 16.0),
                           reads=[Bnb, B_cst], writes=[Bek])
                        op(dve, lambda: nc.vector.tensor_tensor(out=qT_[:, 0:n], in0=psQ[:, 0:n], in1=ekt[:, 0:n], op=ALU.mult),
                           reads=[BpsQ, Bek], writes=[BqT[par]])
            if m:
                stages = [f1] + [(lambda c=c: f2(c)) for c in chains] + [(lambda c=c: f3(c)) for c in chains]
            else:
                stages = [f1, f2, f2b, f3, f3b]
            return stages + [lambda: S0b(bi)]

        def S0b(bi):
            t0, ntl = BLOCKS[bi]
            par = gpar(bi)
            n = ntl * 128
            kT_ = kTb[par]
            for tt in range(ntl):
                op(pe, lambda tt=tt: nc.tensor.transpose(psT[:, tt * 128:(tt + 1) * 128], kT_[:, tt * 128:(tt + 1) * 128], identb[:]),
                   reads=[BkT[par], B_c], writes=[BpsT])
            op(act, lambda: nc.scalar.activation(out=ktok[:, par * 4:par * 4 + ntl, :],
                                                 in_=psT[:, 0:n].rearrange("p (t c) -> p t c", t=ntl),
                                                 func=AF.Copy), reads=[BpsT], writes=Bktok[par * 4:par * 4 + ntl])

        def S1(bi, tt):
            t0, ntl = BLOCKS[bi]
            t = t0 + tt
            sl = gpar(bi) * 4 + tt
            outp = full and t >= 1
            ncol = 512 if outp else 256
            for kk in range(KT):
                op(pe, lambda kk=kk: nc.tensor.matmul(psV[:, 0:ncol], lhsT=xnT[:, kk, t * 128:(t + 1) * 128],
                                                      rhs=w[:, kk, 256:256 + ncol], start=(kk == 0), stop=(kk == KT - 1)),
                   reads=[Bw[slot][2], Bw[slot][3], BxnT[t]], writes=[BpsV])
            if m:
                op(act, lambda: nc.scalar.activation(out=vtok[:, sl, 0:256], in_=psV[:, 0:256], func=AF.Copy,
                                                     scale=etok[:, t, h:h + 1]), reads=[BpsV, Betok], writes=[Bvtok[sl]])
                op(pool, lambda: nc.gpsimd.tensor_copy(out=vtok[:, sl, 256:257], in_=etok[:, t, h:h + 1]),
                   reads=[Betok, Bvtok[sl]], writes=[Bvtok[sl]])
            else:
                op(act, lambda: nc.scalar.activation(out=vtok[:, sl, 0:256], in_=psV[:, 0:256], func=AF.Copy),
                   reads=[BpsV], writes=[Bvtok[sl]])
            if outp:
                hgv = hg[:, u * 256:(u + 1) * 256]
                if m:
                    sigmoid_into(gc_, psV[:, 256:512], ga_, gb_, [BpsV], [Bgt])
                    op(pool, lambda: nc.gpsimd.tensor_tensor(out=Gt[:, sl, :], in0=gc_, in1=hgv, op=ALU.mult),
                       reads=[Bgt, B_c], writes=[BG[sl]])
                else:
                    op(act, lambda: nc.scalar.activation(out=go_, in_=psV[:, 256:512], func=AF.Copy), reads=[BpsV], writes=[Bgt])
                    sigmoid_into(gc_, go_, ga_, gb_, [Bgt], [Bgt])
                    op(pool, lambda: nc.gpsimd.tensor_tensor(out=gc_, in0=gc_, in1=hgv, op=ALU.mult),
                       reads=[Bgt, B_c], writes=[Bgt])
                    op(pool, lambda: nc.gpsimd.tensor_tensor(out=Gt[:, sl, :], in0=go_, in1=gc_, op=ALU.mult),
                       reads=[Bgt], writes=[BG[sl]])

        def S2a(bi, tt):
            t0, ntl = BLOCKS[bi]
            t = t0 + tt
            par = gpar(bi)
            sl = par * 4 + tt
            outp = full and t >= 1
            sidx = []
            st["sidx"][t] = sidx
            for half in range(2):
                c = 2 * t + half
                rows = slice(half * 64, (half + 1) * 64)
                a_c = abc[:, h, c:c + 1] if m else avec[:, c:c + 1]
                Ba = Babc if m else (Bav[0] if c == 0 else Bav[(c - 1) // 8 + 1])
                xin_, Bxin = st["cur"]
                xo = st["xo"]
                xout_, Bxout = Xs[xo], BX[xo]
                if outp:
                    si = c % 4
                    sidx.append(si)
                    op(act, lambda: nc.scalar.activation(out=Sbf[si][:, 0:NV], in_=xin_[:, 0:NV], func=AF.Copy, scale=a_c),
                       reads=[Bxin, Ba], writes=[BS[si]])
                pp = c % 2
                op(pe, lambda: nc.tensor.matmul(psP[pp][:, 0:NV], lhsT=ktok[rows, sl, :], rhs=vtok[rows, sl, 0:NV],
                                                start=True, stop=True), reads=[Bktok[sl], Bvtok[sl]], writes=[BpsP[pp]])
                op(dve, lambda: nc.vector.scalar_tensor_tensor(out=xout_[:, 0:NV], in0=xin_[:, 0:NV], scalar=a_c,
                                                               in1=psP[pp][:, 0:NV], op0=ALU.mult, op1=ALU.add),
                   reads=[Bxin, Ba, BpsP[pp]], writes=[Bxout])
                st["cur"] = (xout_, Bxout)
                st["xo"] = 1 - xo

        def S2b(bi, tt):
            t0, ntl = BLOCKS[bi]
            t = t0 + tt
            par = gpar(bi)
            sl = par * 4 + tt
            outp = full and t >= 1
            if not outp:
                return
            kT_, qT_ = kTb[par], qTb[par]
            sidx = st["sidx"][t]
            tq = slice(tt * 128, (tt + 1) * 128)
            op(pe, lambda: nc.tensor.matmul(psA[:, 0:128], lhsT=kT_[:, tq], rhs=qT_[:, tq], start=True, stop=True),
               reads=[BkT[par], BqT[par]], writes=[BpsA])
            op(dve, lambda: nc.vector.tensor_tensor(out=attb, in0=psA[:, 0:128], in1=cmask[:, :], op=ALU.mult),
               reads=[BpsA, B_c], writes=[Batt])
            op(pe, lambda: nc.tensor.matmul(psO[:, 0:NV], lhsT=attb, rhs=vtok[:, sl, 0:NV], start=True, stop=False),
               reads=[Batt, Bvtok[sl]], writes=[BpsO])
            op(pe, lambda: nc.tensor.matmul(psO[0:64, 0:NV], lhsT=qT_[:, tt * 128:tt * 128 + 64], rhs=Sbf[sidx[0]][:, 0:NV],
                                            start=False, stop=False), reads=[BqT[par], BS[sidx[0]]], writes=[BpsO])
            op(pe, lambda: nc.tensor.matmul(psO[64:128, 0:NV], lhsT=qT_[:, tt * 128 + 64:tt * 128 + 128], rhs=Sbf[sidx[1]][:, 0:NV],
                                            start=False, stop=True), reads=[BqT[par], BS[sidx[1]]], writes=[BpsO])

        def S2c(bi, tt):
            t0, ntl = BLOCKS[bi]
            t = t0 + tt
            sl = gpar(bi) * 4 + tt
            if not (full and t >= 1):
                return
            ci = (u * 16 + (t - 1)) % 16
            ssc = ssA[:, 32 + ci:33 + ci]
            if m:
                d1 = tiny[:, 0:1]
                rden = tiny[:, 1:2]
                op(dve, lambda: nc.vector.scalar_tensor_tensor(out=tiny[:, 3:4], in0=psO[:, 256:257], scalar=-1.0,
                                                               in1=etok[:, t, 4 + h:5 + h], op0=ALU.mult, op1=ALU.max),
                   reads=[BpsO, Betok], writes=[Btiny])
                op(dve, lambda: nc.vector.tensor_tensor(out=d1, in0=psO[:, 256:257], in1=tiny[:, 3:4], op=ALU.max),
                   reads=[BpsO, Btiny], writes=[Btiny])
                op(dve, lambda: nc.vector.reciprocal(out=rden, in_=d1), reads=[Btiny], writes=[Btiny])
                op(act, lambda: nc.scalar.activation(out=junk[:, 0:256], in_=psO[:, 0:256], func=AF.Square, scale=rden,
                                                     accum_out=ssc), reads=[BpsO, Btiny], writes=[Bss])
            else:
                op(act, lambda: nc.scalar.activation(out=junk[:, 0:256], in_=psO[:, 0:256], func=AF.Square, accum_out=ssc),
                   reads=[BpsO], writes=[Bss])
            r = rstd(ssc, 256)
            if m:
                sc = tiny[:, 2:3]
                op(dve, lambda: nc.vector.tensor_tensor(out=sc, in0=rden, in1=r, op=ALU.mult), reads=[Btiny, Bss], writes=[Btiny])
                rb = [Btiny]
            else:
                sc = r
                rb = [Bss]
            yi = t % 2
            op(dve, lambda: nc.vector.scalar_tensor_tensor(out=ybuf[yi], in0=psO[:, 0:256], scalar=sc, in1=Gt[:, sl, :],
                                                           op0=ALU.mult, op1=ALU.mult), reads=[BpsO, BG[sl]] + rb, writes=[By[yi]])

        def S3(bi, tt):
            t0, ntl = BLOCKS[bi]
            t = t0 + tt
            if not (full and t >= 1):
                return
            yi = t % 2
            for j in range(2):
                op(pe, lambda j=j: nc.tensor.transpose(psT[:, 512 + j * 128:512 + (j + 1) * 128], ybuf[yi][:, j * 128:(j + 1) * 128],
                                                       identb[:]), reads=[By[yi], B_c], writes=[BpsT])
            op(dve, lambda: nc.vector.tensor_copy(out=yT[:, 2 * u:2 * u + 2, (t - 1) * 128:t * 128],
                                                  in_=psT[:, 512:768].rearrange("p (j c) -> p j c", j=2)),
               reads=[BpsT], writes=[ByT[t - 1]])

        def run(nxt, first):
            if first:
                preamble()
                for f in S0_stages(0):
                    f()
            S1(*TILES[0])
            sched = {}
            if nxt is not None:
                nst = nxt["stages0"]()
                sched.setdefault((14, 1), []).append(nxt["preamble"])
                slots_ = [(14, 1), (15, 0), (15, 1), (16, 0), (16, 1)]
                for j, f in enumerate(nst[:-1]):
                    sched.setdefault(slots_[min(j * len(slots_) // max(1, len(nst) - 1), len(slots_) - 1)], []).append(f)
                sched.setdefault((16, 1), []).append(nst[-1])
            for i, (bi, tt) in enumerate(TILES):
                ntl_b = BLOCKS[bi][1]
                if tt == 0 and bi + 1 < len(BLOCKS):
                    stages = S0_stages(bi + 1)
                    L = len(stages) - 1
                    nslot = 2 * ntl_b - 2
                    for j, f in enumerate(stages[:-1]):
                        slot_ = (j * nslot) // L if nslot > 0 else 0
                        sched.setdefault((i + slot_ // 2, slot_ % 2), []).append(f)
                    sched.setdefault((i + ntl_b - 1, 0), []).append(stages[-1])
                for f in sched.pop((i, 0), []):
                    f()
                if full:
                    S2a(bi, tt)
                    if i + 1 < len(TILES):
                        S1(*TILES[i + 1])
                    S2b(bi, tt)
                    S2c(bi, tt)
                else:
                    if i + 1 < len(TILES):
                        S1(*TILES[i + 1])
                    S2a(bi, tt)
                if i >= 1:
                    S3(*TILES[i - 1])
                for f in sched.pop((i, 1), []):
                    f()
            assert not sched, sched.keys()
            S3(*TILES[-1])
            xf_, Bxf = st["cur"]
            if not full:
                if m:
                    fin = abc[:, h, 34:35]
                    dec = abc[:, h, 35:36]
                    Bf = [Babc]
                else:
                    fin = avec[:, NCH:NCH + 1]
                    op(dve, lambda: nc.vector.tensor_reduce(out=tiny[:, 4:5], in_=nbend[:, 0:NCH], axis=AX.X, op=ALU.add),
                       reads=Bav, writes=[Btiny])
                    op(act, lambda: nc.scalar.activation(out=tiny[:, 5:6], in_=tiny[:, 4:5], func=AF.Exp, scale=-1.0 / 16.0),
                       reads=[Btiny], writes=[Btiny])
                    dec = tiny[:, 5:6]
                    Bf = Bav
                op(dve, lambda: nc.vector.memset(pubb[:, 0:258], 0.0), writes=[Bpub])
                op(dve, lambda: nc.vector.tensor_scalar(pubb[:, 0:NV], xf_[:, 0:NV], fin, None, ALU.mult),
                   reads=[Bxf] + Bf, writes=[Bpub])
                op(dve, lambda: nc.vector.tensor_copy(out=pubb[:, 257:258], in_=dec), reads=Bf + [Btiny], writes=[Bpub])
                dma(qs, pub[:, u * 258:(u + 1) * 258], pubb[:, 0:258], reads=[Bpub])
        return {"preamble": preamble, "stages0": lambda: S0_stages(0), "run": run}

    for phase in {"fused": (1, 2), "A": (1,), "B": (2,)}[mode]:
        load_unit_weights(0, phase, 0)
        units = [make_unit(u, phase, u % 2) for u in range(8)]
        for u in range(8):
            if u + 1 < 8:
                load_unit_weights(u + 1, phase, (u + 1) % 2)
            units[u]["run"](units[u + 1] if u + 1 < 8 else None, u == 0)
        if phase == 1 and mode == "A":
            for q in (qs, qg):
                for t_ in q.last:
                    k.sp.wait_tok(t_)
            k.barrier()
            es.close()
            return nc
        if phase == 1:
            for t_ in qs.last:
                pool.wait_tok(t_)
            import os
            if os.environ.get("NOCC"):
                nc.gpsimd.dma_start(out=gath[0:128, :], in_=pub[:, :]).then_inc(ccsem, 16)
                k.sp.e.wait_ge(ccsem, 16)
                nc.gpsimd.wait_ge(ccsem, 16)
            else:
                nc.gpsimd.collective_compute("AllGather", ALU.bypass, replica_groups=[list(range(NCORES))],
                                             ins=[pub_t.ap().opt()], outs=[gath_t.ap().opt()]).then_inc(ccsem)
                k.sp.e.wait_ge(ccsem, 1)
                nc.gpsimd.wait_ge(ccsem, 1)
            for e in (pe, act, dve, pool):
                e.new_epoch()
    if "yT" in dbg_d:
        yf = R3[:, 0:2048]
        Byf = Buf()
        k.barrier()
        op(dve, lambda: nc.vector.tensor_copy(out=yf, in_=yT[:, 0, :]), reads=ByT, writes=[Byf])
        dbg("yT", yf, [Byf])
    k.barrier()

    ar = Arena()
    mergedT = ar.bf16(KT * TL).rearrange("p (k t) -> p k t", k=KT)
    x1_mark = ar.off
    wx = [ar.bf16(KT * 512).rearrange("p (k j c) -> p k j c", k=KT, j=4) for _ in range(2)]
    Bwx = [[Buf() for _ in range(4)] for _ in range(2)]
    sa = ar.f32(512)
    sb_ = ar.f32(512)
    sgm = ar.f32(512)
    sgg = ar.f32(512)
    m1 = ar.f32(512)
    m2 = ar.f32(512)
    Bsa, Bsgm, Bsgg, Bm1, Bm2 = Buf(), Buf(), Buf(), Buf(), Buf()
    BmT = [Buf() for _ in range(4)]

    def load_wx(o, slot):
        cs = slice(o * 128, (o + 1) * 128)
        dma(qg, wx[slot][:, :, 0, :], wbm[:, cs].rearrange("(k p) c -> p k c", p=128), writes=[Bwx[slot][0]])
        dma(qg, wx[slot][:, :, 1, :], wbg[:, cs].rearrange("(k p) c -> p k c", p=128), writes=[Bwx[slot][1]])
        dma(qg, wx[slot][:, :, 2, :], w_in[:, 6168 + o * 128:6168 + (o + 1) * 128].rearrange("(k p) c -> p k c", p=128), writes=[Bwx[slot][2]])
        dma(qg, wx[slot][:, :, 3, :], w_in[:, 7192 + o * 128:7192 + (o + 1) * 128].rearrange("(k p) c -> p k c", p=128), writes=[Bwx[slot][3]])

    load_wx(0, 0)
    it = 0
    for o in range(8):
        if o + 1 < 8:
            load_wx(o + 1, (o + 1) % 2)
        s = o % 2
        for b in range(4):
            pb = (it % 2) * 4
            it += 1
            y0 = b * 512
            c0 = 128 + b * 512
            for j in range(4):
                for kk in range(KT):
                    if j == 0:
                        rhs = yT[:, kk, y0:y0 + 512]
                    elif j == 1:
                        rhs = yT[:, 8 + kk, y0:y0 + 512]
                    else:
                        rhs = xnT[:, kk, c0:c0 + 512]
                    op(pe, lambda kk=kk, j=j, rhs=rhs: nc.tensor.matmul(ps[pb + j][:, :], lhsT=wx[s][:, kk, j, :], rhs=rhs,
                                                                        start=(kk == 0), stop=(kk == KT - 1)),
                       reads=[Bwx[s][j]], writes=[psb[pb + j]])
            sigmoid_into(sgm, ps[pb + 2][:, :], sa, sb_, [psb[pb + 2]], [Bsgm])
            sigmoid_into(sgg, ps[pb + 3][:, :], sa, sb_, [psb[pb + 3]], [Bsgg])
            op(dve, lambda: nc.vector.tensor_tensor(out=m1, in0=ps[pb][:, :], in1=sgm, op=ALU.mult), reads=[psb[pb], Bsgm], writes=[Bm1])
            op(dve, lambda: nc.vector.tensor_tensor(out=m2, in0=ps[pb + 1][:, :], in1=sgg, op=ALU.mult), reads=[psb[pb + 1], Bsgg], writes=[Bm2])
            op(pool, lambda: nc.gpsimd.tensor_tensor(out=mergedT[:, o, y0:y0 + 512], in0=m1, in1=m2, op=ALU.add),
               reads=[Bm1, Bm2], writes=[BmT[b]])
    k.barrier()

    ar = Arena()
    ar.off = x1_mark
    wo = ar.bf16(KT * 1024).rearrange("p (k c) -> p k c", k=KT)
    Bwo = [Buf(), Buf()]
    dma(qg, wo[:, :, 0:512], wout[:, 0:512].rearrange("(k p) c -> p k c", p=128), writes=[Bwo[0]])
    dma(qg, wo[:, :, 512:1024], wout[:, 512:1024].rearrange("(k p) c -> p k c", p=128), writes=[Bwo[1]])
    xt = [ar.f32(1024) for _ in range(2)]
    hnb = [ar.bf16(1024) for _ in range(2)]
    Bxt = [Buf(), Buf()]
    Bhnb = [Buf(), Buf()]
    Bh2 = [Buf(f"h2_{t}") for t in range(16)]
    BhnT = [Buf(f"hnT{t}") for t in range(16)]
    def X2a(t):
        i = t % 2
        pb_ = (t % 2) * 2
        dma(qs, xt[i], xin[(t + 1) * 128:(t + 2) * 128, :], writes=[Bxt[i]])
        for half in range(2):
            for kk in range(KT):
                op(pe, lambda kk=kk, half=half: nc.tensor.matmul(ps[pb_ + half][:, :], lhsT=mergedT[:, kk, t * 128:(t + 1) * 128],
                                                                 rhs=wo[:, kk, half * 512:(half + 1) * 512],
                                                                 start=(kk == 0), stop=(kk == KT - 1)),
                   reads=[Bwo[half], BmT[t // 4]], writes=[psb[pb_ + half]])
            op(dve, lambda half=half: nc.vector.tensor_tensor(out=h2[:, t, half * 512:(half + 1) * 512], in0=ps[pb_ + half][:, :],
                                                              in1=xt[i][:, half * 512:(half + 1) * 512], op=ALU.add),
               reads=[psb[pb_ + half], Bxt[i]], writes=[Bh2[t]])

    def X2b(t):
        i = t % 2
        op(act, lambda: nc.scalar.activation(out=junk[:], in_=h2[:, t, :], func=AF.Square, accum_out=ssA[:, t:t + 1]),
           reads=[Bh2[t]], writes=[Bss])
        r = rstd(ssA[:, t:t + 1], D)
        op(act, lambda: nc.scalar.activation(out=hnb[i], in_=h2[:, t, :], func=AF.Copy, scale=r), reads=[Bh2[t], Bss], writes=[Bhnb[i]])

    def X2c(t):
        i = t % 2
        for kk in range(KT):
            op(pe, lambda kk=kk: nc.tensor.transpose(psT[:, kk * 128:(kk + 1) * 128], hnb[i][:, kk * 128:(kk + 1) * 128], identb[:]),
               reads=[Bhnb[i], B_c], writes=[BpsT])
        op(dve, lambda: nc.vector.tensor_tensor(out=hnT[:, :, t * 128:(t + 1) * 128], in0=psT.rearrange("p (k t) -> p k t", k=KT),
                                                in1=g2[:, :].unsqueeze(2).to_broadcast([128, KT, 128]), op=ALU.mult),
           reads=[BpsT, B_c], writes=[BhnT[t]])

    X2a(0)
    for t in range(16):
        if t + 1 < 16:
            X2a(t + 1)
        X2b(t)
        if t >= 1:
            X2c(t - 1)
    X2c(15)
    k.barrier()

    ar = Arena()
    JB = [(0, 4), (4, 4), (8, 4), (12, 4), (16, 4), (20, 2)]
    wg_ = [ar.bf16(KT * 512).rearrange("p (k c) -> p k c", k=KT) for _ in range(2)]
    wu_ = [ar.bf16(KT * 512).rearrange("p (k c) -> p k c", k=KT) for _ in range(2)]
    wd_ = [ar.bf16(4 * 1024).rearrange("p (j c) -> p j c", j=4) for _ in range(2)]
    Bwf = [[Buf() for _ in range(3)] for _ in range(2)]
    ffT = [ar.bf16(4 * 512).rearrange("p (j c) -> p j c", j=4) for _ in range(2)]
    Bff = [Buf(), Buf()]
    fa = ar.f32(512)
    fb = ar.f32(512)
    fsg = ar.f32(512)
    fgs = ar.f32(512)
    Bfa, Bfsg, Bfgs = Buf(), Buf(), Buf()
    ot = [ar.f32(1024) for _ in range(2)]
    Bot = [Buf(), Buf()]

    def load_wf(J, slot):
        j0, nj = JB[J]
        n = nj * 128
        dma(qg, wg_[slot][:, :, 0:n], wfg[:, j0 * 128:j0 * 128 + n].rearrange("(k p) c -> p k c", p=128), writes=[Bwf[slot][0]])
        dma(qg, wu_[slot][:, :, 0:n], wfu[:, j0 * 128:j0 * 128 + n].rearrange("(k p) c -> p k c", p=128), writes=[Bwf[slot][1]])
        dma(qg, wd_[slot][:, 0:nj, :], wfd[j0 * 128:j0 * 128 + n, :].rearrange("(j p) c -> p j c", p=128), writes=[Bwf[slot][2]])

    load_wf(0, 0)
    itg = 0
    ito = 0
    for J in range(len(JB)):
        if J + 1 < len(JB):
            load_wf(J + 1, (J + 1) % 2)
        s = J % 2
        j0, nj = JB[J]
        for b in range(4):
            fi = (J * 4 + b) % 2
            for j in range(nj):
                pg = (itg % 2) * 2
                itg += 1
                for which, wsrc in ((0, wg_[s]), (1, wu_[s])):
                    for kk in range(KT):
                        op(pe, lambda kk=kk, which=which, wsrc=wsrc: nc.tensor.matmul(
                            ps[pg + which][:, :], lhsT=wsrc[:, kk, j * 128:(j + 1) * 128], rhs=hnT[:, kk, b * 512:(b + 1) * 512],
                            start=(kk == 0), stop=(kk == KT - 1)), reads=[Bwf[s][which]] + BhnT[b * 4:(b + 1) * 4], writes=[psb[pg + which]])
                sigmoid_into(fsg, ps[pg][:, :], fa, fb, [psb[pg]], [Bfsg])
                op(dve, lambda: nc.vector.tensor_tensor(out=fgs, in0=ps[pg][:, :], in1=fsg, op=ALU.mult), reads=[psb[pg], Bfsg], writes=[Bfgs])
                op(dve, lambda j=j: nc.vector.tensor_tensor(out=ffT[fi][:, j, :], in0=ps[pg + 1][:, :], in1=fgs, op=ALU.mult),
                   reads=[psb[pg + 1], Bfgs], writes=[Bff[fi]])
            for tt in range(4):
                t = b * 4 + tt
                po = 4 + (ito % 2) * 2
                ito += 1
                for half in range(2):
                    for j in range(nj):
                        op(pe, lambda j=j, half=half: nc.tensor.matmul(
                            ps[po + half][:, :], lhsT=ffT[fi][:, j, tt * 128:(tt + 1) * 128], rhs=wd_[s][:, j, half * 512:(half + 1) * 512],
                            start=(j == 0), stop=(j == nj - 1)), reads=[Bff[fi], Bwf[s][2]], writes=[psb[po + half]])
                    op(dve, lambda half=half: nc.vector.tensor_tensor(out=h2[:, t, half * 512:(half + 1) * 512], in0=ps[po + half][:, :],
                                                                      in1=h2[:, t, half * 512:(half + 1) * 512], op=ALU.add),
                       reads=[psb[po + half], Bh2[t]], writes=[Bh2[t]])
    for t in range(16):
        i = t % 2
        op(act, lambda: nc.scalar.activation(out=junk[:], in_=h2[:, t, :], func=AF.Square, accum_out=ssA[:, 16 + t:17 + t]),
           reads=[Bh2[t]], writes=[Bss])
        r = rstd(ssA[:, 16 + t:17 + t], D)
        op(dve, lambda: nc.vector.scalar_tensor_tensor(out=ot[i], in0=h2[:, t, :], scalar=r, in1=fg[:, :], op0=ALU.mult, op1=ALU.mult),
           reads=[Bh2[t], Bss, B_c], writes=[Bot[i]])
        dma(qs, out_d[t * 128:(t + 1) * 128, :], ot[i], reads=[Bot[i]])
    for q in (qs, qg):
        for t_ in q.last:
            k.sp.wait_tok(t_)
    k.barrier()
    es.close()
    return nc


_CACHE = {}


def _host_inputs(x, meta_tokens, norm1_g, w_in, conv_w, conv_b, m_gate_b, g_a2, g_a2_b, m_head_g, g_head_g,
                 w_branch_m, w_branch_g, w_out, norm2_g, w_ff_gate, w_ff_up, w_ff_down, final_g):
    f = np.float32
    x2 = np.asarray(x, f)[0]
    pre0 = np.zeros((128, D), f)
    pre0[112:128] = np.asarray(meta_tokens, f)

    def pk(v):
        return np.ascontiguousarray(np.asarray(v, f).reshape(KT, 128).T)
    ident = np.eye(128, dtype=f)
    idx = np.arange(128)
    cmask = ((idx[:, None] // 64 == idx[None, :] // 64) & (idx[:, None] <= idx[None, :])).astype(f)
    rst = np.ones((128, 512), f)
    rst[:, ::64] = 0.0
    selh = np.zeros((4, 4, 128), f)
    for h in range(4):
        selh[h, h, :] = 1.0
    selh = selh.reshape(4, 512)
    cw = np.asarray(conv_w, f)[0]
    cw_p = np.ascontiguousarray(cw.reshape(4, KT, 128).transpose(2, 1, 0).reshape(128, 32))
    mgb = np.ascontiguousarray(np.asarray(m_gate_b, f)[0].T)
    a2b = np.ascontiguousarray(np.asarray(g_a2_b, f)[0].reshape(4, 128).T)
    hgrow = np.concatenate([np.asarray(m_head_g, f)[0].reshape(-1), np.asarray(g_head_g, f)[0].reshape(-1)])
    hg = np.ascontiguousarray(np.broadcast_to(hgrow[None, :], (128, 2048)))
    fgb = np.ascontiguousarray(np.broadcast_to(np.asarray(final_g, f)[None, :], (128, 1024)))
    shared = dict(
        ident=ident, cmask=cmask, rst=rst, selh=selh, g1=pk(np.asarray(norm1_g)[0]), g2=pk(np.asarray(norm2_g)[0]),
        cw=cw_p, cb=pk(np.asarray(conv_b)[0]), mgb=mgb, a2b=a2b, hg=hg, fg=fgb,
        w_in=np.ascontiguousarray(np.asarray(w_in, f)[0]), g_a2=np.ascontiguousarray(np.asarray(g_a2, f)[0]),
        wbm=np.ascontiguousarray(np.asarray(w_branch_m, f)[0]), wbg=np.ascontiguousarray(np.asarray(w_branch_g, f)[0]),
        wout=np.ascontiguousarray(np.asarray(w_out, f)[0]), wfg=np.ascontiguousarray(np.asarray(w_ff_gate, f)[0]),
        wfu=np.ascontiguousarray(np.asarray(w_ff_up, f)[0]), wfd=np.ascontiguousarray(np.asarray(w_ff_down, f)[0]),
    )
    maps = []
    for c in range(NCORES):
        if c == 0:
            xin = np.concatenate([pre0, x2[0:TL]], axis=0)
            valid = (np.arange(128) >= 112).astype(f)
        else:
            xin = x2[c * TL - 128:(c + 1) * TL]
            valid = np.zeros(128, f)
        maskrow = np.ascontiguousarray(np.broadcast_to(valid[None, :], (128, 128)))
        negrow = np.ascontiguousarray(np.broadcast_to(((valid - 1.0) * 1e30)[None, :], (128, 128))).astype(f)
        selv = (np.arange(8) < c).astype(f)
        sel = np.ascontiguousarray(np.broadcast_to(selv[None, :], (128, 8)))
        onem = np.ascontiguousarray(1.0 - sel).astype(f)
        m = dict(shared)
        m.update(xin=np.ascontiguousarray(xin, dtype=f), maskrow=maskrow, negrow=negrow, sel=sel, onem=onem)
        maps.append(m)
    return maps


def kernel(**inputs):
    maps = _host_inputs(**inputs)
    if MODE == "fused":
        if "nc" not in _CACHE:
            _CACHE["nc"] = build_program("fused")
        res = run_bass_kernel_spmd(_CACHE["nc"], maps, core_ids=list(range(NCORES)))
    else:
        if "ncA" not in _CACHE:
            _CACHE["ncA"] = build_program("A")
            _CACHE["ncB"] = build_program("B")
        drop = ("fg", "wbm", "wbg", "wout", "wfg", "wfu", "wfd", "g2", "sel", "onem")
        mapsA = [{k_: v for k_, v in m.items() if k_ not in drop} for m in maps]
        resA = run_bass_kernel_spmd(_CACHE["ncA"], mapsA, core_ids=list(range(NCORES)))
        gath = np.ascontiguousarray(np.concatenate([resA.results[c]["pub"] for c in range(NCORES)], axis=0))
        for m in maps:
            m["gath"] = gath
        res = run_bass_kernel_spmd(_CACHE["ncB"], maps, core_ids=list(range(NCORES)))
    _CACHE["last"] = res
    out = np.concatenate([res.results[c]["out"] for c in range(NCORES)], axis=0)
    return out[None].astype(np.float32)
```
